# Optimizing a Trainium2 kernel written in Bass

```python
import jax, jax.numpy as jnp
from jax import lax
import numpy as np

D_MODEL = 2048
BATCH = 2
SEQ = 16384
DEPTH = 2

GRID_W = 64
CTX_LEN = 256
N_MIXERS = 2
D_FF = 4 * D_MODEL
ROPE_BASE = 10000.0
NORM_EPS = 1e-6
Q_BLOCK = 128
NEG_INF = -1e30

MLA_HEADS = 16
MLA_Q_LORA = 512
MLA_KV_LORA = 512
MLA_NOPE = 128
MLA_ROPE = 64
MLA_V = 128

SWA_HEADS = 32
SWA_KV_HEADS = 4
SWA_HEAD_DIM = 64
SWA_WINDOW = 128

N_MLA_LAYERS = (DEPTH + 1) // 2
N_SWA_LAYERS = DEPTH // 2

kernel_name = 'hybrid_mla_swa_dit'


def rmsnorm(x, g):
    xf = x.astype(jnp.float32)
    y = xf * lax.rsqrt(jnp.mean(xf * xf, axis=-1, keepdims=True) + NORM_EPS)
    return (y * g.astype(jnp.float32)).astype(x.dtype)


def axial_angles(n_tokens, rot_dim):
    rows = n_tokens // GRID_W
    row = jnp.repeat(jnp.arange(rows, dtype=jnp.float32), GRID_W)
    col = jnp.tile(jnp.arange(GRID_W, dtype=jnp.float32), rows)
    n_freq = rot_dim // 4
    freqs = ROPE_BASE ** (-jnp.arange(n_freq, dtype=jnp.float32) / n_freq)
    return jnp.concatenate([row[:, None] * freqs, col[:, None] * freqs], axis=-1)


def apply_rope(x, cos, sin):
    half = x.shape[-1] // 2
    x1, x2 = x[..., :half], x[..., half:]
    return jnp.concatenate([x1 * cos - x2 * sin, x1 * sin + x2 * cos], axis=-1)


def mla_mixer(h_lat, h_ctx, w_in, g_qa, g_kva, w_qb, w_kvb, w_out, cos, sin, need_ctx):
    B, S, _ = h_lat.shape
    n_qk = MLA_NOPE + MLA_ROPE
    scale = n_qk ** -0.5

    def queries(q_a):
        q = (rmsnorm(q_a, g_qa) @ w_qb).reshape(q_a.shape[0], q_a.shape[1], MLA_HEADS, n_qk)
        return q[..., :MLA_NOPE], q[..., MLA_NOPE:]

    def keys_values(c_kv):
        kv = (rmsnorm(c_kv, g_kva) @ w_kvb).reshape(c_kv.shape[0], c_kv.shape[1], MLA_HEADS, MLA_NOPE + MLA_V)
        return kv[..., :MLA_NOPE], kv[..., MLA_NOPE:]

    def attend(qn, qr, kn, kr, v):
        s = jnp.einsum('bqhd,bkhd->bhqk', qn, kn) + jnp.einsum('bqhr,bkr->bhqk', qr, kr)
        p = jax.nn.softmax(s.astype(jnp.float32) * scale, axis=-1).astype(v.dtype)
        o = jnp.einsum('bhqk,bkhv->bqhv', p, v)
        return o.reshape(o.shape[0], o.shape[1], MLA_HEADS * MLA_V)

    p_l = h_lat @ w_in
    qn_l, qr_l = queries(p_l[..., :MLA_Q_LORA])
    qr_l = apply_rope(qr_l, cos[:, None], sin[:, None])
    kn_l, v_l = keys_values(p_l[..., MLA_Q_LORA:MLA_Q_LORA + MLA_KV_LORA])
    kr_l = apply_rope(p_l[..., MLA_Q_LORA + MLA_KV_LORA:], cos, sin)

    p_c = h_ctx @ (w_in if need_ctx else w_in[:, MLA_Q_LORA:])
    p_c_kv = p_c[..., p_c.shape[-1] - (MLA_KV_LORA + MLA_ROPE):]
    kn_c, v_c = keys_values(p_c_kv[..., :MLA_KV_LORA])
    kr_c = p_c_kv[..., MLA_KV_LORA:]

    kn_all = jnp.concatenate([kn_c, kn_l], axis=1)
    kr_all = jnp.concatenate([kr_c, kr_l], axis=1)
    v_all = jnp.concatenate([v_c, v_l], axis=1)

    def block(i):
        st = i * Q_BLOCK
        qn = lax.dynamic_slice_in_dim(qn_l, st, Q_BLOCK, axis=1)
        qr = lax.dynamic_slice_in_dim(qr_l, st, Q_BLOCK, axis=1)
        return attend(qn, qr, kn_all, kr_all, v_all)

    o_l = lax.map(block, jnp.arange(S // Q_BLOCK))
    o_l = jnp.moveaxis(o_l, 0, 1).reshape(B, S, MLA_HEADS * MLA_V)
    out_l = o_l @ w_out
    if need_ctx:
        qn_c, qr_c = queries(p_c[..., :MLA_Q_LORA])
        out_c = attend(qn_c, qr_c, kn_c, kr_c, v_c) @ w_out
    else:
        out_c = None
    return out_l, out_c


def swa_mixer(h_lat, h_ctx, w_qkv, sink, w_out, cos, sin, need_ctx):
    B, S, _ = h_lat.shape
    L = h_ctx.shape[1]
    G = SWA_HEADS // SWA_KV_HEADS
    dq = SWA_HEADS * SWA_HEAD_DIM
    dkv = SWA_KV_HEADS * SWA_HEAD_DIM
    scale = SWA_HEAD_DIM ** -0.5
    sink_g = sink.astype(jnp.float32).reshape(SWA_KV_HEADS, G)

    def attend(q, k, v, mask):
        s = jnp.einsum('bqkgd,bskd->bkgqs', q, k).astype(jnp.float32) * scale
        if mask is not None:
            s = jnp.where(mask, s, NEG_INF)
        sk = jnp.broadcast_to(sink_g[None, :, :, None, None], s.shape[:-1] + (1,))
        p = jax.nn.softmax(jnp.concatenate([s, sk], axis=-1), axis=-1)[..., :-1].astype(v.dtype)
        o = jnp.einsum('bkgqs,bskd->bqkgd', p, v)
        return o.reshape(o.shape[0], o.shape[1], dq)

    p_l = h_lat @ w_qkv
    q_l = apply_rope(p_l[..., :dq].reshape(B, S, SWA_KV_HEADS, G, SWA_HEAD_DIM), cos[:, None, None], sin[:, None, None])
    k_l = apply_rope(p_l[..., dq:dq + dkv].reshape(B, S, SWA_KV_HEADS, SWA_HEAD_DIM), cos[:, None], sin[:, None])
    v_l = p_l[..., dq + dkv:].reshape(B, S, SWA_KV_HEADS, SWA_HEAD_DIM)

    p_c = h_ctx @ (w_qkv if need_ctx else w_qkv[:, dq:])
    p_c_kv = p_c[..., p_c.shape[-1] - 2 * dkv:]
    k_c = p_c_kv[..., :dkv].reshape(B, L, SWA_KV_HEADS, SWA_HEAD_DIM)
    v_c = p_c_kv[..., dkv:].reshape(B, L, SWA_KV_HEADS, SWA_HEAD_DIM)

    pad = ((0, 0), (SWA_WINDOW, SWA_WINDOW), (0, 0), (0, 0))
    kp = jnp.pad(k_l, pad)
    vp = jnp.pad(v_l, pad)
    span = Q_BLOCK + 2 * SWA_WINDOW
    ctx_mask = jnp.ones((Q_BLOCK, L), dtype=bool)

    def block(i):
        st = i * Q_BLOCK
        q = lax.dynamic_slice_in_dim(q_l, st, Q_BLOCK, axis=1)
        kw = lax.dynamic_slice_in_dim(kp, st, span, axis=1)
        vw = lax.dynamic_slice_in_dim(vp, st, span, axis=1)
        qpos = st + jnp.arange(Q_BLOCK)
        kpos = st - SWA_WINDOW + jnp.arange(span)
        band = (jnp.abs(qpos[:, None] - kpos[None, :]) <= SWA_WINDOW) & (kpos >= 0)[None, :] & (kpos < S)[None, :]
        mask = jnp.concatenate([ctx_mask, band], axis=1)
        return attend(q, jnp.concatenate([k_c, kw], axis=1), jnp.concatenate([v_c, vw], axis=1), mask)

    o_l = lax.map(block, jnp.arange(S // Q_BLOCK))
    o_l = jnp.moveaxis(o_l, 0, 1).reshape(B, S, dq)
    out_l = o_l @ w_out
    if need_ctx:
        q_c = p_c[..., :dq].reshape(B, L, SWA_KV_HEADS, G, SWA_HEAD_DIM)
        out_c = attend(q_c, k_c, v_c, None) @ w_out
    else:
        out_c = None
    return out_l, out_c


def sq_relu_mlp(h, w_in, w_out):
    return jnp.square(jax.nn.relu(h @ w_in)) @ w_out


def _w(key, shape, fan_in, scale=1.0):
    return jax.random.normal(key, shape, jnp.float32) * (scale * fan_in ** -0.5)


def setup_inputs(seed: int = 0) -> dict:
    key = jax.random.key(seed)
    ks = jax.random.split(key, 20)
    D = D_MODEL
    return {
        'x': jax.random.normal(ks[0], (BATCH, SEQ, D), jnp.float32),
        'c': jax.random.normal(ks[1], (BATCH, D), jnp.float32),
        'ctx': jax.random.normal(ks[2], (BATCH, CTX_LEN, D), jnp.float32),
        'c_ctx': jax.random.normal(ks[3], (D,), jnp.float32),
        'w_mod': _w(ks[4], (DEPTH, D, 6 * D), D, 0.5),
        'b_mod': 0.02 * jax.random.normal(ks[5], (DEPTH, 6 * D), jnp.float32),
        'g_norm': 1.0 + 0.1 * jax.random.normal(ks[6], (DEPTH, 4, D), jnp.float32),
        'w_ff_in': _w(ks[7], (DEPTH, D, D_FF), D),
        'w_ff_out': _w(ks[8], (DEPTH, D_FF, D), D_FF),
        'mla_w_in': _w(ks[9], (N_MLA_LAYERS, D, MLA_Q_LORA + MLA_KV_LORA + MLA_ROPE), D),
        'mla_g_qa': 1.0 + 0.1 * jax.random.normal(ks[10], (N_MLA_LAYERS, MLA_Q_LORA), jnp.float32),
        'mla_g_kva': 1.0 + 0.1 * jax.random.normal(ks[11], (N_MLA_LAYERS, MLA_KV_LORA), jnp.float32),
        'mla_w_qb': _w(ks[12], (N_MLA_LAYERS, MLA_Q_LORA, MLA_HEADS * (MLA_NOPE + MLA_ROPE)), MLA_Q_LORA),
        'mla_w_kvb': _w(ks[13], (N_MLA_LAYERS, MLA_KV_LORA, MLA_HEADS * (MLA_NOPE + MLA_V)), MLA_KV_LORA),
        'mla_w_out': _w(ks[14], (N_MLA_LAYERS, MLA_HEADS * MLA_V, D), MLA_HEADS * MLA_V),
        'swa_w_qkv': _w(ks[15], (N_SWA_LAYERS, D, (SWA_HEADS + 2 * SWA_KV_HEADS) * SWA_HEAD_DIM), D),
        'swa_sink': 0.5 * jax.random.normal(ks[16], (N_SWA_LAYERS, SWA_HEADS), jnp.float32),
        'swa_w_out': _w(ks[17], (N_SWA_LAYERS, SWA_HEADS * SWA_HEAD_DIM, D), SWA_HEADS * SWA_HEAD_DIM),
    }


def reference(x, c, ctx, c_ctx, w_mod, b_mod, g_norm, w_ff_in, w_ff_out,
              mla_w_in, mla_g_qa, mla_g_kva, mla_w_qb, mla_w_kvb, mla_w_out,
              swa_w_qkv, swa_sink, swa_w_out):
    S = x.shape[1]
    ang_mla = axial_angles(S, MLA_ROPE)
    ang_swa = axial_angles(S, SWA_HEAD_DIM)
    cos_mla, sin_mla = jnp.cos(ang_mla).astype(x.dtype), jnp.sin(ang_mla).astype(x.dtype)
    cos_swa, sin_swa = jnp.cos(ang_swa).astype(x.dtype), jnp.sin(ang_swa).astype(x.dtype)

    s = ctx
    for i in range(DEPTH):
        need_ctx = i < DEPTH - 1
        mod_l = (jax.nn.silu(c) @ w_mod[i] + b_mod[i])[:, None, :]
        mod_c = jax.nn.silu(c_ctx) @ w_mod[i] + b_mod[i]
        sh_a, sc_a, gt_a, sh_f, sc_f, gt_f = jnp.split(mod_l, 6, axis=-1)
        csh_a, csc_a, cgt_a, csh_f, csc_f, cgt_f = jnp.split(mod_c, 6, axis=-1)

        h_l = rmsnorm(x, g_norm[i, 0]) * (1.0 + sc_a) + sh_a
        h_c = rmsnorm(s, g_norm[i, 0]) * (1.0 + csc_a) + csh_a
        j = i // N_MIXERS
        if i % N_MIXERS == 0:
            y_l, y_c = mla_mixer(h_l, h_c, mla_w_in[j], mla_g_qa[j], mla_g_kva[j], mla_w_qb[j],
                                 mla_w_kvb[j], mla_w_out[j], cos_mla, sin_mla, need_ctx)
        else:
            y_l, y_c = swa_mixer(h_l, h_c, swa_w_qkv[j], swa_sink[j], swa_w_out[j],
                                 cos_swa, sin_swa, need_ctx)
        x = x + gt_a * rmsnorm(y_l, g_norm[i, 1])

        f_l = rmsnorm(x, g_norm[i, 2]) * (1.0 + sc_f) + sh_f
        x = x + gt_f * rmsnorm(sq_relu_mlp(f_l, w_ff_in[i], w_ff_out[i]), g_norm[i, 3])

        if need_ctx:
            s = s + cgt_a * rmsnorm(y_c, g_norm[i, 1])
            f_c = rmsnorm(s, g_norm[i, 2]) * (1.0 + csc_f) + csh_f
            s = s + cgt_f * rmsnorm(sq_relu_mlp(f_c, w_ff_in[i], w_ff_out[i]), g_norm[i, 3])
    return x
```

```python
import contextlib
import numpy as np
import concourse.bass as bass
import concourse.mybir as mybir
from concourse.bass_utils import run_bass_kernel_spmd

F32 = mybir.dt.float32
BF16 = mybir.dt.bfloat16
AF = mybir.ActivationFunctionType
ALU = mybir.AluOpType
AX = mybir.AxisListType

D = 2048
KC = 16
DFF = 8192
HC = 64
EPS = 1e-6
H = 16
NOPE = 128
ROPE = 64
DV = 128
HS = 32
KVH = 4
HD = 64
CTX = 256
GRID_W = 64


class T:
    __slots__ = ("ap", "keys")

    def __init__(self, ap, keys):
        self.ap = ap
        self.keys = tuple(keys)

    def __getitem__(self, idx):
        return T(self.ap[idx], self.keys)

    def wk(self, *keys):
        return T(self.ap, keys)

    def v(self, ap):
        return T(ap, self.keys)


class Node:
    __slots__ = ("eng", "fn", "deps", "is_dma", "sem", "cnt", "signal", "sigidx", "cover", "emitted", "tag")


COMPUTE = ("pe", "act", "dve", "pool")


class Prog:
    def __init__(self, nc, es):
        self.nc = nc
        self.es = es
        self.engs = dict(pe=nc.tensor, act=nc.scalar, dve=nc.vector, pool=nc.gpsimd, sp=nc.sync)
        self.esem = {e: es.enter_context(nc.semaphore("sem_" + e)) for e in COMPUTE}
        self.sigcount = {e: 0 for e in COMPUTE}
        self.kw = {}
        self.kr = {}
        self.pending = []
        self.waited = {e: {} for e in self.engs}
        self.dsem = {}
        self.n_inst = 0
        self.n_wait = 0
        self.last_node = {}
        self.last_dma = {}
        self._bank = 0

    def op(self, eng, fn, reads, writes, semkey=None):
        n = Node()
        n.eng = eng
        n.fn = fn
        n.is_dma = semkey is not None
        n.tag = getattr(self, "tag", None)
        n.signal = False
        n.sigidx = None
        n.cover = None
        n.emitted = False
        n.sem = None
        n.cnt = None
        if n.is_dma:
            if semkey not in self.dsem:
                self.dsem[semkey] = [self.es.enter_context(self.nc.semaphore("d_" + str(len(self.dsem)))), 0]
            ent = self.dsem[semkey]
            ent[1] += 16
            n.sem = ent[0]
            n.cnt = ent[1]
        deps = {}
        rk = [k for t in reads for k in t.keys]
        wkeys = [k for t in writes for k in t.keys]
        for k in rk:
            w = self.kw.get(k)
            if w is not None:
                deps[id(w)] = (w, True)
        for k in wkeys:
            w = self.kw.get(k)
            if w is not None and id(w) not in deps:
                deps[id(w)] = (w, False)
            for r in self.kr.get(k, {}).values():
                if id(r) not in deps:
                    deps[id(r)] = (r, False)
        ekey = ("dma", id(n)) if n.is_dma else eng
        for k in rk:
            self.kr.setdefault(k, {})[ekey] = n
        for k in wkeys:
            self.kw[k] = n
            self.kr[k] = {}
        fd = []
        for (dn, raw) in deps.values():
            if dn is n:
                continue
            if (not dn.is_dma) and (not n.is_dma) and dn.eng == eng:
                if eng == "pe" or not raw:
                    continue
            fd.append(dn)
        n.deps = fd
        self.pending.append(n)
        if fn is not None:
            if n.is_dma:
                self.last_dma[semkey] = n
            else:
                self.last_node[eng] = n
        return n

    def barrier(self):
        deps = list(self.last_node.values()) + [d for k, d in self.last_dma.items() if not k.startswith("cast_")]
        for e in ("pe", "act", "dve", "pool", "sp"):
            n = self.op(e, None, [], [])
            n.deps = [d for d in deps]
        self.flush()

    def flush(self):
        last = {}
        for n in self.pending:
            for d in n.deps:
                if not d.is_dma and not d.emitted:
                    d.signal = True
            if not n.is_dma and n.fn is not None:
                last[n.eng] = n
        for n in last.values():
            n.signal = True
        for n in self.pending:
            e = self.engs[n.eng]
            need = {}
            for d in n.deps:
                if d.is_dma:
                    sem, val = d.sem, d.cnt
                else:
                    sem = self.esem[d.eng]
                    val = d.sigidx if d.sigidx is not None else d.cover
                    assert val is not None
                key = id(sem)
                if key not in need or need[key][1] < val:
                    need[key] = (sem, val)
            wt = self.waited[n.eng]
            if n.tag:
                print("DBG", n.tag, n.eng, "deps", [(d.eng, d.tag, d.sigidx, d.cover, d.cnt) for d in n.deps], "need", [(s_.name, v_) for s_, v_ in need.values()], "waited", {k_: v_ for k_, v_ in wt.items()})
            for key, (sem, val) in need.items():
                if wt.get(key, 0) >= val:
                    continue
                e.wait_ge(sem, val)
                self.n_wait += 1
                wt[key] = val
            if n.fn is not None:
                ins = n.fn(e)
                self.n_inst += 1
                if n.is_dma:
                    ins.then_inc(n.sem, 16)
                elif n.signal:
                    self.sigcount[n.eng] += 1
                    n.sigidx = self.sigcount[n.eng]
                    ins.then_inc(self.esem[n.eng], 1)
                    if n.tag:
                        print("DBG  signal", n.tag, n.eng, n.sigidx)
            n.emitted = True
        nxt = {}
        for n in reversed(self.pending):
            if n.is_dma or n.fn is None:
                continue
            if n.sigidx is not None:
                nxt[n.eng] = n.sigidx
            else:
                n.cover = nxt[n.eng]
        for n in self.pending:
            n.fn = None
            n.deps = None
        self.pending = []

    def mm(self, out, lhsT, rhs, start, stop, skip=False):
        o, l, r = out.ap, lhsT.ap, rhs.ap
        if skip:
            f = lambda e: e.matmul(o, l, r, start=start, stop=stop, skip_group_check=True)
        else:
            f = lambda e: e.matmul(o, l, r, start=start, stop=stop)
        return self.op("pe", f, [lhsT, rhs], [out])

    def tr(self, out, in_, ident):
        o, i, d = out.ap, in_.ap, ident.ap
        return self.op("pe", lambda e: e.transpose(o, i, d), [in_, ident], [out])

    def act(self, out, in_, func, bias=0.0, scale=1.0, accum=None):
        reads = [in_]
        writes = [out]
        b = bias
        s = scale
        if isinstance(bias, T):
            reads.append(bias)
            b = bias.ap
        if isinstance(scale, T):
            reads.append(scale)
            s = scale.ap
        a = None
        if accum is not None:
            writes.append(accum)
            a = accum.ap
        o, i = out.ap, in_.ap
        if a is None:
            f = lambda e: e.activation(out=o, in_=i, func=func, bias=b, scale=s)
        else:
            f = lambda e: e.activation(out=o, in_=i, func=func, bias=b, scale=s, accum_out=a)
        return self.op("act", f, reads, writes)

    def ts(self, eng, out, in0, s1, s2, op0, op1=None):
        reads = [in0]
        a1, a2 = s1, s2
        if isinstance(s1, T):
            reads.append(s1)
            a1 = s1.ap
        if isinstance(s2, T):
            reads.append(s2)
            a2 = s2.ap
        o, i = out.ap, in0.ap
        if op1 is None:
            f = lambda e: e.tensor_scalar(out=o, in0=i, scalar1=a1, scalar2=None, op0=op0)
        else:
            f = lambda e: e.tensor_scalar(out=o, in0=i, scalar1=a1, scalar2=a2, op0=op0, op1=op1)
        return self.op(eng, f, reads, [out])

    def tt(self, eng, out, in0, in1, op):
        o, a, b = out.ap, in0.ap, in1.ap
        return self.op(eng, lambda e: e.tensor_tensor(out=o, in0=a, in1=b, op=op), [in0, in1], [out])

    def stt(self, eng, out, in0, scalar, in1, op0, op1):
        reads = [in0, in1]
        s = scalar
        if isinstance(scalar, T):
            reads.append(scalar)
            s = scalar.ap
        o, a, b = out.ap, in0.ap, in1.ap
        return self.op(eng, lambda e: e.scalar_tensor_tensor(out=o, in0=a, scalar=s, in1=b, op0=op0, op1=op1),
                       reads, [out])

    def cp(self, eng, out, in_, after=()):
        o, i = out.ap, in_.ap
        if eng == "act":
            return self.op("act", lambda e: e.copy(out=o, in_=i), [in_] + list(after), [out])
        return self.op(eng, lambda e: e.tensor_copy(out=o, in_=i), [in_] + list(after), [out])

    def recip(self, out, in_):
        o, i = out.ap, in_.ap
        return self.op("dve", lambda e: e.reciprocal(out=o, in_=i), [in_], [out])

    def reduce_sum(self, out, in_):
        o, i = out.ap, in_.ap
        return self.op("dve", lambda e: e.reduce_sum(out=o, in_=i, axis=AX.X), [in_], [out])

    def memset(self, eng, out, val):
        o = out.ap
        return self.op(eng, lambda e: e.memset(o, val), [], [out])

    def dma(self, q, out, in_, semkey, slow=False, maxlast=None):
        o, i = out.ap, in_.ap
        kw = {}
        if slow:
            kw["allow_slow_non_contiguous"] = True
        if maxlast is not None:
            kw["max_dma_last_dim"] = maxlast
        return self.op(q, lambda e: e.dma_start(out=o, in_=i, **kw), [in_], [out], semkey=semkey)

    def fence(self, eng, reads):
        return self.op(eng, None, reads, [])


class Dims:
    def __init__(self, S_B):
        self.S_B = S_B
        self.NKT = S_B // 128 + 2
        self.NKEY = self.NKT * 128
        self.HALFC = self.NKT // 2
        self.HK = self.HALFC * 128
        self.NLO = S_B // 4 // 128
        self.NL = self.NLO + 2
        self.NT = self.NL + 2
        self.NTOK = self.NT * 128
        self.NSUP = self.NT // 4
        assert self.NT % 4 == 0 and self.NKT % 2 == 0
        gs = 1
        for g in (13, 5, 3, 2, 1):
            if self.HALFC % g == 0:
                gs = g
                break
        self.GS = gs
        self.NG = self.HALFC // gs


def build(S_B, dbg=None):
    dm = Dims(S_B)
    NKT, NKEY, HALFC, HK, NLO, NL, NT, NTOK, NSUP = (dm.NKT, dm.NKEY, dm.HALFC, dm.HK, dm.NLO, dm.NL,
                                                      dm.NT, dm.NTOK, dm.NSUP)
    nc = bass.Bass("TRN2", target_bir_lowering=False)
    dbg = dbg or ()

    def din(name, shape, dt=F32):
        return nc.dram_tensor(name, list(shape), dt, kind="ExternalInput").ap()

    def dscr(name, shape, dt):
        kind = "ExternalOutput" if name in dbg else "Internal"
        return nc.dram_tensor(name, list(shape), dt, kind=kind).ap()

    xo = din("xo", [NTOK, D])
    xb = din("xb", [NKEY, D])
    cvec = din("cvec", [2, D])
    w_mod = din("w_mod", [2, D, 6 * D])
    b_mod = din("b_mod", [2, 6 * D])
    g_norm = din("g_norm", [2, 4, D])
    w_ff_in = din("w_ff_in", [2, D, DFF])
    w_ff_out = din("w_ff_out", [2, DFF, D])
    mla_w_in = din("mla_w_in", [D, 1088])
    mla_g_qa = din("mla_g_qa", [512])
    mla_g_kva = din("mla_g_kva", [512])
    mla_w_qb = din("mla_w_qb", [512, 3072])
    mla_w_kvb = din("mla_w_kvb", [512, 4096])
    mla_w_out = din("mla_w_out", [2048, 2048])
    swa_w_qkv = din("swa_w_qkv", [2048, 2560])
    swa_sink = din("swa_sink", [32])
    swa_w_out = din("swa_w_out", [2048, 2048])
    ident_in = din("ident", [128, 128])
    sel_in = din("sel", [2, 256])
    cosk2 = din("cosk2", [NKEY, 64])
    sink2 = din("sink2", [NKEY, 64])
    cosq2 = din("cosq2", [64, NTOK])
    sinq2 = din("sinq2", [64, NTOK])
    coso2 = din("coso2", [NTOK, 64])
    sino2 = din("sino2", [NTOK, 64])
    masks = din("masks", [4, 128, 128])
    y_out = nc.dram_tensor("y_out", [NLO * 128, D], F32, kind="ExternalOutput").ap()

    w_in_bf = dscr("w_in_bf", [D, 1088], BF16)
    w_qb_bf = dscr("w_qb_bf", [512, 3072], BF16)
    w_kvb_bf = dscr("w_kvb_bf", [512, 4096], BF16)
    mla_wo_bf = dscr("mla_wo_bf", [2048, 2048], BF16)
    w1_bf = dscr("w1_bf", [2, D, DFF], BF16)
    w2_bf = dscr("w2_bf", [2, DFF, D], BF16)
    wqkv_bf = dscr("wqkv_bf", [2048, 2560], BF16)
    swa_wo_bf = dscr("swa_wo_bf", [2048, 2048], BF16)
    modvec = dscr("modvec", [2, 6, 2, D], F32)
    KTs = dscr("KTs", [H, 128, NKEY], BF16)
    Vs = dscr("Vs", [NKEY, H * DV], BF16)
    QTN = dscr("QTN", [H, 128, NTOK], BF16)
    QTR = dscr("QTR", [H, 64, NTOK], BF16)
    OTs = dscr("OTs", [H * DV, NTOK], BF16)
    X1 = dscr("X1", [NTOK, D], F32)
    FT = dscr("FT", [D, NTOK], BF16)
    X2 = dscr("X2", [NTOK, D], F32)
    Q1T = dscr("Q1T", [NT, 64, HS * 128], BF16)
    K1Ts = dscr("K1Ts", [NT, 64, KVH * 128], BF16)
    V1s = dscr("V1s", [NT, 128, KVH * 65], BF16)

    es = contextlib.ExitStack()
    with es:
        p = Prog(nc, es)

        L0s = [None]

        def sb(name, shape, dt, stack=None):
            p._uid = getattr(p, "_uid", 0) + 1
            uname = "sb%d_%s" % (p._uid, name)
            nbytes = int(np.prod(shape[1:])) * (2 if dt == BF16 else 4)
            acc = p.__dict__.setdefault("_sbacc", {})
            sid = id(stack or es)
            acc[sid] = acc.get(sid, 0) + nbytes
            p._sbmax = max(getattr(p, "_sbmax", 0), acc.get(id(es), 0) + acc.get(id(L0s[0]), 0) * (L0s[0] is not None and sid != id(L0s[0]) or 0) + acc[sid])
            t = (stack or es).enter_context(nc.sbuf_tensor(uname, list(shape), dt))
            return T(t[tuple(slice(None) for _ in shape)], (uname,))

        banks = [es.enter_context(nc.psum_tensor("bank%d" % i, [128, 512], F32)) for i in range(8)]

        def bank():
            i = p._bank
            p._bank = (i + 1) % 8
            return T(banks[i][:, :], (("ps", i),))

        def bank_bf(t):
            return T(t.ap.bitcast(BF16), t.keys)

        ident_f = sb("ident_f", [128, 128], F32)
        ident_b = sb("ident_b", [128, 128], BF16)
        ones_b = sb("ones_b", [128, 128], BF16)
        sel = sb("sel", [2, 256], F32)
        p.dma("sp", ident_f[:, :], T(ident_in, ()), "c0")
        p.dma("sp", sel[:, :], T(sel_in, ()), "c1")
        p.cp("dve", ident_b[:, :], ident_f[:, :])
        p.memset("dve", ones_b[:, :], 1.0)
        ones_f = sb("ones_f", [128, 128], F32)
        p.memset("dve", ones_f[:, :], 1.0)

        with contextlib.ExitStack() as ph:
            cf = [sb("cf%d" % i, [128, 8192], F32, ph) for i in range(2)]
            cb = [sb("cb%d" % i, [128, 8192], BF16, ph) for i in range(2)]
            cctr = [0]

            def cast_w(src, dst, key, nsplit):
                rows, cols = src.shape
                for r in range(0, rows, 128):
                    i = cctr[0] % 2
                    e = ("dve", "pool", "act")[cctr[0] % 3]
                    cctr[0] += 1
                    p.dma("sp", cf[i][:, 0:cols], T(src[r:r + 128, :], ()), "cf%d" % i)
                    p.cp(e, cb[i][:, 0:cols], cf[i][:, 0:cols])
                    p.dma("sp", T(dst[r:r + 128, :], (key,)), cb[i][:, 0:cols], "cb%d" % i)

            cast_w(mla_w_in, w_in_bf, "w_in_bf", 1)
            cast_w(mla_w_kvb, w_kvb_bf, "w_kvb_bf", 1)
            cast_w(mla_w_qb, w_qb_bf, "w_qb_bf", 1)
            cast_w(mla_w_out, mla_wo_bf, "mla_wo_bf", 1)
            for l in range(2):
                cast_w(w_ff_in[l], w1_bf[l], "w1_bf%d" % l, 4)
                cast_w(w_ff_out[l], w2_bf[l], "w2_bf%d" % l, 4)
                if l == 0:
                    cast_w(swa_w_qkv, wqkv_bf, "wqkv_bf", 1)
                    cast_w(swa_w_out, swa_wo_bf, "swa_wo_bf", 1)
            p.barrier()

        with contextlib.ExitStack() as ph:
            craw = sb("craw", [128, 2, 16], F32, ph)
            s_sb = sb("s_sb", [128, 2, 16], F32, ph)
            p.dma("sp", craw[:, :, :], T(cvec.rearrange("c (p k) -> p c k", k=16), ()), "c2", slow=True)
            p.act(s_sb[:, :, :], craw[:, :, :], AF.Silu)
            wm = [sb("wm%d" % i, [128, 16, 512], F32, ph) for i in range(3)]
            brow = [sb("brow%d" % i, [2, D], F32, ph) for i in range(2)]
            grow = [sb("grow%d" % i, [2, D], F32, ph) for i in range(2)]
            rrow = [sb("rrow%d" % i, [2, D], F32, ph) for i in range(2)]
            cnt = 0
            gidx = {1: 0, 2: 1, 4: 2, 5: 3}
            for l in range(2):
                wv = w_mod[l].rearrange("(p k) n -> p k n", k=16)
                for j in range(6):
                    i2 = (l * 6 + j) % 2
                    for c in range(2):
                        p.dma("sp", brow[i2][c:c + 1, :], T(b_mod[l:l + 1, j * D:(j + 1) * D], ()), "brow%d" % i2)
                        if j in gidx:
                            p.dma("sp", grow[i2][c:c + 1, :], T(g_norm[l, gidx[j]:gidx[j] + 1, :], ()),
                                  "grow%d" % i2)
                    pss = []
                    for nb in range(4):
                        w = wm[cnt % 3]
                        cnt += 1
                        col = j * D + nb * 512
                        p.dma("sp", w[:, :, :], T(wv[:, :, col:col + 512], ()), "wm%d" % ((cnt - 1) % 3))
                        ps = bank()
                        for k in range(16):
                            p.mm(ps[0:2, :], s_sb[:, :, k], w[:, k, :], k == 0, k == 15)
                        pss.append(ps)
                    r = rrow[i2]
                    for nb in range(4):
                        p.tt("dve", r[:, nb * 512:(nb + 1) * 512], pss[nb][0:2, :], brow[i2][:, nb * 512:(nb + 1) * 512],
                             ALU.add)
                    if j in (1, 4):
                        p.stt("dve", r[:, :], r[:, :], 1.0, grow[i2][:, :], ALU.add, ALU.mult)
                    elif j in (2, 5):
                        p.tt("dve", r[:, :], r[:, :], grow[i2][:, :], ALU.mult)
                    p.dma("sp", T(modvec[l, j, :, :], (("modvec", l),)), r[:, :], "rrow%d" % i2)
            p.barrier()

        def load_cols(dst, l, j, q="sp", sk="mc"):
            for c in range(2):
                p.dma(q, dst[:, c, :], T(modvec[l, j, c, :].rearrange("(k p) -> p k", p=128), (("modvec", l),)),
                      sk, slow=True)

        def load_rowbc(dst, l, j, c, sk):
            p.dma("sp", T(dst.ap.unsqueeze(1), dst.keys), T(modvec[l, j, c:c + 1, :].partition_broadcast(128), (("modvec", l),)), sk)

        def rstd_of(dst, ms_):
            p.act(dst, ms_, AF.Sqrt, bias=EPS)
            p.recip(dst, dst)

        def ln_T(x_t, hT_dst, Acol, Bcol, typ, tmp):
            junk, ms, rstd, xn = tmp
            p.act(junk[:, :], x_t[:, :], AF.Square, scale=float(D) ** -0.5, accum=ms[:, 0:1])
            rstd_of(rstd[:, 0:1], ms[:, 0:1])
            p.ts("dve", xn[:, :], x_t[:, :], rstd[:, 0:1], None, ALU.mult)
            for g in range(2):
                pb = bank_bf(bank())
                for kk in range(8):
                    k = g * 8 + kk
                    p.tr(pb[:, kk * 128:(kk + 1) * 128], xn[:, k * 128:(k + 1) * 128], ident_b[:, :])
                for kk in range(8):
                    k = g * 8 + kk
                    src = pb[:, kk * 128:(kk + 1) * 128]
                    if kk % 2 == 0:
                        p.act(hT_dst[:, k, :], src, AF.Identity, bias=Bcol[:, typ, k:k + 1],
                              scale=Acol[:, typ, k:k + 1])
                    else:
                        p.ts("dve", hT_dst[:, k, :], src, Acol[:, typ, k:k + 1], Bcol[:, typ, k:k + 1],
                             ALU.mult, ALU.add)

        L0 = contextlib.ExitStack()
        krT = sb("krT", [128, HK], BF16, L0)
        Acol = sb("Acol", [128, 2, 16], F32, L0)
        Bcol = sb("Bcol", [128, 2, 16], F32, L0)
        load_cols(Acol, 0, 1, sk="mcA")
        load_cols(Bcol, 0, 0, sk="mcB")

        with contextlib.ExitStack() as ph:
            gkva = sb("gkva", [128, 4], F32, ph)
            p.dma("sp", gkva[:, :], T(mla_g_kva.rearrange("(k p) -> p k", p=128), ()), "c3", slow=True)
            win = sb("win", [128, 16, 576], BF16, ph)
            p.dma("sp", win[:, :, :], T(w_in_bf.rearrange("(k p) c -> p k c", p=128)[:, :, 512:1088], ("w_in_bf",)),
                  "c4")
            wk = sb("wk", [128, 4, 2048], BF16, ph)
            wvv = sb("wvv", [128, 4, 2048], BF16, ph)
            kvv = w_kvb_bf.rearrange("(k p) (h two d) -> p k h two d", p=128, two=2, d=128)
            for k4 in range(4):
                p.dma("sp", T(wk.ap[:, k4, :].rearrange("p (h d) -> p h d", d=128), wk.keys),
                      T(kvv[:, k4, :, 0, :], ("w_kvb_bf",)), "c5")
                p.dma("sp", T(wvv.ap[:, k4, :].rearrange("p (h d) -> p h d", d=128), wvv.keys),
                      T(kvv[:, k4, :, 1, :], ("w_kvb_bf",)), "c6")
            cosk = [sb("cosk%d" % i, [128, 64], F32, ph) for i in range(2)]
            sink = [sb("sink%d" % i, [128, 64], F32, ph) for i in range(2)]
            xs = [sb("xs%d" % i, [128, D], F32, ph) for i in range(2)]
            junk = sb("junk", [128, D], BF16, ph)
            xn = [sb("xn%d" % i, [128, D], BF16, ph) for i in range(2)]
            ms = [sb("ms%d" % i, [128, 4], F32, ph) for i in range(2)]
            rstd = [sb("rstd%d" % i, [128, 4], F32, ph) for i in range(2)]
            hT = [sb("hT0", [128, 16, 512], BF16, ph)] * 2
            ckvn = [sb("ckvn%d" % i, [128, 512], BF16, ph) for i in range(2)]
            ckT = [sb("ckT%d" % i, [128, 4, 512], BF16, ph) for i in range(2)]
            krtok = [sb("krtok%d" % i, [128, 128], BF16, ph) for i in range(2)]
            rt1 = [sb("rt1_%d" % i, [128, 64], F32, ph) for i in range(2)]
            rt2 = [sb("rt2_%d" % i, [128, 64], F32, ph) for i in range(2)]
            ktsb = [sb("ktsb0", [128, 16, 512], BF16, ph)] * 2
            vsb = [sb("vsb%d" % i, [128, 2048], BF16, ph) for i in range(2)]
            for i in range(2):
                p.memset("dve", krtok[i][:, :], 0.0)
            nblk = (NKT + 3) // 4
            tcount = 0
            for blk in range(nblk):
                tiles = list(range(blk * 4, min(NKT, blk * 4 + 4)))
                ntk = len(tiles) * 128
                bs = blk % 2
                for ti, t in enumerate(tiles):
                    s = tcount % 2
                    tcount += 1
                    typ = 1 if t >= NKT - 2 else 0
                    half = 0 if t < HALFC else 1
                    lt = t - half * HALFC
                    p.dma("sp", xs[s][:, :], T(xb[t * 128:(t + 1) * 128, :], ()), "xs%d" % s)
                    p.dma("sp", cosk[s][:, :], T(cosk2[t * 128:(t + 1) * 128, :], ()), "cosk%d" % s)
                    p.dma("sp", sink[s][:, :], T(sink2[t * 128:(t + 1) * 128, :], ()), "sink%d" % s)
                    hTd = T(hT[bs].ap[:, :, ti * 128:(ti + 1) * 128], (("hT", 0, ti),))
                    ln_T(xs[s], hTd, Acol, Bcol, typ, (junk, ms[s], rstd[s], xn[s]))
                    psx = bank()
                    psy = bank()
                    for k in range(16):
                        p.mm(psx[:, :], hTd[:, k, :], win[:, k, 0:512], k == 0, k == 15)
                    for k in range(16):
                        p.mm(psy[:, 0:64], hTd[:, k, :], win[:, k, 512:576], k == 0, k == 15)
                    p.act(junk[:, 0:512], psx[:, :], AF.Square, scale=512.0 ** -0.5, accum=ms[s][:, 1:2])
                    rstd_of(rstd[s][:, 1:2], ms[s][:, 1:2])
                    p.ts("dve", ckvn[s][:, :], psx[:, :], rstd[s][:, 1:2], None, ALU.mult)
                    pb = bank_bf(bank())
                    for k4 in range(4):
                        p.tr(pb[:, k4 * 128:(k4 + 1) * 128], ckvn[s][:, k4 * 128:(k4 + 1) * 128], ident_b[:, :])
                    ckd = T(ckT[bs].ap[:, :, ti * 128:(ti + 1) * 128], (("ckT", bs, ti),))
                    for k4 in range(4):
                        p.act(ckd[:, k4, :], pb[:, k4 * 128:(k4 + 1) * 128], AF.Identity, scale=gkva[:, k4:k4 + 1])
                    p.tt("dve", rt1[s][:, :], psy[:, 0:64], cosk[s][:, :], ALU.mult)
                    p.tt("dve", rt2[s][:, 0:32], psy[:, 32:64], sink[s][:, 0:32], ALU.mult)
                    p.tt("dve", rt2[s][:, 32:64], psy[:, 0:32], sink[s][:, 32:64], ALU.mult)
                    p.tt("dve", krtok[s][:, half * 64:(half + 1) * 64], rt1[s][:, :], rt2[s][:, :], ALU.add)
                    pk = bank_bf(bank())
                    p.tr(pk[:, 0:128], krtok[s][:, :], ident_b[:, :])
                    p.cp("act", T(krT.ap[half * 64:(half + 1) * 64, lt * 128:(lt + 1) * 128], (("krT", t),)),
                         pk[half * 64:(half + 1) * 64, 0:128])
                ckb = T(ckT[bs].ap, tuple(("ckT", bs, ti) for ti in range(len(tiles))))
                for h in range(H):
                    ps = bank()
                    for k4 in range(4):
                        p.mm(ps[:, 0:ntk], wk[:, k4, h * 128:(h + 1) * 128], ckb[:, k4, 0:ntk], k4 == 0, k4 == 3)
                    if h % 2 == 0:
                        p.cp("act", ktsb[bs][:, h, 0:ntk], ps[:, 0:ntk])
                    else:
                        p.cp("dve", ktsb[bs][:, h, 0:ntk], ps[:, 0:ntk])
                c0 = blk * 512
                p.dma("sp", T(KTs[:, :, c0:c0 + ntk].rearrange("h d c -> d h c"), (("KTs", blk),)),
                      ktsb[bs][:, :, 0:ntk], "ktsb0")
                for ti, t in enumerate(tiles):
                    s = tcount % 2
                    tcount += 1
                    ckd = T(ckT[bs].ap[:, :, ti * 128:(ti + 1) * 128], (("ckT", bs, ti),))
                    for hg in range(4):
                        ps = bank()
                        for k4 in range(4):
                            p.mm(ps[:, :], ckd[:, k4, :], wvv[:, k4, hg * 512:(hg + 1) * 512], k4 == 0, k4 == 3)
                        if hg % 2 == 0:
                            p.cp("act", vsb[s][:, hg * 512:(hg + 1) * 512], ps[:, :])
                        else:
                            p.cp("dve", vsb[s][:, hg * 512:(hg + 1) * 512], ps[:, :])
                    p.dma("sp", T(Vs[t * 128:(t + 1) * 128, :], (("Vs", t),)), vsb[s][:, :], "vsb%d" % s)
            p.barrier()

        def pool_bank(ids, ctr):
            i = ids[ctr[0] % len(ids)]
            ctr[0] += 1
            return T(banks[i][:, :], (("ps", i),))

        if "STOP_AB" in dbg:
            fin = [T(KTs, tuple(("KTs", b) for b in range((NKT + 3) // 4))), T(Vs, tuple(("Vs", t) for t in range(NKT)))]
            p.fence("sp", fin)
            p.flush()
            L0.close()
            return nc, dm

        with contextlib.ExitStack() as ph:
            gqa = sb("gqa", [128, 4], F32, ph)
            p.dma("sp", gqa[:, :], T(mla_g_qa.rearrange("(k p) -> p k", p=128), ()), "c3", slow=True)
            winq = sb("winq", [128, 16, 512], BF16, ph)
            p.dma("sp", winq[:, :, :], T(w_in_bf.rearrange("(k p) c -> p k c", p=128)[:, :, 0:512], ("w_in_bf",)), "c4")
            wqb = sb("wqb", [128, 4, 3072], BF16, ph)
            p.dma("sp", wqb[:, :, :], T(w_qb_bf.rearrange("(k p) c -> p k c", p=128), ("w_qb_bf",)), "c5")
            wsw = sb("wsw", [128, 4, 1024], BF16, ph)
            for k4 in range(4):
                w4 = wqb.ap[:, k4, :].rearrange("p (h c) -> p h c", c=192)
                s4 = wsw.ap[:, k4, :].rearrange("p (h c) -> p h c", c=64)
                p.ts("dve", T(s4[:, :, 0:32], wsw.keys), T(w4[:, :, 160:192], wqb.keys), -1.0, None, ALU.mult)
                p.cp("dve", T(s4[:, :, 32:64], wsw.keys), T(w4[:, :, 128:160], wqb.keys))
            cq = [sb("cq%d" % i, [64, 512], F32, ph) for i in range(2)]
            sq = [sb("sq%d" % i, [64, 512], F32, ph) for i in range(2)]
            xs = [sb("xs%d" % i, [128, D], F32, ph) for i in range(2)]
            junk = sb("junk", [128, D], BF16, ph)
            xn = [sb("xn%d" % i, [128, D], BF16, ph) for i in range(2)]
            ms = [sb("ms%d" % i, [128, 4], F32, ph) for i in range(2)]
            rstd = [sb("rstd%d" % i, [128, 4], F32, ph) for i in range(2)]
            hT = sb("hT", [128, 16, 512], BF16, ph)
            qan = [sb("qan%d" % i, [128, 512], BF16, ph) for i in range(2)]
            qaT = sb("qaT", [128, 4, 512], BF16, ph)
            qnsb = sb("qnsb", [128, 16, 512], BF16, ph)
            qrsb = sb("qrsb", [64, 16, 512], BF16, ph)
            t1 = [sb("t1_%d" % i, [64, 512], F32, ph) for i in range(2)]
            t2 = [sb("t2_%d" % i, [64, 512], F32, ph) for i in range(2)]
            tcount = 0
            for blk in range(NSUP):
                c0 = blk * 512
                bs = blk % 2
                p.dma("sp", cq[bs][:, :], T(cosq2[:, c0:c0 + 512], ()), "cq%d" % bs)
                p.dma("sp", sq[bs][:, :], T(sinq2[:, c0:c0 + 512], ()), "sq%d" % bs)
                for ti in range(4):
                    t = blk * 4 + ti
                    s = tcount % 2
                    tcount += 1
                    typ = 1 if t >= NL else 0
                    p.dma("sp", xs[s][:, :], T(xo[t * 128:(t + 1) * 128, :], ()), "xs%d" % s)
                    hTd = T(hT.ap[:, :, ti * 128:(ti + 1) * 128], (("hT", ti),))
                    ln_T(xs[s], hTd, Acol, Bcol, typ, (junk, ms[s], rstd[s], xn[s]))
                    psx = bank()
                    for k in range(16):
                        p.mm(psx[:, :], hTd[:, k, :], winq[:, k, :], k == 0, k == 15)
                    p.act(junk[:, 0:512], psx[:, :], AF.Square, scale=512.0 ** -0.5, accum=ms[s][:, 1:2])
                    rstd_of(rstd[s][:, 1:2], ms[s][:, 1:2])
                    p.ts("dve", qan[s][:, :], psx[:, :], rstd[s][:, 1:2], None, ALU.mult)
                    pb = bank_bf(bank())
                    for k4 in range(4):
                        p.tr(pb[:, k4 * 128:(k4 + 1) * 128], qan[s][:, k4 * 128:(k4 + 1) * 128], ident_b[:, :])
                    qad = T(qaT.ap[:, :, ti * 128:(ti + 1) * 128], (("qaT", ti),))
                    for k4 in range(4):
                        p.act(qad[:, k4, :], pb[:, k4 * 128:(k4 + 1) * 128], AF.Identity, scale=gqa[:, k4:k4 + 1])
                qab = T(qaT.ap, tuple(("qaT", ti) for ti in range(4)))
                for h in range(H):
                    hs = h % 2
                    psn = bank()
                    for k4 in range(4):
                        p.mm(psn[:, :], wqb[:, k4, h * 192:h * 192 + 128], qab[:, k4, :], k4 == 0, k4 == 3)
                    qnd = T(qnsb.ap[:, h, :], (("qnsb", h),))
                    p.cp("act", qnd, psn[:, :])
                    psr = bank()
                    pss = bank()
                    for k4 in range(4):
                        p.mm(psr[0:64, :], wqb[:, k4, h * 192 + 128:h * 192 + 192], qab[:, k4, :], k4 == 0, k4 == 3)
                    for k4 in range(4):
                        p.mm(pss[0:64, :], wsw[:, k4, h * 64:(h + 1) * 64], qab[:, k4, :], k4 == 0, k4 == 3)
                    p.tt("dve", t1[hs][:, :], psr[0:64, :], cq[bs][:, :], ALU.mult)
                    p.tt("dve", t2[hs][:, :], pss[0:64, :], sq[bs][:, :], ALU.mult)
                    qrd = T(qrsb.ap[:, h, :], (("qrsb", h),))
                    p.tt("dve", qrd, t1[hs][:, :], t2[hs][:, :], ALU.add)
                allq = tuple(("qnsb", h) for h in range(H))
                allr = tuple(("qrsb", h) for h in range(H))
                p.dma("sp", T(QTN[:, :, c0:c0 + 512].rearrange("h d c -> d h c"), tuple(("QTN", h, blk) for h in range(H))),
                      T(qnsb.ap, allq), "qnsb")
                p.dma("sp", T(QTR[:, :, c0:c0 + 512].rearrange("h d c -> d h c"), tuple(("QTR", h, blk) for h in range(H))),
                      T(qrsb.ap, allr), "qrsb")
            p.barrier()

        if "STOP_C" in dbg:
            p.fence("sp", [T(QTN, tuple(("QTN", h, b) for h in range(H) for b in range(NSUP)))])
            p.flush()
            L0.close()
            return nc, dm

        with contextlib.ExitStack() as ph:
            GS, NG = dm.GS, dm.NG
            kbuf = [sb("kbuf%d" % i, [128, HK], BF16, ph) for i in range(2)]
            vbuf = [sb("vbuf%d" % i, [128, HALFC, 128], BF16, ph) for i in range(2)]
            qn = [sb("qn%d" % i, [128, 512], BF16, ph) for i in range(2)]
            qr = [[sb("qr%d_%d" % (hf, i), [128, 512], BF16, ph) for i in range(2)] for hf in range(2)]
            for hf in range(2):
                for i in range(2):
                    p.memset("dve", qr[hf][i][:, :], 0.0)
            Pb = [sb("Pb%d" % i, [128, 512], BF16, ph) for i in range(4)]
            Oacc = sb("Oacc", [128, NL * 128], F32, ph)
            Sacc = sb("Sacc", [128, NL * 128], F32, ph)
            tmpS = [sb("tmpS%d" % i, [128, 512], F32, ph) for i in range(2)]
            tmpO = [sb("tmpO%d" % i, [128, 512], F32, ph) for i in range(2)]
            osb = [sb("osb%d" % i, [128, 512], BF16, ph) for i in range(2)]
            PaccA = [sb("PaccA%d" % i, [128, 512], F32, ph) for i in range(2)]
            PaccB = [sb("PaccB%d" % i, [128, 512], F32, ph) for i in range(2)]
            cS, cO, cZ = [0], [0], [0]
            SC = float(NOPE + ROPE) ** -0.5
            qblocks = []
            q0 = 0
            while q0 < NL * 128:
                nq = min(512, NL * 128 - q0)
                qblocks.append((q0, nq, False))
                q0 += nq
            ctxblk = (NL * 128, 256, True)
            steps = []
            u = 0
            for h in range(H):
                for half in range(2):
                    qbs = qblocks + ([ctxblk] if half == 1 else [])
                    for qi, (q0, nq, isctx) in enumerate(qbs):
                        chunks = [HALFC - 2, HALFC - 1] if isctx else list(range(HALFC))
                        for ci, c in enumerate(chunks):
                            steps.append(dict(u=u, h=h, half=half, qi=qi, q0=q0, nq=nq, isctx=isctx, c=c, ci=ci,
                                              first=ci == 0, last=ci == len(chunks) - 1,
                                              ufirst=(qi == 0 and ci == 0)))
                    u += 1
            qctr = [0]
            state = {}

            def load_unit(st):
                ub = st["u"] % 2
                h, half = st["h"], st["half"]
                for g in range(NG):
                    a = half * HK + g * GS * 128
                    b = a + GS * 128
                    kkeys = tuple(("KTs", bb) for bb in range(a // 512, (b - 1) // 512 + 1))
                    p.dma("sp", T(kbuf[ub].ap[:, g * GS * 128:(g + 1) * GS * 128], (("kb", ub, g),)),
                          T(KTs[h, :, a:b], kkeys), "kb%d_%d" % (ub, g))
                    vkeys = tuple(("Vs", tt) for tt in range(a // 128, b // 128))
                    p.dma("sp", T(vbuf[ub].ap[:, g * GS:(g + 1) * GS, :], (("vb", ub, g),)),
                          T(Vs[a:b, :].rearrange("(c p) d -> p c d", p=128)[:, :, h * 128:(h + 1) * 128], vkeys),
                          "vb%d_%d" % (ub, g))

            ufirsts = [st for st in steps if st["ufirst"]]

            def emit_qk(st):
                ub = st["u"] % 2
                h, half, q0, nq, c = st["h"], st["half"], st["q0"], st["nq"], st["c"]
                if st["first"]:
                    qs = qctr[0] % 2
                    qctr[0] += 1
                    st["qs"] = qs
                    blkq = q0 // 512
                    p.dma("sp", qn[qs][:, 0:nq], T(QTN[h, :, q0:q0 + nq], (("QTN", h, blkq),)), "qn%d" % qs)
                    p.dma("sp", qr[half][qs][half * 64:(half + 1) * 64, 0:nq], T(QTR[h, :, q0:q0 + nq], (("QTR", h, blkq),)),
                          "qr%d_%d" % (half, qs))
                    state[(st["u"], st["qi"])] = dict(qs=qs)
                sd = state[(st["u"], st["qi"])]
                qs = sd["qs"]
                ps = pool_bank([0, 1, 2, 3], cS)
                st["ps"] = ps
                g = c // GS
                p.mm(ps[:, 0:nq], T(kbuf[ub].ap[:, c * 128:(c + 1) * 128], (("kb", ub, g),)), qn[qs][:, 0:nq],
                     True, False)
                tkey = half * HALFC + c
                p.mm(ps[:, 0:nq], T(krT.ap[:, c * 128:(c + 1) * 128], (("krT", c), ("krT", HALFC + c))),
                     qr[half][qs][:, 0:nq], False, True)

            pcount = [0]

            def emit_rest(st):
                ub = st["u"] % 2
                h, half, q0, nq, c = st["h"], st["half"], st["q0"], st["nq"], st["c"]
                sd = state[(st["u"], st["qi"])]
                P = Pb[pcount[0] % 4]
                pcount[0] += 1
                p.act(P[:, 0:nq], st["ps"][:, 0:nq], AF.Exp, scale=SC)
                if st["first"]:
                    sd["psO"] = pool_bank([4, 5], cO)
                    sd["psZ"] = pool_bank([6, 7], cZ)
                psO, psZ = sd["psO"], sd["psZ"]
                g = c // GS
                p.mm(psO[:, 0:nq], T(vbuf[ub].ap[:, c, :], (("vb", ub, g),)), P[:, 0:nq], st["first"], st["last"])
                ci = st["ci"]
                if ci % 3 == 2 or ci == 1:
                    aeng, acc, fresh = "pool", PaccB[sd["qs"]], (ci == 1)
                else:
                    aeng, acc, fresh = "dve", PaccA[sd["qs"]], (ci == 0)
                if fresh:
                    p.cp(aeng, acc[:, 0:nq], P[:, 0:nq])
                else:
                    p.tt(aeng, acc[:, 0:nq], acc[:, 0:nq], P[:, 0:nq], ALU.add)
                if st["last"]:
                    p.mm(psZ[:, 0:nq], ones_f[:, :], PaccA[sd["qs"]][:, 0:nq], True, False)
                    p.mm(psZ[:, 0:nq], ones_f[:, :], PaccB[sd["qs"]][:, 0:nq], False, True)
                if st["last"]:
                    fs = sd["qs"]
                    if st["isctx"]:
                        p.recip(tmpS[fs][:, 0:nq], psZ[:, 0:nq])
                        p.tt("dve", osb[fs][:, 0:nq], psO[:, 0:nq], tmpS[fs][:, 0:nq], ALU.mult)
                    elif half == 0:
                        ak = (("acc", st["qi"]),)
                        p.cp("dve", T(Oacc.ap[:, q0:q0 + nq], ak), psO[:, 0:nq])
                        p.cp("dve", T(Sacc.ap[:, q0:q0 + nq], ak), psZ[:, 0:nq])
                        return
                    else:
                        ak = (("acc", st["qi"]),)
                        p.tt("dve", tmpS[fs][:, 0:nq], psZ[:, 0:nq], T(Sacc.ap[:, q0:q0 + nq], ak), ALU.add)
                        p.recip(tmpS[fs][:, 0:nq], tmpS[fs][:, 0:nq])
                        p.tt("dve", tmpO[fs][:, 0:nq], psO[:, 0:nq], T(Oacc.ap[:, q0:q0 + nq], ak), ALU.add)
                        p.tt("dve", osb[fs][:, 0:nq], tmpO[fs][:, 0:nq], tmpS[fs][:, 0:nq], ALU.mult)
                    okey = ("OTs", h, q0 // 512, 1 if st["isctx"] else 0)
                    p.dma("sp", T(OTs[h * 128:(h + 1) * 128, q0:q0 + nq], (okey,)), osb[fs][:, 0:nq], "osb%d" % fs)

            LA = 2
            load_unit(ufirsts[0])
            if len(ufirsts) > 1:
                load_unit(ufirsts[1])
            for i in range(len(steps) + LA):
                if i < len(steps):
                    emit_qk(steps[i])
                if i >= LA:
                    st = steps[i - LA]
                    emit_rest(st)
                    is_ulast = (i - LA + 1 == len(steps)) or steps[i - LA + 1]["u"] != st["u"]
                    if is_ulast and st["u"] + 2 < len(ufirsts):
                        load_unit(ufirsts[st["u"] + 2])
            p.barrier()
        L0.close()

        if "STOP_D" in dbg:
            p.flush()
            return nc, dm

        def proj_res_ln2(l, t, oT_t, x_t, typ, R, yset, dst_row):
            wout, G1, A2, B2, junk, ms, rstd, xn, tmpy, fTt = R
            ys = [T(banks[b][:, :], (("ps", b),)) for b in yset]
            for nb in range(4):
                for k in range(16):
                    p.mm(ys[nb][:, :], oT_t[:, k, :], wout[:, k, nb * 512:(nb + 1) * 512], k == 0, k == 15)
            for nb in range(4):
                p.act(junk[:, nb * 512:(nb + 1) * 512], ys[nb][:, :], AF.Square, scale=float(D) ** -0.5,
                      accum=ms[:, nb:nb + 1])
            p.reduce_sum(ms[:, 4:5], ms[:, 0:4])
            rstd_of(rstd[:, 0:1], ms[:, 4:5])
            for nb in range(4):
                p.stt("dve", tmpy[:, nb * 512:(nb + 1) * 512], ys[nb][:, :], rstd[:, 0:1],
                      G1[typ][:, nb * 512:(nb + 1) * 512], ALU.mult, ALU.mult)
            p.tt("pool", x_t[:, :], x_t[:, :], tmpy[:, :], ALU.add)
            p.dma("sp", T(X1[dst_row * 128:(dst_row + 1) * 128, :], (("X1", dst_row),)), x_t[:, :], "x1st%d" % (t % 2))
            p.act(junk[:, :], x_t[:, :], AF.Square, scale=float(D) ** -0.5, accum=ms[:, 5:6])
            rstd_of(rstd[:, 1:2], ms[:, 5:6])
            p.ts("dve", xn[:, :], x_t[:, :], rstd[:, 1:2], None, ALU.mult)
            for g in range(2):
                pb = bank_bf(ys[g])
                for kk in range(8):
                    k = g * 8 + kk
                    p.tr(pb[:, kk * 128:(kk + 1) * 128], xn[:, k * 128:(k + 1) * 128], ident_b[:, :])
                for kk in range(8):
                    k = g * 8 + kk
                    src = pb[:, kk * 128:(kk + 1) * 128]
                    if kk % 2 == 0:
                        p.act(fTt[:, k, :], src, AF.Identity, bias=B2[:, typ, k:k + 1], scale=A2[:, typ, k:k + 1])
                    else:
                        p.ts("dve", fTt[:, k, :], src, A2[:, typ, k:k + 1], B2[:, typ, k:k + 1], ALU.mult, ALU.add)
            p.dma("sp", T(FT.rearrange("(k p) c -> p k c", p=128)[:, :, dst_row * 128:(dst_row + 1) * 128],
                          (("FT", dst_row),)), fTt[:, :, :], "ftst%d" % (t % 2))

        def alloc_proj(ph, l, wo_bf, wo_key):
            wout = sb("wout", [128, 16, 2048], BF16, ph)
            p.dma("sp", wout[:, :, :], T(wo_bf.rearrange("(k p) c -> p k c", p=128), (wo_key,)), "c4")
            ntyp = 2 if l == 0 else 1
            G1 = [sb("G1_%d" % c, [128, D], F32, ph) for c in range(ntyp)]
            for c in range(ntyp):
                load_rowbc(G1[c], l, 2, c, "c5")
            A2 = sb("A2", [128, 2, 16], F32, ph)
            B2 = sb("B2", [128, 2, 16], F32, ph)
            load_cols(A2, l, 4, sk="mcA")
            load_cols(B2, l, 3, sk="mcB")
            junk = sb("junk", [128, D], BF16, ph)
            tmpy = sb("tmpy", [128, D], F32, ph)
            res = []
            for i in range(2):
                res.append((wout, G1, A2, B2, junk, sb("ms%d" % i, [128, 8], F32, ph), sb("rstd%d" % i, [128, 4], F32, ph),
                            sb("xn%d" % i, [128, D], BF16, ph), tmpy,
                            sb("fTt%d" % i, [128, 16, 128], BF16, ph)))
            return res

        def ffn(l, tile_groups, final):
            with contextlib.ExitStack() as ph:
                ntyp = 2 if l == 0 else 1
                G3 = [sb("G3_%d" % c, [128, D], F32, ph) for c in range(ntyp)]
                for c in range(ntyp):
                    load_rowbc(G3[c], l, 5, c, "c5")
                fTb = sb("fTb", [128, 16, 512], BF16, ph)
                w1s = [sb("w1s%d" % i, [128, 16, 256], BF16, ph) for i in range(2)] * 2
                w2s = [sb("w2s%d" % i, [128, 2, 1024], BF16, ph) for i in range(3)]
                uT = sb("uT", [128, 64, 512], BF16, ph)
                rl = [sb("rl%d" % i, [128, 512], F32, ph) for i in range(2)]
                ysb = [sb("ysb%d" % i, [128, 1024], F32, ph) for i in range(4)]
                xt = [sb("xt%d" % i, [128, D], F32, ph) for i in range(2)]
                junk = sb("junk", [128, 1024], BF16, ph)
                ms = [sb("ms%d" % i, [128, 8], F32, ph) for i in range(4)]
                rstd = [sb("rstd%d" % i, [128, 4], F32, ph) for i in range(4)]
                w1v = w1_bf[l].rearrange("(k p) n -> p k n", p=128)
                w2v = w2_bf[l].rearrange("(j p) n -> p j n", p=128)
                cA = [0]
                c1 = 0
                c2 = 0
                rc = 0
                xc = 0
                for grp in tile_groups:
                    ntk = len(grp) * 128
                    r0 = grp[0][0]
                    p.dma("sp", fTb[:, :, 0:ntk], T(FT.rearrange("(k p) c -> p k c", p=128)[:, :, r0 * 128:r0 * 128 + ntk],
                                                     tuple(("FT", g[0]) for g in grp)), "fTb")
                    for jg in range(32):
                        w = w1s[c1 % 2]
                        p.dma("sp", w[:, :, :], T(w1v[:, :, jg * 256:(jg + 1) * 256], ("w1_bf%d" % l,)), "w1s%d" % (c1 % 2))
                        c1 += 1
                        for j2 in range(2):
                            j = jg * 2 + j2
                            ps = pool_bank([0, 1, 2, 3, 4, 5, 6, 7], cA)
                            for k in range(16):
                                p.mm(ps[:, 0:ntk], w[:, k, j2 * 128:(j2 + 1) * 128], fTb[:, k, 0:ntk], k == 0, k == 15)
                            r = rl[rc % 2]
                            rc += 1
                            p.act(r[:, 0:ntk], ps[:, 0:ntk], AF.Relu)
                            ud = T(uT.ap[:, j, 0:ntk], (("uT", j),))
                            p.tt("dve", ud, r[:, 0:ntk], r[:, 0:ntk], ALU.mult)
                    if "FFN_A" in dbg:
                        break
                    for dh in range(2):
                        p.tag = None
                        yb = [[T(banks[ti * 2 + n2][:, :], (("ps", ti * 2 + n2),)) for n2 in range(2)] for ti in range(len(grp))]
                        for ti in range(len(grp)):
                            for n2 in range(2):
                                for jg in range(32):
                                    w = w2s[c2 % 3]
                                    p.dma("sp", w[:, :, :], T(w2v[:, jg * 2:(jg + 1) * 2, dh * 1024:(dh + 1) * 1024], ("w2_bf%d" % l,)),
                                          "w2s%d" % (c2 % 3))
                                    c2 += 1
                                    for j2 in range(2):
                                        j = jg * 2 + j2
                                        p.tag = ("mmB_dh%d_ti%d_n%d_j%d" % (dh, ti, n2, j)) if ("FFN_DBG" in dbg and j in (0, 63)) else None
                                        p.mm(yb[ti][n2][:, :], T(uT.ap[:, j, ti * 128:(ti + 1) * 128], (("uT", j),)),
                                             w[:, j2, n2 * 512:(n2 + 1) * 512], j == 0, j == 63)
                        for ti, (row, typ, orow) in enumerate(grp):
                            p.tag = "epi_dh%d_ti%d" % (dh, ti) if "FFN_DBG" in dbg else None
                            if "FFN_NOEPI" in dbg:
                                continue
                            for n2 in range(2):
                                if "FFN_NOSQ" in dbg:
                                    continue
                                p.act(junk[:, n2 * 512:(n2 + 1) * 512], yb[ti][n2][:, :], AF.Square, scale=float(D) ** -0.5,
                                      accum=ms[ti][:, dh * 2 + n2:dh * 2 + n2 + 1])
                            if dh == 0 and "FFN_NOCP" in dbg:
                                pass
                            elif dh == 0:
                                for n2 in range(2):
                                    p.cp("dve", ysb[ti][:, n2 * 512:(n2 + 1) * 512], yb[ti][n2][:, :], after=[junk])
                            elif "FFN_B" in dbg:
                                pass
                            else:
                                p.reduce_sum(ms[ti][:, 4:5], ms[ti][:, 0:4])
                                rstd_of(rstd[ti][:, 0:1], ms[ti][:, 4:5])
                                x_t = xt[xc % 2]
                                xc += 1
                                p.dma("sp", x_t[:, :], T(X1[row * 128:(row + 1) * 128, :], (("X1", row),)), "xt%d" % ((xc - 1) % 2))
                                p.stt("dve", ysb[ti][:, :], ysb[ti][:, :], rstd[ti][:, 0:1], G3[typ][:, 0:1024], ALU.mult, ALU.mult)
                                p.tt("pool", x_t[:, 0:1024], x_t[:, 0:1024], ysb[ti][:, :], ALU.add)
                                for n2 in range(2):
                                    p.stt("dve", ysb[ti][:, n2 * 512:(n2 + 1) * 512], yb[ti][n2][:, :], rstd[ti][:, 0:1],
                                          G3[typ][:, 1024 + n2 * 512:1024 + (n2 + 1) * 512], ALU.mult, ALU.mult)
                                p.tt("pool", x_t[:, 1024:2048], x_t[:, 1024:2048], ysb[ti][:, :], ALU.add)
                                final(row, orow, x_t, "xt%d" % ((xc - 1) % 2))
                    if "FFN_G0" in dbg:
                        break
                p.barrier()

        with contextlib.ExitStack() as ph:
            R = alloc_proj(ph, 0, mla_wo_bf, "mla_wo_bf")
            xs = [sb("xs%d" % i, [128, D], F32, ph) for i in range(2)]
            ot = [sb("ot%d" % i, [128, 16, 128], BF16, ph) for i in range(2)]
            OTv = OTs.rearrange("(k p) c -> p k c", p=128)
            for t in range(NT):
                s = t % 2
                typ = 1 if t >= NL else 0
                okeys = tuple(("OTs", h, t // 4, typ) for h in range(H))
                p.dma("sp", ot[s][:, :, :], T(OTv[:, :, t * 128:(t + 1) * 128], okeys), "ot%d" % s)
                p.dma("sp", xs[s][:, :], T(xo[t * 128:(t + 1) * 128, :], ()), "xs%d" % s)
                proj_res_ln2(0, t, ot[s], xs[s], typ, R[s], [0, 1, 2, 3] if s == 0 else [4, 5, 6, 7], t)
            p.barrier()

        if "STOP_E1" in dbg:
            p.flush()
            return nc, dm

        def final0(row, orow, x_t, sk):
            p.dma("sp", T(X2[row * 128:(row + 1) * 128, :], (("X2", row),)), x_t[:, :], sk)

        ffn(0, [[(t, 1 if t >= NL else 0, None) for t in range(b * 4, b * 4 + 4)] for b in range(NSUP)], final0)

        if "STOP_L0" in dbg:
            p.fence("sp", [T(X2, tuple(("X2", t) for t in range(NT)))])
            p.flush()
            return nc, dm

        L1 = contextlib.ExitStack()
        sinkexp = sb("sinkexp", [128, HS], F32, L1)
        p.dma("sp", T(sinkexp.ap.unsqueeze(1), sinkexp.keys), T(swa_sink.rearrange("(o n) -> o n", o=1).partition_broadcast(128), ()), "c6")
        p.act(sinkexp[:, :], sinkexp[:, :], AF.Exp)

        with contextlib.ExitStack() as ph:
            A1c = sb("A1c", [128, 2, 16], F32, ph)
            B1c = sb("B1c", [128, 2, 16], F32, ph)
            load_cols(A1c, 1, 1, sk="mcA")
            load_cols(B1c, 1, 0, sk="mcB")
            wqkv = sb("wqkv", [128, 16, 2560], BF16, ph)
            wqv = wqkv_bf.rearrange("(k p) c -> p k c", p=128)
            for k in range(0, 16, 4):
                p.dma("sp", wqkv[:, k:k + 4, :], T(wqv[:, k:k + 4, :], ("wqkv_bf",)), "c4")
            xs = [sb("xs%d" % i, [128, D], F32, ph) for i in range(2)]
            junk = sb("junk", [128, D], BF16, ph)
            xn = [sb("xn%d" % i, [128, D], BF16, ph) for i in range(2)]
            ms = [sb("ms%d" % i, [128, 4], F32, ph) for i in range(2)]
            rstd = [sb("rstd%d" % i, [128, 4], F32, ph) for i in range(2)]
            hTt = [sb("hTt%d" % i, [128, 16, 128], BF16, ph) for i in range(2)]
            co = [sb("co%d" % i, [128, 64], F32, ph) for i in range(2)]
            so = [sb("so%d" % i, [128, 64], F32, ph) for i in range(2)]
            ra = [sb("ra%d" % i, [128, 512], F32, ph) for i in range(2)]
            rb = [sb("rb%d" % i, [128, 512], F32, ph) for i in range(2)]
            qtok = [sb("qtok%d" % i, [128, HS, 128], BF16, ph) for i in range(2)]
            ktok = [sb("ktok%d" % i, [128, KVH, 128], BF16, ph) for i in range(2)]
            for i in range(2):
                p.memset("dve", qtok[i][:, :, :], 0.0)
                p.memset("dve", ktok[i][:, :, :], 0.0)
            qTt = [sb("qTt%d" % i, [64, HS * 128], BF16, ph) for i in range(2)]
            kTt = [sb("kTt%d" % i, [64, KVH * 128], BF16, ph) for i in range(2)]
            vtt = [sb("vtt%d" % i, [128, KVH * 65], BF16, ph) for i in range(2)]
            for i in range(2):
                p.memset("dve", vtt[i][:, :], 1.0)

            def rope_tok(ps, ncol, dst, s):
                nh = ncol // 64
                pv = ps.ap[:, 0:ncol].rearrange("p (h c) -> p h c", c=64)
                av = ra[s].ap[:, 0:ncol].rearrange("p (h c) -> p h c", c=64)
                bv = rb[s].ap[:, 0:ncol].rearrange("p (h c) -> p h c", c=64)
                cb = co[s].ap.unsqueeze(1).to_broadcast([128, nh, 64])
                sb1 = so[s].ap[:, 0:32].unsqueeze(1).to_broadcast([128, nh, 32])
                sb2 = so[s].ap[:, 32:64].unsqueeze(1).to_broadcast([128, nh, 32])
                p.tt("dve", T(av, ra[s].keys), T(pv, ps.keys), T(cb, co[s].keys), ALU.mult)
                p.tt("dve", T(bv[:, :, 0:32], rb[s].keys), T(pv[:, :, 32:64], ps.keys), T(sb1, so[s].keys), ALU.mult)
                p.tt("dve", T(bv[:, :, 32:64], rb[s].keys), T(pv[:, :, 0:32], ps.keys), T(sb2, so[s].keys), ALU.mult)
                p.tt("pool", dst, T(av, ra[s].keys), T(bv, rb[s].keys), ALU.add)

            for t in range(NT):
                s = t % 2
                typ = 1 if t >= NL else 0
                p.dma("sp", xs[s][:, :], T(X2[t * 128:(t + 1) * 128, :], (("X2", t),)), "xs%d" % s)
                p.dma("sp", co[s][:, :], T(coso2[t * 128:(t + 1) * 128, :], ()), "co%d" % s)
                p.dma("sp", so[s][:, :], T(sino2[t * 128:(t + 1) * 128, :], ()), "so%d" % s)
                ln_T(xs[s], hTt[s], A1c, B1c, typ, (junk, ms[s], rstd[s], xn[s]))
                pq = [bank() for _ in range(5)]
                for nb in range(5):
                    for k in range(16):
                        p.mm(pq[nb][:, :], hTt[s][:, k, :], wqkv[:, k, nb * 512:(nb + 1) * 512], k == 0, k == 15)
                for nb in range(4):
                    rope_tok(pq[nb], 512, qtok[s][:, nb * 8:(nb + 1) * 8, 0:64], s)
                rope_tok(pq[4], 256, ktok[s][:, :, 0:64], s)
                v1d = T(vtt[s].ap.rearrange("p (k c) -> p k c", c=65)[:, :, 0:64], vtt[s].keys)
                p.cp("act", v1d, T(pq[4].ap[:, 256:512].rearrange("p (k c) -> p k c", c=64), pq[4].keys), after=[rb[s]])
                p.dma("sp", T(V1s[t, :, :], (("V1s", t),)), vtt[s][:, :], "vtt%d" % s)
                for g in range(4):
                    pb = bank_bf(bank())
                    for hh in range(8):
                        hq = g * 8 + hh
                        p.tr(pb[:, hh * 128:(hh + 1) * 128], qtok[s][:, hq, :], ident_b[:, :])
                    if g % 2 == 0:
                        p.cp("act", qTt[s][:, g * 1024:(g + 1) * 1024], pb[0:64, :])
                    else:
                        p.cp("dve", qTt[s][:, g * 1024:(g + 1) * 1024], pb[0:64, :])
                pb = bank_bf(bank())
                for kv in range(KVH):
                    p.tr(pb[:, kv * 128:(kv + 1) * 128], ktok[s][:, kv, :], ident_b[:, :])
                p.cp("act", kTt[s][:, :], pb[0:64, 0:512])
                p.dma("sp", T(K1Ts[t, :, :], (("K1Ts", t),)), kTt[s][:, :], "kTt%d" % s)
                p.dma("sp", T(Q1T[t, :, :], (("Q1T", t),)), qTt[s][:, :], "qTt%d" % s)
            p.barrier()

        if "STOP_A1" in dbg:
            p.flush()
            L1.close()
            return nc, dm

        with contextlib.ExitStack() as ph:
            R = alloc_proj(ph, 1, swa_wo_bf, "swa_wo_bf")
            mk = sb("mk", [128, 4, 128], BF16, ph)
            mkf = sb("mkf", [128, 4, 128], F32, ph)
            p.dma("sp", mkf[:, :, :], T(masks.rearrange("m k q -> k m q"), ()), "c6")
            p.cp("dve", mk[:, :, :], mkf[:, :, :])
            xs = [sb("xs%d" % i, [128, D], F32, ph) for i in range(2)]
            qT = [sb("qT%d" % i, [128, HS * 128], BF16, ph) for i in range(2)]
            for i in range(2):
                p.memset("dve", qT[i][:, :], 0.0)
            Pb = [sb("Pb%d" % i, [128, 512], BF16, ph) for i in range(4)]
            otok = [sb("otok%d" % i, [128, D], BF16, ph) for i in range(2)]
            oTt = [sb("oTt%d" % i, [128, 16, 128], BF16, ph) for i in range(2)]
            den = [sb("den%d" % i, [128, 4], F32, ph) for i in range(2)]
            ctxK = sb("ctxK", [128, 2, KVH * 128], BF16, ph)
            ctxV = sb("ctxV", [128, 2, KVH * 65], BF16, ph)
            p.memset("dve", ctxK[:, :, :], 0.0)
            p.dma("sp", ctxK[0:64, :, :], T(K1Ts[NT - 2:NT, :, :].rearrange("t d c -> d t c"), (("K1Ts", NT - 2), ("K1Ts", NT - 1))), "c7")
            p.dma("sp", ctxV[:, :, :], T(V1s[NT - 2:NT, :, :].rearrange("t k c -> k t c"), (("V1s", NT - 2), ("V1s", NT - 1))), "c8")
            kwin = [sb("kwin%d" % i, [128, 3, KVH * 128], BF16, ph) for i in range(2)]
            for i in range(2):
                p.memset("dve", kwin[i][:, :, :], 0.0)
            vwin = [sb("vwin%d" % i, [128, 3, KVH * 65], BF16, ph) for i in range(2)]
            cS, cO = [0], [0]
            pc = 0
            dc = 0
            for i in range(1, NL - 1):
                s = i % 2
                p.dma("sp", xs[s][:, :], T(X2[i * 128:(i + 1) * 128, :], (("X2", i),)), "xs%d" % s)
                p.dma("sp", qT[s][0:64, :], T(Q1T[i, :, :], (("Q1T", i),)), "qT%d" % s)
                wkeys = tuple(("K1Ts", tt) for tt in (i - 1, i, i + 1))
                vkeys = tuple(("V1s", tt) for tt in (i - 1, i, i + 1))
                p.dma("sp", kwin[s][0:64, :, :], T(K1Ts[i - 1:i + 2, :, :].rearrange("t d c -> d t c"), wkeys), "kwin%d" % s)
                p.dma("sp", vwin[s][:, :, :], T(V1s[i - 1:i + 2, :, :].rearrange("t k c -> k t c"), vkeys), "vwin%d" % s)
                chunks = [((ctxK, ctxV, 0), None), ((ctxK, ctxV, 1), None), ((kwin[s], vwin[s], 0), 0 if i == 1 else 1),
                          ((kwin[s], vwin[s], 1), None), ((kwin[s], vwin[s], 2), 3 if i == NL - 2 else 2)]
                for kv in range(KVH):
                    for h2 in range(2):
                        hb = kv * 8 + h2 * 4
                        po = pool_bank([4, 5], cO)
                        for ci, (kt, mi) in enumerate(chunks):
                            ps = pool_bank([6, 7], cS)
                            kT_, vT_, ki = kt
                            p.mm(ps[:, :], kT_[:, ki, kv * 128:(kv + 1) * 128],
                                 qT[s][:, hb * 128:(hb + 4) * 128], True, True)
                            P = Pb[pc % 4]
                            pc += 1
                            p.act(P[:, :], ps[:, :], AF.Exp, scale=float(HD) ** -0.5)
                            if mi is not None:
                                pv = P.ap.rearrange("p (h q) -> p h q", q=128)
                                mb = mk.ap[:, mi, :].unsqueeze(1).to_broadcast([128, 4, 128])
                                p.tt("dve", T(pv, P.keys), T(pv, P.keys), T(mb, mk.keys), ALU.mult)
                            for hh in range(4):
                                p.mm(po[:, hh * 65:(hh + 1) * 65], P[:, hh * 128:(hh + 1) * 128],
                                     vT_[:, ki, kv * 65:(kv + 1) * 65],
                                     ci == 0 and hh == 0, ci == 4 and hh == 3, skip=True)
                        d = den[dc % 2]
                        dc += 1
                        pov = po.ap[:, 0:260].rearrange("p (h c) -> p h c", c=65)
                        p.tt("dve", d[:, 0:4], T(pov[:, :, 64], po.keys), sinkexp[:, hb:hb + 4], ALU.add)
                        p.recip(d[:, 0:4], d[:, 0:4])
                        ov = otok[s].ap[:, hb * 64:(hb + 4) * 64].rearrange("p (h c) -> p h c", c=64)
                        db = d.ap[:, 0:4].unsqueeze(2).to_broadcast([128, 4, 64])
                        p.tt("dve", T(ov, otok[s].keys), T(pov[:, :, 0:64], po.keys), T(db, d.keys), ALU.mult)
                yset = [0, 1, 2, 3]
                for g in range(2):
                    pb = bank_bf(T(banks[yset[g + 2]][:, :], (("ps", yset[g + 2]),)))
                    for kk in range(8):
                        k = g * 8 + kk
                        p.tr(pb[:, kk * 128:(kk + 1) * 128], otok[s][:, k * 128:(k + 1) * 128], ident_b[:, :])
                    if g == 0:
                        p.cp("act", T(oTt[s].ap[:, 0:8, :].rearrange("p k c -> p (k c)"), oTt[s].keys), pb[:, :])
                    else:
                        p.cp("dve", T(oTt[s].ap[:, 8:16, :].rearrange("p k c -> p (k c)"), oTt[s].keys), pb[:, :])
                proj_res_ln2(1, i, oTt[s], xs[s], 0, R[s], yset, i)
            p.barrier()
        L1.close()

        if "STOP_A2" in dbg:
            p.flush()
            return nc, dm

        def final1(row, orow, x_t, sk):
            p.dma("sp", T(y_out[orow * 128:(orow + 1) * 128, :], (("y", orow),)), x_t[:, :], sk)

        ffn(1, [[(t, 0, t - 1) for t in range(1 + b * 4, 1 + b * 4 + 4)] for b in range(NLO // 4)], final1)
        p.fence("sp", [T(y_out, tuple(("y", t) for t in range(NLO)))])
        p.flush()
        print("program: %d instructions, %d waits" % (p.n_inst, p.n_wait), "sbuf KB per stack:", [round(v / 1024, 1) for v in p._sbacc.values()])
    return nc, dm


def rope_tables(S_B):
    rows = S_B // GRID_W
    row = np.repeat(np.arange(rows, dtype=np.float32), GRID_W)
    col = np.tile(np.arange(GRID_W, dtype=np.float32), rows)
    n_freq = 16
    freqs = (np.float32(10000.0) ** (-np.arange(n_freq, dtype=np.float32) / np.float32(n_freq))).astype(np.float32)
    ang = np.concatenate([row[:, None] * freqs, col[:, None] * freqs], axis=-1).astype(np.float32)
    return np.cos(ang).astype(np.float32), np.sin(ang).astype(np.float32)


def make_in_maps(inputs, S_B):
    dm = Dims(S_B)
    f = lambda a: np.ascontiguousarray(np.asarray(a, dtype=np.float32))
    x = f(inputs["x"])
    ctx = f(inputs["ctx"])
    c = f(inputs["c"])
    c_ctx = f(inputs["c_ctx"])
    B = x.shape[0]
    cos, sin = rope_tables(S_B)
    cosk2 = np.ones((dm.NKEY, 64), np.float32)
    sink2 = np.zeros((dm.NKEY, 64), np.float32)
    cosk2[:S_B] = np.concatenate([cos, cos], axis=1)
    sink2[:S_B] = np.concatenate([-sin, sin], axis=1)
    ident = np.eye(128, dtype=np.float32)
    sel = np.zeros((2, 256), np.float32)
    sel[0, :128] = 1.0
    sel[1, 128:] = 1.0
    kp = np.arange(128)[:, None]
    qp = np.arange(128)[None, :]
    mL = (qp <= kp).astype(np.float32)
    mR = (kp <= qp).astype(np.float32)
    shared = dict(
        w_mod=f(inputs["w_mod"]), b_mod=f(inputs["b_mod"]), g_norm=f(inputs["g_norm"]),
        w_ff_in=f(inputs["w_ff_in"]), w_ff_out=f(inputs["w_ff_out"]),
        mla_w_in=f(inputs["mla_w_in"])[0], mla_g_qa=f(inputs["mla_g_qa"])[0], mla_g_kva=f(inputs["mla_g_kva"])[0],
        mla_w_qb=f(inputs["mla_w_qb"])[0], mla_w_kvb=f(inputs["mla_w_kvb"])[0], mla_w_out=f(inputs["mla_w_out"])[0],
        swa_w_qkv=f(inputs["swa_w_qkv"])[0], swa_sink=f(inputs["swa_sink"])[0], swa_w_out=f(inputs["swa_w_out"])[0],
        ident=ident, sel=sel, cosk2=cosk2, sink2=sink2,
    )
    own = S_B // 4
    in_maps = []
    xbs = [np.ascontiguousarray(np.concatenate([x[b], ctx[b]], axis=0)) for b in range(B)]
    for core in range(8):
        b, j = core // 4, core % 4
        t0 = j * own
        lo, hi = t0 - 128, t0 + own + 128
        xo = np.zeros((dm.NTOK, D), np.float32)
        pos = np.arange(lo, hi)
        valid = (pos >= 0) & (pos < S_B)
        xo[:dm.NL * 128][valid] = x[b, pos[valid]]
        xo[dm.NL * 128:] = ctx[b]
        cq = np.ones((dm.NTOK, 32), np.float32)
        sq = np.zeros((dm.NTOK, 32), np.float32)
        cq[:dm.NL * 128][valid] = cos[pos[valid]]
        sq[:dm.NL * 128][valid] = sin[pos[valid]]
        m = np.stack([mL if j > 0 else np.zeros_like(mL), mL, mR, mR if j < 3 else np.zeros_like(mR)])
        d = dict(shared)
        d.update(
            xo=xo, xb=xbs[b], cvec=np.ascontiguousarray(np.stack([c[b], c_ctx])),
            cosq2=np.ascontiguousarray(np.concatenate([cq, cq], axis=1).T),
            sinq2=np.ascontiguousarray(np.concatenate([sq, sq], axis=1).T),
            coso2=np.ascontiguousarray(np.concatenate([cq, cq], axis=1)),
            sino2=np.ascontiguousarray(np.concatenate([-sq, sq], axis=1)),
            masks=np.ascontiguousarray(m),
        )
        in_maps.append(d)
    return in_maps


_CACHE = {}


def kernel(**inputs):
    S_B = int(np.asarray(inputs["x"]).shape[1])
    if S_B not in _CACHE:
        _CACHE[S_B] = build(S_B)
    nc, dm = _CACHE[S_B]
    in_maps = make_in_maps(inputs, S_B)
    res = run_bass_kernel_spmd(nc, in_maps, core_ids=list(range(8)))
    B = 2
    out = np.zeros((B, S_B, D), np.float32)
    own = S_B // 4
    for core in range(8):
        b, j = core // 4, core % 4
        out[b, j * own:(j + 1) * own] = res.results[core]["y_out"]
    return out
```

```python
import contextlib
import numpy as np
import concourse.bass as bass
import concourse.mybir as mybir
from concourse.bass_utils import run_bass_kernel_spmd

F32 = mybir.dt.float32
BF16 = mybir.dt.bfloat16
AF = mybir.ActivationFunctionType
ALU = mybir.AluOpType
AX = mybir.AxisListType

D = 2048
KC = 16
DFF = 8192
HC = 64
EPS = 1e-6
H = 16
NOPE = 128
ROPE = 64
DV = 128
HS = 32
KVH = 4
HD = 64
CTX = 256
GRID_W = 64


class T:
    __slots__ = ("ap", "keys")

    def __init__(self, ap, keys):
        self.ap = ap
        self.keys = tuple(keys)

    def __getitem__(self, idx):
        return T(self.ap[idx], self.keys)

    def wk(self, *keys):
        return T(self.ap, keys)

    def v(self, ap):
        return T(ap, self.keys)


class Node:
    __slots__ = ("eng", "fn", "deps", "is_dma", "sem", "cnt", "signal", "sigidx", "cover", "emitted", "tag")


COMPUTE = ("pe", "act", "dve", "pool")


class Prog:
    def __init__(self, nc, es):
        self.nc = nc
        self.es = es
        self.engs = dict(pe=nc.tensor, act=nc.scalar, dve=nc.vector, pool=nc.gpsimd, sp=nc.sync)
        self.esem = {e: es.enter_context(nc.semaphore("sem_" + e)) for e in COMPUTE}
        self.sigcount = {e: 0 for e in COMPUTE}
        self.kw = {}
        self.kr = {}
        self.pending = []
        self.waited = {e: {} for e in self.engs}
        self.dsem = {}
        self.n_inst = 0
        self.n_wait = 0
        self.last_node = {}
        self.last_dma = {}
        self._bank = 0

    def op(self, eng, fn, reads, writes, semkey=None):
        n = Node()
        n.eng = eng
        n.fn = fn
        n.is_dma = semkey is not None
        n.tag = getattr(self, "tag", None)
        n.signal = False
        n.sigidx = None
        n.cover = None
        n.emitted = False
        n.sem = None
        n.cnt = None
        if n.is_dma:
            if semkey not in self.dsem:
                self.dsem[semkey] = [self.es.enter_context(self.nc.semaphore("d_" + str(len(self.dsem)))), 0]
            ent = self.dsem[semkey]
            ent[1] += 16
            n.sem = ent[0]
            n.cnt = ent[1]
        deps = {}
        rk = [k for t in reads for k in t.keys]
        wkeys = [k for t in writes for k in t.keys]
        for k in rk:
            w = self.kw.get(k)
            if w is not None:
                deps[id(w)] = (w, True)
        for k in wkeys:
            w = self.kw.get(k)
            if w is not None and id(w) not in deps:
                deps[id(w)] = (w, False)
            for r in self.kr.get(k, {}).values():
                if id(r) not in deps:
                    deps[id(r)] = (r, False)
        ekey = ("dma", id(n)) if n.is_dma else eng
        for k in rk:
            self.kr.setdefault(k, {})[ekey] = n
        for k in wkeys:
            self.kw[k] = n
            self.kr[k] = {}
        fd = []
        for (dn, raw) in deps.values():
            if dn is n:
                continue
            if (not dn.is_dma) and (not n.is_dma) and dn.eng == eng:
                if eng == "pe" or not raw:
                    continue
            fd.append(dn)
        n.deps = fd
        self.pending.append(n)
        if fn is not None:
            if n.is_dma:
                self.last_dma[semkey] = n
            else:
                self.last_node[eng] = n
        return n

    def barrier(self):
        deps = list(self.last_node.values()) + [d for k, d in self.last_dma.items() if not k.startswith("cast_")]
        for e in ("pe", "act", "dve", "pool", "sp"):
            n = self.op(e, None, [], [])
            n.deps = [d for d in deps]
        self.flush()

    def flush(self):
        last = {}
        for n in self.pending:
            for d in n.deps:
                if not d.is_dma and not d.emitted:
                    d.signal = True
            if not n.is_dma and n.fn is not None:
                last[n.eng] = n
        for n in last.values():
            n.signal = True
        for n in self.pending:
            e = self.engs[n.eng]
            need = {}
            for d in n.deps:
                if d.is_dma:
                    sem, val = d.sem, d.cnt
                else:
                    sem = self.esem[d.eng]
                    val = d.sigidx if d.sigidx is not None else d.cover
                    assert val is not None
                key = id(sem)
                if key not in need or need[key][1] < val:
                    need[key] = (sem, val)
            wt = self.waited[n.eng]
            if n.tag:
                print("DBG", n.tag, n.eng, "deps", [(d.eng, d.tag, d.sigidx, d.cover, d.cnt) for d in n.deps], "need", [(s_.name, v_) for s_, v_ in need.values()], "waited", {k_: v_ for k_, v_ in wt.items()})
            for key, (sem, val) in need.items():
                if wt.get(key, 0) >= val:
                    continue
                e.wait_ge(sem, val)
                self.n_wait += 1
                wt[key] = val
            if n.fn is not None:
                ins = n.fn(e)
                self.n_inst += 1
                if n.is_dma:
                    ins.then_inc(n.sem, 16)
                elif n.signal:
                    self.sigcount[n.eng] += 1
                    n.sigidx = self.sigcount[n.eng]
                    ins.then_inc(self.esem[n.eng], 1)
                    if n.tag:
                        print("DBG  signal", n.tag, n.eng, n.sigidx)
            n.emitted = True
        nxt = {}
        for n in reversed(self.pending):
            if n.is_dma or n.fn is None:
                continue
            if n.sigidx is not None:
                nxt[n.eng] = n.sigidx
            else:
                n.cover = nxt[n.eng]
        for n in self.pending:
            n.fn = None
            n.deps = None
        self.pending = []

    def mm(self, out, lhsT, rhs, start, stop, skip=False):
        o, l, r = out.ap, lhsT.ap, rhs.ap
        if skip:
            f = lambda e: e.matmul(o, l, r, start=start, stop=stop, skip_group_check=True)
        else:
            f = lambda e: e.matmul(o, l, r, start=start, stop=stop)
        return self.op("pe", f, [lhsT, rhs], [out])

    def tr(self, out, in_, ident):
        o, i, d = out.ap, in_.ap, ident.ap
        return self.op("pe", lambda e: e.transpose(o, i, d), [in_, ident], [out])

    def act(self, out, in_, func, bias=0.0, scale=1.0, accum=None):
        reads = [in_]
        writes = [out]
        b = bias
        s = scale
        if isinstance(bias, T):
            reads.append(bias)
            b = bias.ap
        if isinstance(scale, T):
            reads.append(scale)
            s = scale.ap
        a = None
        if accum is not None:
            writes.append(accum)
            a = accum.ap
        o, i = out.ap, in_.ap
        if a is None:
            f = lambda e: e.activation(out=o, in_=i, func=func, bias=b, scale=s)
        else:
            f = lambda e: e.activation(out=o, in_=i, func=func, bias=b, scale=s, accum_out=a)
        return self.op("act", f, reads, writes)

    def ts(self, eng, out, in0, s1, s2, op0, op1=None):
        reads = [in0]
        a1, a2 = s1, s2
        if isinstance(s1, T):
            reads.append(s1)
            a1 = s1.ap
        if isinstance(s2, T):
            reads.append(s2)
            a2 = s2.ap
        o, i = out.ap, in0.ap
        if op1 is None:
            f = lambda e: e.tensor_scalar(out=o, in0=i, scalar1=a1, scalar2=None, op0=op0)
        else:
            f = lambda e: e.tensor_scalar(out=o, in0=i, scalar1=a1, scalar2=a2, op0=op0, op1=op1)
        return self.op(eng, f, reads, [out])

    def tt(self, eng, out, in0, in1, op):
        o, a, b = out.ap, in0.ap, in1.ap
        return self.op(eng, lambda e: e.tensor_tensor(out=o, in0=a, in1=b, op=op), [in0, in1], [out])

    def stt(self, eng, out, in0, scalar, in1, op0, op1):
        reads = [in0, in1]
        s = scalar
        if isinstance(scalar, T):
            reads.append(scalar)
            s = scalar.ap
        o, a, b = out.ap, in0.ap, in1.ap
        return self.op(eng, lambda e: e.scalar_tensor_tensor(out=o, in0=a, scalar=s, in1=b, op0=op0, op1=op1),
                       reads, [out])

    def cp(self, eng, out, in_, after=()):
        o, i = out.ap, in_.ap
        if eng == "act":
            return self.op("act", lambda e: e.copy(out=o, in_=i), [in_] + list(after), [out])
        return self.op(eng, lambda e: e.tensor_copy(out=o, in_=i), [in_] + list(after), [out])

    def recip(self, out, in_):
        o, i = out.ap, in_.ap
        return self.op("dve", lambda e: e.reciprocal(out=o, in_=i), [in_], [out])

    def reduce_sum(self, out, in_):
        o, i = out.ap, in_.ap
        return self.op("dve", lambda e: e.reduce_sum(out=o, in_=i, axis=AX.X), [in_], [out])

    def memset(self, eng, out, val):
        o = out.ap
        return self.op(eng, lambda e: e.memset(o, val), [], [out])

    def dma(self, q, out, in_, semkey, slow=False, maxlast=None):
        o, i = out.ap, in_.ap
        kw = {}
        if slow:
            kw["allow_slow_non_contiguous"] = True
        if maxlast is not None:
            kw["max_dma_last_dim"] = maxlast
        return self.op(q, lambda e: e.dma_start(out=o, in_=i, **kw), [in_], [out], semkey=semkey)

    def fence(self, eng, reads):
        return self.op(eng, None, reads, [])


class Dims:
    def __init__(self, S_B):
        self.S_B = S_B
        self.NKT = S_B // 128 + 2
        self.NKEY = self.NKT * 128
        self.HALFC = self.NKT // 2
        self.HK = self.HALFC * 128
        self.NLO = S_B // 4 // 128
        self.NL = self.NLO + 2
        self.NT = self.NL + 2
        self.NTOK = self.NT * 128
        self.NSUP = self.NT // 4
        assert self.NT % 4 == 0 and self.NKT % 2 == 0
        gs = 1
        for g in (13, 5, 3, 2, 1):
            if self.HALFC % g == 0:
                gs = g
                break
        self.GS = gs
        self.NG = self.HALFC // gs


def build(S_B, dbg=None):
    dm = Dims(S_B)
    NKT, NKEY, HALFC, HK, NLO, NL, NT, NTOK, NSUP = (dm.NKT, dm.NKEY, dm.HALFC, dm.HK, dm.NLO, dm.NL,
                                                      dm.NT, dm.NTOK, dm.NSUP)
    nc = bass.Bass("TRN2", target_bir_lowering=False)
    dbg = dbg or ()

    def din(name, shape, dt=F32):
        return nc.dram_tensor(name, list(shape), dt, kind="ExternalInput").ap()

    def dscr(name, shape, dt):
        kind = "ExternalOutput" if name in dbg else "Internal"
        return nc.dram_tensor(name, list(shape), dt, kind=kind).ap()

    xo = din("xo", [NTOK, D])
    xb = din("xb", [NKEY, D])
    cvec = din("cvec", [2, D])
    w_mod = din("w_mod", [2, D, 6 * D])
    b_mod = din("b_mod", [2, 6 * D])
    g_norm = din("g_norm", [2, 4, D])
    w_ff_in = din("w_ff_in", [2, D, DFF])
    w_ff_out = din("w_ff_out", [2, DFF, D])
    mla_w_in = din("mla_w_in", [D, 1088])
    mla_g_qa = din("mla_g_qa", [512])
    mla_g_kva = din("mla_g_kva", [512])
    mla_w_qb = din("mla_w_qb", [512, 3072])
    mla_w_kvb = din("mla_w_kvb", [512, 4096])
    mla_w_out = din("mla_w_out", [2048, 2048])
    swa_w_qkv = din("swa_w_qkv", [2048, 2560])
    swa_sink = din("swa_sink", [32])
    swa_w_out = din("swa_w_out", [2048, 2048])
    ident_in = din("ident", [128, 128])
    sel_in = din("sel", [2, 256])
    cosk2 = din("cosk2", [NKEY, 64])
    sink2 = din("sink2", [NKEY, 64])
    cosq2 = din("cosq2", [64, NTOK])
    sinq2 = din("sinq2", [64, NTOK])
    coso2 = din("coso2", [NTOK, 64])
    sino2 = din("sino2", [NTOK, 64])
    masks = din("masks", [4, 128, 128])
    y_out = nc.dram_tensor("y_out", [NLO * 128, D], F32, kind="ExternalOutput").ap()

    w_in_bf = dscr("w_in_bf", [D, 1088], BF16)
    w_qb_bf = dscr("w_qb_bf", [512, 3072], BF16)
    w_kvb_bf = dscr("w_kvb_bf", [512, 4096], BF16)
    mla_wo_bf = dscr("mla_wo_bf", [2048, 2048], BF16)
    w1_bf = dscr("w1_bf", [2, D, DFF], BF16)
    w2_bf = dscr("w2_bf", [2, DFF, D], BF16)
    wqkv_bf = dscr("wqkv_bf", [2048, 2560], BF16)
    swa_wo_bf = dscr("swa_wo_bf", [2048, 2048], BF16)
    modvec = dscr("modvec", [2, 6, 2, D], F32)
    KTs = dscr("KTs", [H, 128, NKEY], BF16)
    Vs = dscr("Vs", [NKEY, H * DV], BF16)
    QTN = dscr("QTN", [H, 128, NTOK], BF16)
    QTR = dscr("QTR", [H, 64, NTOK], BF16)
    OTs = dscr("OTs", [H * DV, NTOK], BF16)
    X1 = dscr("X1", [NTOK, D], F32)
    FT = dscr("FT", [D, NTOK], BF16)
    X2 = dscr("X2", [NTOK, D], F32)
    Q1T = dscr("Q1T", [NT, 64, HS * 128], BF16)
    K1Ts = dscr("K1Ts", [NT, 64, KVH * 128], BF16)
    V1s = dscr("V1s", [NT, 128, KVH * 65], BF16)

    es = contextlib.ExitStack()
    with es:
        p = Prog(nc, es)

        L0s = [None]

        def sb(name, shape, dt, stack=None):
            p._uid = getattr(p, "_uid", 0) + 1
            uname = "sb%d_%s" % (p._uid, name)
            nbytes = int(np.prod(shape[1:])) * (2 if dt == BF16 else 4)
            acc = p.__dict__.setdefault("_sbacc", {})
            sid = id(stack or es)
            acc[sid] = acc.get(sid, 0) + nbytes
            p._sbmax = max(getattr(p, "_sbmax", 0), acc.get(id(es), 0) + acc.get(id(L0s[0]), 0) * (L0s[0] is not None and sid != id(L0s[0]) or 0) + acc[sid])
            t = (stack or es).enter_context(nc.sbuf_tensor(uname, list(shape), dt))
            return T(t[tuple(slice(None) for _ in shape)], (uname,))

        banks = [es.enter_context(nc.psum_tensor("bank%d" % i, [128, 512], F32)) for i in range(8)]

        def bank():
            i = p._bank
            p._bank = (i + 1) % 8
            return T(banks[i][:, :], (("ps", i),))

        def bank_bf(t):
            return T(t.ap.bitcast(BF16), t.keys)

        ident_f = sb("ident_f", [128, 128], F32)
        ident_b = sb("ident_b", [128, 128], BF16)
        ones_b = sb("ones_b", [128, 128], BF16)
        sel = sb("sel", [2, 256], F32)
        p.dma("sp", ident_f[:, :], T(ident_in, ()), "c0")
        p.dma("sp", sel[:, :], T(sel_in, ()), "c1")
        p.cp("dve", ident_b[:, :], ident_f[:, :])
        p.memset("dve", ones_b[:, :], 1.0)

        with contextlib.ExitStack() as ph:
            cf = [sb("cf%d" % i, [128, 8192], F32, ph) for i in range(2)]
            cb = [sb("cb%d" % i, [128, 8192], BF16, ph) for i in range(2)]
            cctr = [0]

            def cast_w(src, dst, key, nsplit):
                rows, cols = src.shape
                for r in range(0, rows, 128):
                    i = cctr[0] % 2
                    e = ("dve", "pool", "act")[cctr[0] % 3]
                    cctr[0] += 1
                    p.dma("sp", cf[i][:, 0:cols], T(src[r:r + 128, :], ()), "cf%d" % i)
                    p.cp(e, cb[i][:, 0:cols], cf[i][:, 0:cols])
                    p.dma("sp", T(dst[r:r + 128, :], (key,)), cb[i][:, 0:cols], "cb%d" % i)

            cast_w(mla_w_in, w_in_bf, "w_in_bf", 1)
            cast_w(mla_w_kvb, w_kvb_bf, "w_kvb_bf", 1)
            cast_w(mla_w_qb, w_qb_bf, "w_qb_bf", 1)
            cast_w(mla_w_out, mla_wo_bf, "mla_wo_bf", 1)
            for l in range(2):
                cast_w(w_ff_in[l], w1_bf[l], "w1_bf%d" % l, 4)
                cast_w(w_ff_out[l], w2_bf[l], "w2_bf%d" % l, 4)
                if l == 0:
                    cast_w(swa_w_qkv, wqkv_bf, "wqkv_bf", 1)
                    cast_w(swa_w_out, swa_wo_bf, "swa_wo_bf", 1)
            p.barrier()

        with contextlib.ExitStack() as ph:
            craw = sb("craw", [128, 2, 16], F32, ph)
            s_sb = sb("s_sb", [128, 2, 16], F32, ph)
            p.dma("sp", craw[:, :, :], T(cvec.rearrange("c (p k) -> p c k", k=16), ()), "c2", slow=True)
            p.act(s_sb[:, :, :], craw[:, :, :], AF.Silu)
            wm = [sb("wm%d" % i, [128, 16, 512], F32, ph) for i in range(3)]
            brow = [sb("brow%d" % i, [2, D], F32, ph) for i in range(2)]
            grow = [sb("grow%d" % i, [2, D], F32, ph) for i in range(2)]
            rrow = [sb("rrow%d" % i, [2, D], F32, ph) for i in range(2)]
            cnt = 0
            gidx = {1: 0, 2: 1, 4: 2, 5: 3}
            for l in range(2):
                wv = w_mod[l].rearrange("(p k) n -> p k n", k=16)
                for j in range(6):
                    i2 = (l * 6 + j) % 2
                    for c in range(2):
                        p.dma("sp", brow[i2][c:c + 1, :], T(b_mod[l:l + 1, j * D:(j + 1) * D], ()), "brow%d" % i2)
                        if j in gidx:
                            p.dma("sp", grow[i2][c:c + 1, :], T(g_norm[l, gidx[j]:gidx[j] + 1, :], ()),
                                  "grow%d" % i2)
                    pss = []
                    for nb in range(4):
                        w = wm[cnt % 3]
                        cnt += 1
                        col = j * D + nb * 512
                        p.dma("sp", w[:, :, :], T(wv[:, :, col:col + 512], ()), "wm%d" % ((cnt - 1) % 3))
                        ps = bank()
                        for k in range(16):
                            p.mm(ps[0:2, :], s_sb[:, :, k], w[:, k, :], k == 0, k == 15)
                        pss.append(ps)
                    r = rrow[i2]
                    for nb in range(4):
                        p.tt("dve", r[:, nb * 512:(nb + 1) * 512], pss[nb][0:2, :], brow[i2][:, nb * 512:(nb + 1) * 512],
                             ALU.add)
                    if j in (1, 4):
                        p.stt("dve", r[:, :], r[:, :], 1.0, grow[i2][:, :], ALU.add, ALU.mult)
                    elif j in (2, 5):
                        p.tt("dve", r[:, :], r[:, :], grow[i2][:, :], ALU.mult)
                    p.dma("sp", T(modvec[l, j, :, :], (("modvec", l),)), r[:, :], "rrow%d" % i2)
            p.barrier()

        def load_cols(dst, l, j, q="sp", sk="mc"):
            for c in range(2):
                p.dma(q, dst[:, c, :], T(modvec[l, j, c, :].rearrange("(k p) -> p k", p=128), (("modvec", l),)),
                      sk, slow=True)

        def load_rowbc(dst, l, j, c, sk):
            p.dma("sp", T(dst.ap.unsqueeze(1), dst.keys), T(modvec[l, j, c:c + 1, :].partition_broadcast(128), (("modvec", l),)), sk)

        def rstd_of(dst, ms_):
            p.act(dst, ms_, AF.Sqrt, bias=EPS)
            p.recip(dst, dst)

        def ln_T(x_t, hT_dst, Acol, Bcol, typ, tmp):
            junk, ms, rstd, xn = tmp
            p.act(junk[:, :], x_t[:, :], AF.Square, scale=float(D) ** -0.5, accum=ms[:, 0:1])
            rstd_of(rstd[:, 0:1], ms[:, 0:1])
            p.ts("dve", xn[:, :], x_t[:, :], rstd[:, 0:1], None, ALU.mult)
            for g in range(2):
                pb = bank_bf(bank())
                for kk in range(8):
                    k = g * 8 + kk
                    p.tr(pb[:, kk * 128:(kk + 1) * 128], xn[:, k * 128:(k + 1) * 128], ident_b[:, :])
                for kk in range(8):
                    k = g * 8 + kk
                    src = pb[:, kk * 128:(kk + 1) * 128]
                    if kk % 2 == 0:
                        p.act(hT_dst[:, k, :], src, AF.Identity, bias=Bcol[:, typ, k:k + 1],
                              scale=Acol[:, typ, k:k + 1])
                    else:
                        p.ts("dve", hT_dst[:, k, :], src, Acol[:, typ, k:k + 1], Bcol[:, typ, k:k + 1],
                             ALU.mult, ALU.add)

        L0 = contextlib.ExitStack()
        krT = sb("krT", [128, HK], BF16, L0)
        Acol = sb("Acol", [128, 2, 16], F32, L0)
        Bcol = sb("Bcol", [128, 2, 16], F32, L0)
        load_cols(Acol, 0, 1, sk="mcA")
        load_cols(Bcol, 0, 0, sk="mcB")

        with contextlib.ExitStack() as ph:
            gkva = sb("gkva", [128, 4], F32, ph)
            p.dma("sp", gkva[:, :], T(mla_g_kva.rearrange("(k p) -> p k", p=128), ()), "c3", slow=True)
            win = sb("win", [128, 16, 576], BF16, ph)
            p.dma("sp", win[:, :, :], T(w_in_bf.rearrange("(k p) c -> p k c", p=128)[:, :, 512:1088], ("w_in_bf",)),
                  "c4")
            wk = sb("wk", [128, 4, 2048], BF16, ph)
            wvv = sb("wvv", [128, 4, 2048], BF16, ph)
            kvv = w_kvb_bf.rearrange("(k p) (h two d) -> p k h two d", p=128, two=2, d=128)
            for k4 in range(4):
                p.dma("sp", T(wk.ap[:, k4, :].rearrange("p (h d) -> p h d", d=128), wk.keys),
                      T(kvv[:, k4, :, 0, :], ("w_kvb_bf",)), "c5")
                p.dma("sp", T(wvv.ap[:, k4, :].rearrange("p (h d) -> p h d", d=128), wvv.keys),
                      T(kvv[:, k4, :, 1, :], ("w_kvb_bf",)), "c6")
            cosk = [sb("cosk%d" % i, [128, 64], F32, ph) for i in range(2)]
            sink = [sb("sink%d" % i, [128, 64], F32, ph) for i in range(2)]
            xs = [sb("xs%d" % i, [128, D], F32, ph) for i in range(2)]
            junk = sb("junk", [128, D], BF16, ph)
            xn = [sb("xn%d" % i, [128, D], BF16, ph) for i in range(2)]
            ms = [sb("ms%d" % i, [128, 4], F32, ph) for i in range(2)]
            rstd = [sb("rstd%d" % i, [128, 4], F32, ph) for i in range(2)]
            hT = [sb("hT0", [128, 16, 512], BF16, ph)] * 2
            ckvn = [sb("ckvn%d" % i, [128, 512], BF16, ph) for i in range(2)]
            ckT = [sb("ckT%d" % i, [128, 4, 512], BF16, ph) for i in range(2)]
            krtok = [sb("krtok%d" % i, [128, 128], BF16, ph) for i in range(2)]
            rt1 = [sb("rt1_%d" % i, [128, 64], F32, ph) for i in range(2)]
            rt2 = [sb("rt2_%d" % i, [128, 64], F32, ph) for i in range(2)]
            ktsb = [sb("ktsb0", [128, 16, 512], BF16, ph)] * 2
            vsb = [sb("vsb%d" % i, [128, 2048], BF16, ph) for i in range(2)]
            for i in range(2):
                p.memset("dve", krtok[i][:, :], 0.0)
            nblk = (NKT + 3) // 4
            tcount = 0
            for blk in range(nblk):
                tiles = list(range(blk * 4, min(NKT, blk * 4 + 4)))
                ntk = len(tiles) * 128
                bs = blk % 2
                for ti, t in enumerate(tiles):
                    s = tcount % 2
                    tcount += 1
                    typ = 1 if t >= NKT - 2 else 0
                    half = 0 if t < HALFC else 1
                    lt = t - half * HALFC
                    p.dma("sp", xs[s][:, :], T(xb[t * 128:(t + 1) * 128, :], ()), "xs%d" % s)
                    p.dma("sp", cosk[s][:, :], T(cosk2[t * 128:(t + 1) * 128, :], ()), "cosk%d" % s)
                    p.dma("sp", sink[s][:, :], T(sink2[t * 128:(t + 1) * 128, :], ()), "sink%d" % s)
                    hTd = T(hT[bs].ap[:, :, ti * 128:(ti + 1) * 128], (("hT", 0, ti),))
                    ln_T(xs[s], hTd, Acol, Bcol, typ, (junk, ms[s], rstd[s], xn[s]))
                    psx = bank()
                    psy = bank()
                    for k in range(16):
                        p.mm(psx[:, :], hTd[:, k, :], win[:, k, 0:512], k == 0, k == 15)
                    for k in range(16):
                        p.mm(psy[:, 0:64], hTd[:, k, :], win[:, k, 512:576], k == 0, k == 15)
                    p.act(junk[:, 0:512], psx[:, :], AF.Square, scale=512.0 ** -0.5, accum=ms[s][:, 1:2])
                    rstd_of(rstd[s][:, 1:2], ms[s][:, 1:2])
                    p.ts("dve", ckvn[s][:, :], psx[:, :], rstd[s][:, 1:2], None, ALU.mult)
                    pb = bank_bf(bank())
                    for k4 in range(4):
                        p.tr(pb[:, k4 * 128:(k4 + 1) * 128], ckvn[s][:, k4 * 128:(k4 + 1) * 128], ident_b[:, :])
                    ckd = T(ckT[bs].ap[:, :, ti * 128:(ti + 1) * 128], (("ckT", bs, ti),))
                    for k4 in range(4):
                        p.act(ckd[:, k4, :], pb[:, k4 * 128:(k4 + 1) * 128], AF.Identity, scale=gkva[:, k4:k4 + 1])
                    p.tt("dve", rt1[s][:, :], psy[:, 0:64], cosk[s][:, :], ALU.mult)
                    p.tt("dve", rt2[s][:, 0:32], psy[:, 32:64], sink[s][:, 0:32], ALU.mult)
                    p.tt("dve", rt2[s][:, 32:64], psy[:, 0:32], sink[s][:, 32:64], ALU.mult)
                    p.tt("dve", krtok[s][:, half * 64:(half + 1) * 64], rt1[s][:, :], rt2[s][:, :], ALU.add)
                    pk = bank_bf(bank())
                    p.tr(pk[:, 0:128], krtok[s][:, :], ident_b[:, :])
                    p.cp("act", T(krT.ap[half * 64:(half + 1) * 64, lt * 128:(lt + 1) * 128], (("krT", t),)),
                         pk[half * 64:(half + 1) * 64, 0:128])
                ckb = T(ckT[bs].ap, tuple(("ckT", bs, ti) for ti in range(len(tiles))))
                for h in range(H):
                    ps = bank()
                    for k4 in range(4):
                        p.mm(ps[:, 0:ntk], wk[:, k4, h * 128:(h + 1) * 128], ckb[:, k4, 0:ntk], k4 == 0, k4 == 3)
                    if h % 2 == 0:
                        p.cp("act", ktsb[bs][:, h, 0:ntk], ps[:, 0:ntk])
                    else:
                        p.cp("dve", ktsb[bs][:, h, 0:ntk], ps[:, 0:ntk])
                c0 = blk * 512
                p.dma("sp", T(KTs[:, :, c0:c0 + ntk].rearrange("h d c -> d h c"), (("KTs", blk),)),
                      ktsb[bs][:, :, 0:ntk], "ktsb0")
                for ti, t in enumerate(tiles):
                    s = tcount % 2
                    tcount += 1
                    ckd = T(ckT[bs].ap[:, :, ti * 128:(ti + 1) * 128], (("ckT", bs, ti),))
                    for hg in range(4):
                        ps = bank()
                        for k4 in range(4):
                            p.mm(ps[:, :], ckd[:, k4, :], wvv[:, k4, hg * 512:(hg + 1) * 512], k4 == 0, k4 == 3)
                        if hg % 2 == 0:
                            p.cp("act", vsb[s][:, hg * 512:(hg + 1) * 512], ps[:, :])
                        else:
                            p.cp("dve", vsb[s][:, hg * 512:(hg + 1) * 512], ps[:, :])
                    p.dma("sp", T(Vs[t * 128:(t + 1) * 128, :], (("Vs", t),)), vsb[s][:, :], "vsb%d" % s)
            p.barrier()

        def pool_bank(ids, ctr):
            i = ids[ctr[0] % len(ids)]
            ctr[0] += 1
            return T(banks[i][:, :], (("ps", i),))

        if "STOP_AB" in dbg:
            fin = [T(KTs, tuple(("KTs", b) for b in range((NKT + 3) // 4))), T(Vs, tuple(("Vs", t) for t in range(NKT)))]
            p.fence("sp", fin)
            p.flush()
            L0.close()
            return nc, dm

        with contextlib.ExitStack() as ph:
            gqa = sb("gqa", [128, 4], F32, ph)
            p.dma("sp", gqa[:, :], T(mla_g_qa.rearrange("(k p) -> p k", p=128), ()), "c3", slow=True)
            winq = sb("winq", [128, 16, 512], BF16, ph)
            p.dma("sp", winq[:, :, :], T(w_in_bf.rearrange("(k p) c -> p k c", p=128)[:, :, 0:512], ("w_in_bf",)), "c4")
            wqb = sb("wqb", [128, 4, 3072], BF16, ph)
            p.dma("sp", wqb[:, :, :], T(w_qb_bf.rearrange("(k p) c -> p k c", p=128), ("w_qb_bf",)), "c5")
            wsw = sb("wsw", [128, 4, 1024], BF16, ph)
            for k4 in range(4):
                w4 = wqb.ap[:, k4, :].rearrange("p (h c) -> p h c", c=192)
                s4 = wsw.ap[:, k4, :].rearrange("p (h c) -> p h c", c=64)
                p.ts("dve", T(s4[:, :, 0:32], wsw.keys), T(w4[:, :, 160:192], wqb.keys), -1.0, None, ALU.mult)
                p.cp("dve", T(s4[:, :, 32:64], wsw.keys), T(w4[:, :, 128:160], wqb.keys))
            cq = [sb("cq%d" % i, [64, 512], F32, ph) for i in range(2)]
            sq = [sb("sq%d" % i, [64, 512], F32, ph) for i in range(2)]
            xs = [sb("xs%d" % i, [128, D], F32, ph) for i in range(2)]
            junk = sb("junk", [128, D], BF16, ph)
            xn = [sb("xn%d" % i, [128, D], BF16, ph) for i in range(2)]
            ms = [sb("ms%d" % i, [128, 4], F32, ph) for i in range(2)]
            rstd = [sb("rstd%d" % i, [128, 4], F32, ph) for i in range(2)]
            hT = sb("hT", [128, 16, 512], BF16, ph)
            qan = [sb("qan%d" % i, [128, 512], BF16, ph) for i in range(2)]
            qaT = sb("qaT", [128, 4, 512], BF16, ph)
            qnsb = sb("qnsb", [128, 16, 512], BF16, ph)
            qrsb = sb("qrsb", [64, 16, 512], BF16, ph)
            t1 = [sb("t1_%d" % i, [64, 512], F32, ph) for i in range(2)]
            t2 = [sb("t2_%d" % i, [64, 512], F32, ph) for i in range(2)]
            tcount = 0
            for blk in range(NSUP):
                c0 = blk * 512
                bs = blk % 2
                p.dma("sp", cq[bs][:, :], T(cosq2[:, c0:c0 + 512], ()), "cq%d" % bs)
                p.dma("sp", sq[bs][:, :], T(sinq2[:, c0:c0 + 512], ()), "sq%d" % bs)
                for ti in range(4):
                    t = blk * 4 + ti
                    s = tcount % 2
                    tcount += 1
                    typ = 1 if t >= NL else 0
                    p.dma("sp", xs[s][:, :], T(xo[t * 128:(t + 1) * 128, :], ()), "xs%d" % s)
                    hTd = T(hT.ap[:, :, ti * 128:(ti + 1) * 128], (("hT", ti),))
                    ln_T(xs[s], hTd, Acol, Bcol, typ, (junk, ms[s], rstd[s], xn[s]))
                    psx = bank()
                    for k in range(16):
                        p.mm(psx[:, :], hTd[:, k, :], winq[:, k, :], k == 0, k == 15)
                    p.act(junk[:, 0:512], psx[:, :], AF.Square, scale=512.0 ** -0.5, accum=ms[s][:, 1:2])
                    rstd_of(rstd[s][:, 1:2], ms[s][:, 1:2])
                    p.ts("dve", qan[s][:, :], psx[:, :], rstd[s][:, 1:2], None, ALU.mult)
                    pb = bank_bf(bank())
                    for k4 in range(4):
                        p.tr(pb[:, k4 * 128:(k4 + 1) * 128], qan[s][:, k4 * 128:(k4 + 1) * 128], ident_b[:, :])
                    qad = T(qaT.ap[:, :, ti * 128:(ti + 1) * 128], (("qaT", ti),))
                    for k4 in range(4):
                        p.act(qad[:, k4, :], pb[:, k4 * 128:(k4 + 1) * 128], AF.Identity, scale=gqa[:, k4:k4 + 1])
                qab = T(qaT.ap, tuple(("qaT", ti) for ti in range(4)))
                for h in range(H):
                    hs = h % 2
                    psn = bank()
                    for k4 in range(4):
                        p.mm(psn[:, :], wqb[:, k4, h * 192:h * 192 + 128], qab[:, k4, :], k4 == 0, k4 == 3)
                    qnd = T(qnsb.ap[:, h, :], (("qnsb", h),))
                    p.cp("act", qnd, psn[:, :])
                    psr = bank()
                    pss = bank()
                    for k4 in range(4):
                        p.mm(psr[0:64, :], wqb[:, k4, h * 192 + 128:h * 192 + 192], qab[:, k4, :], k4 == 0, k4 == 3)
                    for k4 in range(4):
                        p.mm(pss[0:64, :], wsw[:, k4, h * 64:(h + 1) * 64], qab[:, k4, :], k4 == 0, k4 == 3)
                    p.tt("dve", t1[hs][:, :], psr[0:64, :], cq[bs][:, :], ALU.mult)
                    p.tt("dve", t2[hs][:, :], pss[0:64, :], sq[bs][:, :], ALU.mult)
                    qrd = T(qrsb.ap[:, h, :], (("qrsb", h),))
                    p.tt("dve", qrd, t1[hs][:, :], t2[hs][:, :], ALU.add)
                allq = tuple(("qnsb", h) for h in range(H))
                allr = tuple(("qrsb", h) for h in range(H))
                p.dma("sp", T(QTN[:, :, c0:c0 + 512].rearrange("h d c -> d h c"), tuple(("QTN", h, blk) for h in range(H))),
                      T(qnsb.ap, allq), "qnsb")
                p.dma("sp", T(QTR[:, :, c0:c0 + 512].rearrange("h d c -> d h c"), tuple(("QTR", h, blk) for h in range(H))),
                      T(qrsb.ap, allr), "qrsb")
            p.barrier()

        if "STOP_C" in dbg:
            p.fence("sp", [T(QTN, tuple(("QTN", h, b) for h in range(H) for b in range(NSUP)))])
            p.flush()
            L0.close()
            return nc, dm

        with contextlib.ExitStack() as ph:
            GS, NG = dm.GS, dm.NG
            kbuf = [sb("kbuf%d" % i, [128, HK], BF16, ph) for i in range(2)]
            vbuf = [sb("vbuf%d" % i, [128, HALFC, 128], BF16, ph) for i in range(2)]
            qn = [sb("qn%d" % i, [128, 512], BF16, ph) for i in range(2)]
            qr = [[sb("qr%d_%d" % (hf, i), [128, 512], BF16, ph) for i in range(2)] for hf in range(2)]
            for hf in range(2):
                for i in range(2):
                    p.memset("dve", qr[hf][i][:, :], 0.0)
            Pb = [sb("Pb%d" % i, [128, 512], BF16, ph) for i in range(4)]
            Oacc = sb("Oacc", [128, NL * 128], F32, ph)
            Sacc = sb("Sacc", [128, NL * 128], F32, ph)
            tmpS = [sb("tmpS%d" % i, [128, 512], F32, ph) for i in range(2)]
            tmpO = [sb("tmpO%d" % i, [128, 512], F32, ph) for i in range(2)]
            osb = [sb("osb%d" % i, [128, 512], BF16, ph) for i in range(2)]
            cS, cO, cZ = [0], [0], [0]
            SC = float(NOPE + ROPE) ** -0.5
            qblocks = []
            q0 = 0
            while q0 < NL * 128:
                nq = min(512, NL * 128 - q0)
                qblocks.append((q0, nq, False))
                q0 += nq
            ctxblk = (NL * 128, 256, True)
            steps = []
            u = 0
            for h in range(H):
                for half in range(2):
                    qbs = qblocks + ([ctxblk] if half == 1 else [])
                    for qi, (q0, nq, isctx) in enumerate(qbs):
                        chunks = [HALFC - 2, HALFC - 1] if isctx else list(range(HALFC))
                        for ci, c in enumerate(chunks):
                            steps.append(dict(u=u, h=h, half=half, qi=qi, q0=q0, nq=nq, isctx=isctx, c=c,
                                              first=ci == 0, last=ci == len(chunks) - 1,
                                              ufirst=(qi == 0 and ci == 0)))
                    u += 1
            qctr = [0]
            state = {}

            def load_unit(st):
                ub = st["u"] % 2
                h, half = st["h"], st["half"]
                for g in range(NG):
                    a = half * HK + g * GS * 128
                    b = a + GS * 128
                    kkeys = tuple(("KTs", bb) for bb in range(a // 512, (b - 1) // 512 + 1))
                    p.dma("sp", T(kbuf[ub].ap[:, g * GS * 128:(g + 1) * GS * 128], (("kb", ub, g),)),
                          T(KTs[h, :, a:b], kkeys), "kb%d_%d" % (ub, g))
                    vkeys = tuple(("Vs", tt) for tt in range(a // 128, b // 128))
                    p.dma("sp", T(vbuf[ub].ap[:, g * GS:(g + 1) * GS, :], (("vb", ub, g),)),
                          T(Vs[a:b, :].rearrange("(c p) d -> p c d", p=128)[:, :, h * 128:(h + 1) * 128], vkeys),
                          "vb%d_%d" % (ub, g))

            ufirsts = [st for st in steps if st["ufirst"]]

            def emit_qk(st):
                ub = st["u"] % 2
                h, half, q0, nq, c = st["h"], st["half"], st["q0"], st["nq"], st["c"]
                if st["first"]:
                    qs = qctr[0] % 2
                    qctr[0] += 1
                    st["qs"] = qs
                    blkq = q0 // 512
                    p.dma("sp", qn[qs][:, 0:nq], T(QTN[h, :, q0:q0 + nq], (("QTN", h, blkq),)), "qn%d" % qs)
                    p.dma("sp", qr[half][qs][half * 64:(half + 1) * 64, 0:nq], T(QTR[h, :, q0:q0 + nq], (("QTR", h, blkq),)),
                          "qr%d_%d" % (half, qs))
                    state[(st["u"], st["qi"])] = dict(qs=qs)
                sd = state[(st["u"], st["qi"])]
                qs = sd["qs"]
                ps = pool_bank([0, 1, 2, 3], cS)
                st["ps"] = ps
                g = c // GS
                p.mm(ps[:, 0:nq], T(kbuf[ub].ap[:, c * 128:(c + 1) * 128], (("kb", ub, g),)), qn[qs][:, 0:nq],
                     True, False)
                tkey = half * HALFC + c
                p.mm(ps[:, 0:nq], T(krT.ap[:, c * 128:(c + 1) * 128], (("krT", c), ("krT", HALFC + c))),
                     qr[half][qs][:, 0:nq], False, True)

            pcount = [0]

            def emit_rest(st):
                ub = st["u"] % 2
                h, half, q0, nq, c = st["h"], st["half"], st["q0"], st["nq"], st["c"]
                sd = state[(st["u"], st["qi"])]
                P = Pb[pcount[0] % 4]
                pcount[0] += 1
                p.act(P[:, 0:nq], st["ps"][:, 0:nq], AF.Exp, scale=SC)
                if st["first"]:
                    sd["psO"] = pool_bank([4, 5], cO)
                    sd["psZ"] = pool_bank([6, 7], cZ)
                psO, psZ = sd["psO"], sd["psZ"]
                g = c // GS
                p.mm(psO[:, 0:nq], T(vbuf[ub].ap[:, c, :], (("vb", ub, g),)), P[:, 0:nq], st["first"], st["last"])
                p.mm(psZ[:, 0:nq], ones_b[:, :], P[:, 0:nq], st["first"], st["last"])
                if st["last"]:
                    fs = sd["qs"]
                    if st["isctx"]:
                        p.recip(tmpS[fs][:, 0:nq], psZ[:, 0:nq])
                        p.tt("dve", osb[fs][:, 0:nq], psO[:, 0:nq], tmpS[fs][:, 0:nq], ALU.mult)
                    elif half == 0:
                        ak = (("acc", st["qi"]),)
                        p.cp("dve", T(Oacc.ap[:, q0:q0 + nq], ak), psO[:, 0:nq])
                        p.cp("dve", T(Sacc.ap[:, q0:q0 + nq], ak), psZ[:, 0:nq])
                        return
                    else:
                        ak = (("acc", st["qi"]),)
                        p.tt("dve", tmpS[fs][:, 0:nq], psZ[:, 0:nq], T(Sacc.ap[:, q0:q0 + nq], ak), ALU.add)
                        p.recip(tmpS[fs][:, 0:nq], tmpS[fs][:, 0:nq])
                        p.tt("dve", tmpO[fs][:, 0:nq], psO[:, 0:nq], T(Oacc.ap[:, q0:q0 + nq], ak), ALU.add)
                        p.tt("dve", osb[fs][:, 0:nq], tmpO[fs][:, 0:nq], tmpS[fs][:, 0:nq], ALU.mult)
                    okey = ("OTs", h, q0 // 512, 1 if st["isctx"] else 0)
                    p.dma("sp", T(OTs[h * 128:(h + 1) * 128, q0:q0 + nq], (okey,)), osb[fs][:, 0:nq], "osb%d" % fs)

            LA = 2
            load_unit(ufirsts[0])
            if len(ufirsts) > 1:
                load_unit(ufirsts[1])
            for i in range(len(steps) + LA):
                if i < len(steps):
                    emit_qk(steps[i])
                if i >= LA:
                    st = steps[i - LA]
                    emit_rest(st)
                    is_ulast = (i - LA + 1 == len(steps)) or steps[i - LA + 1]["u"] != st["u"]
                    if is_ulast and st["u"] + 2 < len(ufirsts):
                        load_unit(ufirsts[st["u"] + 2])
            p.barrier()
        L0.close()

        if "STOP_D" in dbg:
            p.flush()
            return nc, dm

        def proj_res_ln2(l, t, oT_t, x_t, typ, R, yset, dst_row):
            wout, G1, A2, B2, junk, ms, rstd, xn, tmpy, fTt = R
            ys = [T(banks[b][:, :], (("ps", b),)) for b in yset]
            for nb in range(4):
                for k in range(16):
                    p.mm(ys[nb][:, :], oT_t[:, k, :], wout[:, k, nb * 512:(nb + 1) * 512], k == 0, k == 15)
            for nb in range(4):
                p.act(junk[:, nb * 512:(nb + 1) * 512], ys[nb][:, :], AF.Square, scale=float(D) ** -0.5,
                      accum=ms[:, nb:nb + 1])
            p.reduce_sum(ms[:, 4:5], ms[:, 0:4])
            rstd_of(rstd[:, 0:1], ms[:, 4:5])
            for nb in range(4):
                p.stt("dve", tmpy[:, nb * 512:(nb + 1) * 512], ys[nb][:, :], rstd[:, 0:1],
                      G1[typ][:, nb * 512:(nb + 1) * 512], ALU.mult, ALU.mult)
            p.tt("pool", x_t[:, :], x_t[:, :], tmpy[:, :], ALU.add)
            p.dma("sp", T(X1[dst_row * 128:(dst_row + 1) * 128, :], (("X1", dst_row),)), x_t[:, :], "x1st%d" % (t % 2))
            p.act(junk[:, :], x_t[:, :], AF.Square, scale=float(D) ** -0.5, accum=ms[:, 5:6])
            rstd_of(rstd[:, 1:2], ms[:, 5:6])
            p.ts("dve", xn[:, :], x_t[:, :], rstd[:, 1:2], None, ALU.mult)
            for g in range(2):
                pb = bank_bf(ys[g])
                for kk in range(8):
                    k = g * 8 + kk
                    p.tr(pb[:, kk * 128:(kk + 1) * 128], xn[:, k * 128:(k + 1) * 128], ident_b[:, :])
                for kk in range(8):
                    k = g * 8 + kk
                    src = pb[:, kk * 128:(kk + 1) * 128]
                    if kk % 2 == 0:
                        p.act(fTt[:, k, :], src, AF.Identity, bias=B2[:, typ, k:k + 1], scale=A2[:, typ, k:k + 1])
                    else:
                        p.ts("dve", fTt[:, k, :], src, A2[:, typ, k:k + 1], B2[:, typ, k:k + 1], ALU.mult, ALU.add)
            p.dma("sp", T(FT.rearrange("(k p) c -> p k c", p=128)[:, :, dst_row * 128:(dst_row + 1) * 128],
                          (("FT", dst_row),)), fTt[:, :, :], "ftst%d" % (t % 2))

        def alloc_proj(ph, l, wo_bf, wo_key):
            wout = sb("wout", [128, 16, 2048], BF16, ph)
            p.dma("sp", wout[:, :, :], T(wo_bf.rearrange("(k p) c -> p k c", p=128), (wo_key,)), "c4")
            ntyp = 2 if l == 0 else 1
            G1 = [sb("G1_%d" % c, [128, D], F32, ph) for c in range(ntyp)]
            for c in range(ntyp):
                load_rowbc(G1[c], l, 2, c, "c5")
            A2 = sb("A2", [128, 2, 16], F32, ph)
            B2 = sb("B2", [128, 2, 16], F32, ph)
            load_cols(A2, l, 4, sk="mcA")
            load_cols(B2, l, 3, sk="mcB")
            junk = sb("junk", [128, D], BF16, ph)
            tmpy = sb("tmpy", [128, D], F32, ph)
            res = []
            for i in range(2):
                res.append((wout, G1, A2, B2, junk, sb("ms%d" % i, [128, 8], F32, ph), sb("rstd%d" % i, [128, 4], F32, ph),
                            sb("xn%d" % i, [128, D], BF16, ph), tmpy,
                            sb("fTt%d" % i, [128, 16, 128], BF16, ph)))
            return res

        def ffn(l, tile_groups, final):
            with contextlib.ExitStack() as ph:
                ntyp = 2 if l == 0 else 1
                G3 = [sb("G3_%d" % c, [128, D], F32, ph) for c in range(ntyp)]
                for c in range(ntyp):
                    load_rowbc(G3[c], l, 5, c, "c5")
                fTb = sb("fTb", [128, 16, 512], BF16, ph)
                w1s = [sb("w1s%d" % i, [128, 16, 256], BF16, ph) for i in range(2)] * 2
                w2s = [sb("w2s%d" % i, [128, 2, 1024], BF16, ph) for i in range(3)]
                uT = sb("uT", [128, 64, 512], BF16, ph)
                rl = [sb("rl%d" % i, [128, 512], F32, ph) for i in range(2)]
                ysb = [sb("ysb%d" % i, [128, 1024], F32, ph) for i in range(4)]
                xt = [sb("xt%d" % i, [128, D], F32, ph) for i in range(2)]
                junk = sb("junk", [128, 1024], BF16, ph)
                ms = [sb("ms%d" % i, [128, 8], F32, ph) for i in range(4)]
                rstd = [sb("rstd%d" % i, [128, 4], F32, ph) for i in range(4)]
                w1v = w1_bf[l].rearrange("(k p) n -> p k n", p=128)
                w2v = w2_bf[l].rearrange("(j p) n -> p j n", p=128)
                cA = [0]
                c1 = 0
                c2 = 0
                rc = 0
                xc = 0
                for grp in tile_groups:
                    ntk = len(grp) * 128
                    r0 = grp[0][0]
                    p.dma("sp", fTb[:, :, 0:ntk], T(FT.rearrange("(k p) c -> p k c", p=128)[:, :, r0 * 128:r0 * 128 + ntk],
                                                     tuple(("FT", g[0]) for g in grp)), "fTb")
                    for jg in range(32):
                        w = w1s[c1 % 2]
                        p.dma("sp", w[:, :, :], T(w1v[:, :, jg * 256:(jg + 1) * 256], ("w1_bf%d" % l,)), "w1s%d" % (c1 % 2))
                        c1 += 1
                        for j2 in range(2):
                            j = jg * 2 + j2
                            ps = pool_bank([0, 1, 2, 3, 4, 5, 6, 7], cA)
                            for k in range(16):
                                p.mm(ps[:, 0:ntk], w[:, k, j2 * 128:(j2 + 1) * 128], fTb[:, k, 0:ntk], k == 0, k == 15)
                            r = rl[rc % 2]
                            rc += 1
                            p.act(r[:, 0:ntk], ps[:, 0:ntk], AF.Relu)
                            ud = T(uT.ap[:, j, 0:ntk], (("uT", j),))
                            p.tt("dve", ud, r[:, 0:ntk], r[:, 0:ntk], ALU.mult)
                    if "FFN_A" in dbg:
                        break
                    for dh in range(2):
                        p.tag = None
                        yb = [[T(banks[ti * 2 + n2][:, :], (("ps", ti * 2 + n2),)) for n2 in range(2)] for ti in range(len(grp))]
                        for jg in range(32):
                            w = w2s[c2 % 3]
                            p.dma("sp", w[:, :, :], T(w2v[:, jg * 2:(jg + 1) * 2, dh * 1024:(dh + 1) * 1024], ("w2_bf%d" % l,)),
                                  "w2s%d" % (c2 % 3))
                            c2 += 1
                            for j2 in range(2):
                                j = jg * 2 + j2
                                for ti in range(len(grp)):
                                    for n2 in range(2):
                                        p.mm(yb[ti][n2][:, :], T(uT.ap[:, j, ti * 128:(ti + 1) * 128], (("uT", j),)),
                                             w[:, j2, n2 * 512:(n2 + 1) * 512], j == 0, j == 63)
                        for ti, (row, typ, orow) in enumerate(grp):
                            p.tag = "epi_dh%d_ti%d" % (dh, ti) if "FFN_DBG" in dbg else None
                            if "FFN_NOEPI" in dbg:
                                continue
                            for n2 in range(2):
                                if "FFN_NOSQ" in dbg:
                                    continue
                                p.act(junk[:, n2 * 512:(n2 + 1) * 512], yb[ti][n2][:, :], AF.Square, scale=float(D) ** -0.5,
                                      accum=ms[ti][:, dh * 2 + n2:dh * 2 + n2 + 1])
                            if dh == 0 and "FFN_NOCP" in dbg:
                                pass
                            elif dh == 0:
                                for n2 in range(2):
                                    p.cp("dve", ysb[ti][:, n2 * 512:(n2 + 1) * 512], yb[ti][n2][:, :], after=[junk])
                            elif "FFN_B" in dbg:
                                pass
                            else:
                                p.reduce_sum(ms[ti][:, 4:5], ms[ti][:, 0:4])
                                rstd_of(rstd[ti][:, 0:1], ms[ti][:, 4:5])
                                x_t = xt[xc % 2]
                                xc += 1
                                p.dma("sp", x_t[:, :], T(X1[row * 128:(row + 1) * 128, :], (("X1", row),)), "xt%d" % ((xc - 1) % 2))
                                p.stt("dve", ysb[ti][:, :], ysb[ti][:, :], rstd[ti][:, 0:1], G3[typ][:, 0:1024], ALU.mult, ALU.mult)
                                p.tt("pool", x_t[:, 0:1024], x_t[:, 0:1024], ysb[ti][:, :], ALU.add)
                                for n2 in range(2):
                                    p.stt("dve", ysb[ti][:, n2 * 512:(n2 + 1) * 512], yb[ti][n2][:, :], rstd[ti][:, 0:1],
                                          G3[typ][:, 1024 + n2 * 512:1024 + (n2 + 1) * 512], ALU.mult, ALU.mult)
                                p.tt("pool", x_t[:, 1024:2048], x_t[:, 1024:2048], ysb[ti][:, :], ALU.add)
                                final(row, orow, x_t, "xt%d" % ((xc - 1) % 2))
                    if "FFN_G0" in dbg:
                        break
                p.barrier()

        with contextlib.ExitStack() as ph:
            R = alloc_proj(ph, 0, mla_wo_bf, "mla_wo_bf")
            xs = [sb("xs%d" % i, [128, D], F32, ph) for i in range(2)]
            ot = [sb("ot%d" % i, [128, 16, 128], BF16, ph) for i in range(2)]
            OTv = OTs.rearrange("(k p) c -> p k c", p=128)
            for t in range(NT):
                s = t % 2
                typ = 1 if t >= NL else 0
                okeys = tuple(("OTs", h, t // 4, typ) for h in range(H))
                p.dma("sp", ot[s][:, :, :], T(OTv[:, :, t * 128:(t + 1) * 128], okeys), "ot%d" % s)
                p.dma("sp", xs[s][:, :], T(xo[t * 128:(t + 1) * 128, :], ()), "xs%d" % s)
                proj_res_ln2(0, t, ot[s], xs[s], typ, R[s], [0, 1, 2, 3] if s == 0 else [4, 5, 6, 7], t)
            p.barrier()

        if "STOP_E1" in dbg:
            p.flush()
            return nc, dm

        def final0(row, orow, x_t, sk):
            p.dma("sp", T(X2[row * 128:(row + 1) * 128, :], (("X2", row),)), x_t[:, :], sk)

        ffn(0, [[(t, 1 if t >= NL else 0, None) for t in range(b * 4, b * 4 + 4)] for b in range(NSUP)], final0)

        if "STOP_L0" in dbg:
            p.fence("sp", [T(X2, tuple(("X2", t) for t in range(NT)))])
            p.flush()
            return nc, dm

        L1 = contextlib.ExitStack()
        sinkexp = sb("sinkexp", [128, HS], F32, L1)
        p.dma("sp", T(sinkexp.ap.unsqueeze(1), sinkexp.keys), T(swa_sink.rearrange("(o n) -> o n", o=1).partition_broadcast(128), ()), "c6")
        p.act(sinkexp[:, :], sinkexp[:, :], AF.Exp)

        with contextlib.ExitStack() as ph:
            A1c = sb("A1c", [128, 2, 16], F32, ph)
            B1c = sb("B1c", [128, 2, 16], F32, ph)
            load_cols(A1c, 1, 1, sk="mcA")
            load_cols(B1c, 1, 0, sk="mcB")
            wqkv = sb("wqkv", [128, 16, 2560], BF16, ph)
            wqv = wqkv_bf.rearrange("(k p) c -> p k c", p=128)
            for k in range(0, 16, 4):
                p.dma("sp", wqkv[:, k:k + 4, :], T(wqv[:, k:k + 4, :], ("wqkv_bf",)), "c4")
            xs = [sb("xs%d" % i, [128, D], F32, ph) for i in range(2)]
            junk = sb("junk", [128, D], BF16, ph)
            xn = [sb("xn%d" % i, [128, D], BF16, ph) for i in range(2)]
            ms = [sb("ms%d" % i, [128, 4], F32, ph) for i in range(2)]
            rstd = [sb("rstd%d" % i, [128, 4], F32, ph) for i in range(2)]
            hTt = [sb("hTt%d" % i, [128, 16, 128], BF16, ph) for i in range(2)]
            co = [sb("co%d" % i, [128, 64], F32, ph) for i in range(2)]
            so = [sb("so%d" % i, [128, 64], F32, ph) for i in range(2)]
            ra = [sb("ra%d" % i, [128, 512], F32, ph) for i in range(2)]
            rb = [sb("rb%d" % i, [128, 512], F32, ph) for i in range(2)]
            qtok = [sb("qtok%d" % i, [128, HS, 128], BF16, ph) for i in range(2)]
            ktok = [sb("ktok%d" % i, [128, KVH, 128], BF16, ph) for i in range(2)]
            for i in range(2):
                p.memset("dve", qtok[i][:, :, :], 0.0)
                p.memset("dve", ktok[i][:, :, :], 0.0)
            qTt = [sb("qTt%d" % i, [64, HS * 128], BF16, ph) for i in range(2)]
            kTt = [sb("kTt%d" % i, [64, KVH * 128], BF16, ph) for i in range(2)]
            vtt = [sb("vtt%d" % i, [128, KVH * 65], BF16, ph) for i in range(2)]
            for i in range(2):
                p.memset("dve", vtt[i][:, :], 1.0)

            def rope_tok(ps, ncol, dst, s):
                nh = ncol // 64
                pv = ps.ap[:, 0:ncol].rearrange("p (h c) -> p h c", c=64)
                av = ra[s].ap[:, 0:ncol].rearrange("p (h c) -> p h c", c=64)
                bv = rb[s].ap[:, 0:ncol].rearrange("p (h c) -> p h c", c=64)
                cb = co[s].ap.unsqueeze(1).to_broadcast([128, nh, 64])
                sb1 = so[s].ap[:, 0:32].unsqueeze(1).to_broadcast([128, nh, 32])
                sb2 = so[s].ap[:, 32:64].unsqueeze(1).to_broadcast([128, nh, 32])
                p.tt("dve", T(av, ra[s].keys), T(pv, ps.keys), T(cb, co[s].keys), ALU.mult)
                p.tt("dve", T(bv[:, :, 0:32], rb[s].keys), T(pv[:, :, 32:64], ps.keys), T(sb1, so[s].keys), ALU.mult)
                p.tt("dve", T(bv[:, :, 32:64], rb[s].keys), T(pv[:, :, 0:32], ps.keys), T(sb2, so[s].keys), ALU.mult)
                p.tt("pool", dst, T(av, ra[s].keys), T(bv, rb[s].keys), ALU.add)

            for t in range(NT):
                s = t % 2
                typ = 1 if t >= NL else 0
                p.dma("sp", xs[s][:, :], T(X2[t * 128:(t + 1) * 128, :], (("X2", t),)), "xs%d" % s)
                p.dma("sp", co[s][:, :], T(coso2[t * 128:(t + 1) * 128, :], ()), "co%d" % s)
                p.dma("sp", so[s][:, :], T(sino2[t * 128:(t + 1) * 128, :], ()), "so%d" % s)
                ln_T(xs[s], hTt[s], A1c, B1c, typ, (junk, ms[s], rstd[s], xn[s]))
                pq = [bank() for _ in range(5)]
                for nb in range(5):
                    for k in range(16):
                        p.mm(pq[nb][:, :], hTt[s][:, k, :], wqkv[:, k, nb * 512:(nb + 1) * 512], k == 0, k == 15)
                for nb in range(4):
                    rope_tok(pq[nb], 512, qtok[s][:, nb * 8:(nb + 1) * 8, 0:64], s)
                rope_tok(pq[4], 256, ktok[s][:, :, 0:64], s)
                v1d = T(vtt[s].ap.rearrange("p (k c) -> p k c", c=65)[:, :, 0:64], vtt[s].keys)
                p.cp("act", v1d, T(pq[4].ap[:, 256:512].rearrange("p (k c) -> p k c", c=64), pq[4].keys), after=[rb[s]])
                p.dma("sp", T(V1s[t, :, :], (("V1s", t),)), vtt[s][:, :], "vtt%d" % s)
                for g in range(4):
                    pb = bank_bf(bank())
                    for hh in range(8):
                        hq = g * 8 + hh
                        p.tr(pb[:, hh * 128:(hh + 1) * 128], qtok[s][:, hq, :], ident_b[:, :])
                    if g % 2 == 0:
                        p.cp("act", qTt[s][:, g * 1024:(g + 1) * 1024], pb[0:64, :])
                    else:
                        p.cp("dve", qTt[s][:, g * 1024:(g + 1) * 1024], pb[0:64, :])
                pb = bank_bf(bank())
                for kv in range(KVH):
                    p.tr(pb[:, kv * 128:(kv + 1) * 128], ktok[s][:, kv, :], ident_b[:, :])
                p.cp("act", kTt[s][:, :], pb[0:64, 0:512])
                p.dma("sp", T(K1Ts[t, :, :], (("K1Ts", t),)), kTt[s][:, :], "kTt%d" % s)
                p.dma("sp", T(Q1T[t, :, :], (("Q1T", t),)), qTt[s][:, :], "qTt%d" % s)
            p.barrier()

        if "STOP_A1" in dbg:
            p.flush()
            L1.close()
            return nc, dm

        with contextlib.ExitStack() as ph:
            R = alloc_proj(ph, 1, swa_wo_bf, "swa_wo_bf")
            mk = sb("mk", [128, 4, 128], BF16, ph)
            mkf = sb("mkf", [128, 4, 128], F32, ph)
            p.dma("sp", mkf[:, :, :], T(masks.rearrange("m k q -> k m q"), ()), "c6")
            p.cp("dve", mk[:, :, :], mkf[:, :, :])
            xs = [sb("xs%d" % i, [128, D], F32, ph) for i in range(2)]
            qT = [sb("qT%d" % i, [128, HS * 128], BF16, ph) for i in range(2)]
            for i in range(2):
                p.memset("dve", qT[i][:, :], 0.0)
            Pb = [sb("Pb%d" % i, [128, 512], BF16, ph) for i in range(4)]
            otok = [sb("otok%d" % i, [128, D], BF16, ph) for i in range(2)]
            oTt = [sb("oTt%d" % i, [128, 16, 128], BF16, ph) for i in range(2)]
            den = [sb("den%d" % i, [128, 4], F32, ph) for i in range(2)]
            ctxK = sb("ctxK", [128, 2, KVH * 128], BF16, ph)
            ctxV = sb("ctxV", [128, 2, KVH * 65], BF16, ph)
            p.memset("dve", ctxK[:, :, :], 0.0)
            p.dma("sp", ctxK[0:64, :, :], T(K1Ts[NT - 2:NT, :, :].rearrange("t d c -> d t c"), (("K1Ts", NT - 2), ("K1Ts", NT - 1))), "c7")
            p.dma("sp", ctxV[:, :, :], T(V1s[NT - 2:NT, :, :].rearrange("t k c -> k t c"), (("V1s", NT - 2), ("V1s", NT - 1))), "c8")
            kwin = [sb("kwin%d" % i, [128, 3, KVH * 128], BF16, ph) for i in range(2)]
            for i in range(2):
                p.memset("dve", kwin[i][:, :, :], 0.0)
            vwin = [sb("vwin%d" % i, [128, 3, KVH * 65], BF16, ph) for i in range(2)]
            cS, cO = [0], [0]
            pc = 0
            dc = 0
            for i in range(1, NL - 1):
                s = i % 2
                p.dma("sp", xs[s][:, :], T(X2[i * 128:(i + 1) * 128, :], (("X2", i),)), "xs%d" % s)
                p.dma("sp", qT[s][0:64, :], T(Q1T[i, :, :], (("Q1T", i),)), "qT%d" % s)
                wkeys = tuple(("K1Ts", tt) for tt in (i - 1, i, i + 1))
                vkeys = tuple(("V1s", tt) for tt in (i - 1, i, i + 1))
                p.dma("sp", kwin[s][0:64, :, :], T(K1Ts[i - 1:i + 2, :, :].rearrange("t d c -> d t c"), wkeys), "kwin%d" % s)
                p.dma("sp", vwin[s][:, :, :], T(V1s[i - 1:i + 2, :, :].rearrange("t k c -> k t c"), vkeys), "vwin%d" % s)
                chunks = [((ctxK, ctxV, 0), None), ((ctxK, ctxV, 1), None), ((kwin[s], vwin[s], 0), 0 if i == 1 else 1),
                          ((kwin[s], vwin[s], 1), None), ((kwin[s], vwin[s], 2), 3 if i == NL - 2 else 2)]
                for kv in range(KVH):
                    for h2 in range(2):
                        hb = kv * 8 + h2 * 4
                        po = pool_bank([4, 5], cO)
                        for ci, (kt, mi) in enumerate(chunks):
                            ps = pool_bank([6, 7], cS)
                            kT_, vT_, ki = kt
                            p.mm(ps[:, :], kT_[:, ki, kv * 128:(kv + 1) * 128],
                                 qT[s][:, hb * 128:(hb + 4) * 128], True, True)
                            P = Pb[pc % 4]
                            pc += 1
                            p.act(P[:, :], ps[:, :], AF.Exp, scale=float(HD) ** -0.5)
                            if mi is not None:
                                pv = P.ap.rearrange("p (h q) -> p h q", q=128)
                                mb = mk.ap[:, mi, :].unsqueeze(1).to_broadcast([128, 4, 128])
                                p.tt("dve", T(pv, P.keys), T(pv, P.keys), T(mb, mk.keys), ALU.mult)
                            for hh in range(4):
                                p.mm(po[:, hh * 65:(hh + 1) * 65], P[:, hh * 128:(hh + 1) * 128],
                                     vT_[:, ki, kv * 65:(kv + 1) * 65],
                                     ci == 0 and hh == 0, ci == 4 and hh == 3, skip=True)
                        d = den[dc % 2]
                        dc += 1
                        pov = po.ap[:, 0:260].rearrange("p (h c) -> p h c", c=65)
                        p.tt("dve", d[:, 0:4], T(pov[:, :, 64], po.keys), sinkexp[:, hb:hb + 4], ALU.add)
                        p.recip(d[:, 0:4], d[:, 0:4])
                        ov = otok[s].ap[:, hb * 64:(hb + 4) * 64].rearrange("p (h c) -> p h c", c=64)
                        db = d.ap[:, 0:4].unsqueeze(2).to_broadcast([128, 4, 64])
                        p.tt("dve", T(ov, otok[s].keys), T(pov[:, :, 0:64], po.keys), T(db, d.keys), ALU.mult)
                yset = [0, 1, 2, 3]
                for g in range(2):
                    pb = bank_bf(T(banks[yset[g + 2]][:, :], (("ps", yset[g + 2]),)))
                    for kk in range(8):
                        k = g * 8 + kk
                        p.tr(pb[:, kk * 128:(kk + 1) * 128], otok[s][:, k * 128:(k + 1) * 128], ident_b[:, :])
                    if g == 0:
                        p.cp("act", T(oTt[s].ap[:, 0:8, :].rearrange("p k c -> p (k c)"), oTt[s].keys), pb[:, :])
                    else:
                        p.cp("dve", T(oTt[s].ap[:, 8:16, :].rearrange("p k c -> p (k c)"), oTt[s].keys), pb[:, :])
                proj_res_ln2(1, i, oTt[s], xs[s], 0, R[s], yset, i)
            p.barrier()
        L1.close()

        if "STOP_A2" in dbg:
            p.flush()
            return nc, dm

        def final1(row, orow, x_t, sk):
            p.dma("sp", T(y_out[orow * 128:(orow + 1) * 128, :], (("y", orow),)), x_t[:, :], sk)

        ffn(1, [[(t, 0, t - 1) for t in range(1 + b * 4, 1 + b * 4 + 4)] for b in range(NLO // 4)], final1)
        p.fence("sp", [T(y_out, tuple(("y", t) for t in range(NLO)))])
        p.flush()
        print("program: %d instructions, %d waits" % (p.n_inst, p.n_wait), "sbuf KB per stack:", [round(v / 1024, 1) for v in p._sbacc.values()])
    return nc, dm


def rope_tables(S_B):
    rows = S_B // GRID_W
    row = np.repeat(np.arange(rows, dtype=np.float32), GRID_W)
    col = np.tile(np.arange(GRID_W, dtype=np.float32), rows)
    n_freq = 16
    freqs = (np.float32(10000.0) ** (-np.arange(n_freq, dtype=np.float32) / np.float32(n_freq))).astype(np.float32)
    ang = np.concatenate([row[:, None] * freqs, col[:, None] * freqs], axis=-1).astype(np.float32)
    return np.cos(ang).astype(np.float32), np.sin(ang).astype(np.float32)


def make_in_maps(inputs, S_B):
    dm = Dims(S_B)
    f = lambda a: np.ascontiguousarray(np.asarray(a, dtype=np.float32))
    x = f(inputs["x"])
    ctx = f(inputs["ctx"])
    c = f(inputs["c"])
    c_ctx = f(inputs["c_ctx"])
    B = x.shape[0]
    cos, sin = rope_tables(S_B)
    cosk2 = np.ones((dm.NKEY, 64), np.float32)
    sink2 = np.zeros((dm.NKEY, 64), np.float32)
    cosk2[:S_B] = np.concatenate([cos, cos], axis=1)
    sink2[:S_B] = np.concatenate([-sin, sin], axis=1)
    ident = np.eye(128, dtype=np.float32)
    sel = np.zeros((2, 256), np.float32)
    sel[0, :128] = 1.0
    sel[1, 128:] = 1.0
    kp = np.arange(128)[:, None]
    qp = np.arange(128)[None, :]
    mL = (qp <= kp).astype(np.float32)
    mR = (kp <= qp).astype(np.float32)
    shared = dict(
        w_mod=f(inputs["w_mod"]), b_mod=f(inputs["b_mod"]), g_norm=f(inputs["g_norm"]),
        w_ff_in=f(inputs["w_ff_in"]), w_ff_out=f(inputs["w_ff_out"]),
        mla_w_in=f(inputs["mla_w_in"])[0], mla_g_qa=f(inputs["mla_g_qa"])[0], mla_g_kva=f(inputs["mla_g_kva"])[0],
        mla_w_qb=f(inputs["mla_w_qb"])[0], mla_w_kvb=f(inputs["mla_w_kvb"])[0], mla_w_out=f(inputs["mla_w_out"])[0],
        swa_w_qkv=f(inputs["swa_w_qkv"])[0], swa_sink=f(inputs["swa_sink"])[0], swa_w_out=f(inputs["swa_w_out"])[0],
        ident=ident, sel=sel, cosk2=cosk2, sink2=sink2,
    )
    own = S_B // 4
    in_maps = []
    xbs = [np.ascontiguousarray(np.concatenate([x[b], ctx[b]], axis=0)) for b in range(B)]
    for core in range(8):
        b, j = core // 4, core % 4
        t0 = j * own
        lo, hi = t0 - 128, t0 + own + 128
        xo = np.zeros((dm.NTOK, D), np.float32)
        pos = np.arange(lo, hi)
        valid = (pos >= 0) & (pos < S_B)
        xo[:dm.NL * 128][valid] = x[b, pos[valid]]
        xo[dm.NL * 128:] = ctx[b]
        cq = np.ones((dm.NTOK, 32), np.float32)
        sq = np.zeros((dm.NTOK, 32), np.float32)
        cq[:dm.NL * 128][valid] = cos[pos[valid]]
        sq[:dm.NL * 128][valid] = sin[pos[valid]]
        m = np.stack([mL if j > 0 else np.zeros_like(mL), mL, mR, mR if j < 3 else np.zeros_like(mR)])
        d = dict(shared)
        d.update(
            xo=xo, xb=xbs[b], cvec=np.ascontiguousarray(np.stack([c[b], c_ctx])),
            cosq2=np.ascontiguousarray(np.concatenate([cq, cq], axis=1).T),
            sinq2=np.ascontiguousarray(np.concatenate([sq, sq], axis=1).T),
            coso2=np.ascontiguousarray(np.concatenate([cq, cq], axis=1)),
            sino2=np.ascontiguousarray(np.concatenate([-sq, sq], axis=1)),
            masks=np.ascontiguousarray(m),
        )
        in_maps.append(d)
    return in_maps


_CACHE = {}


def kernel(**inputs):
    S_B = int(np.asarray(inputs["x"]).shape[1])
    if S_B not in _CACHE:
        _CACHE[S_B] = build(S_B)
    nc, dm = _CACHE[S_B]
    in_maps = make_in_maps(inputs, S_B)
    res = run_bass_kernel_spmd(nc, in_maps, core_ids=list(range(8)))
    B = 2
    out = np.zeros((B, S_B, D), np.float32)
    own = S_B // 4
    for core in range(8):
        b, j = core // 4, core % 4
        out[b, j * own:(j + 1) * own] = res.results[core]["y_out"]
    return out
```

```python
import contextlib
import numpy as np
import concourse.bass as bass
import concourse.mybir as mybir
from concourse.bass_utils import run_bass_kernel_spmd

F32 = mybir.dt.float32
BF16 = mybir.dt.bfloat16
AF = mybir.ActivationFunctionType
ALU = mybir.AluOpType
AX = mybir.AxisListType

D = 2048
KC = 16
DFF = 8192
HC = 64
EPS = 1e-6
H = 16
NOPE = 128
ROPE = 64
DV = 128
HS = 32
KVH = 4
HD = 64
CTX = 256
GRID_W = 64


class T:
    __slots__ = ("ap", "keys")

    def __init__(self, ap, keys):
        self.ap = ap
        self.keys = tuple(keys)

    def __getitem__(self, idx):
        return T(self.ap[idx], self.keys)

    def wk(self, *keys):
        return T(self.ap, keys)

    def v(self, ap):
        return T(ap, self.keys)


class Node:
    __slots__ = ("eng", "fn", "deps", "is_dma", "sem", "cnt", "signal", "sigidx", "cover", "emitted", "tag")


COMPUTE = ("pe", "act", "dve", "pool")


class Prog:
    def __init__(self, nc, es):
        self.nc = nc
        self.es = es
        self.engs = dict(pe=nc.tensor, act=nc.scalar, dve=nc.vector, pool=nc.gpsimd, sp=nc.sync)
        self.esem = {e: es.enter_context(nc.semaphore("sem_" + e)) for e in COMPUTE}
        self.sigcount = {e: 0 for e in COMPUTE}
        self.kw = {}
        self.kr = {}
        self.pending = []
        self.waited = {e: {} for e in self.engs}
        self.dsem = {}
        self.n_inst = 0
        self.n_wait = 0
        self.last_node = {}
        self.last_dma = {}
        self._bank = 0

    def op(self, eng, fn, reads, writes, semkey=None):
        n = Node()
        n.eng = eng
        n.fn = fn
        n.is_dma = semkey is not None
        n.tag = getattr(self, "tag", None)
        n.signal = False
        n.sigidx = None
        n.cover = None
        n.emitted = False
        n.sem = None
        n.cnt = None
        if n.is_dma:
            if semkey not in self.dsem:
                self.dsem[semkey] = [self.es.enter_context(self.nc.semaphore("d_" + str(len(self.dsem)))), 0]
            ent = self.dsem[semkey]
            ent[1] += 16
            n.sem = ent[0]
            n.cnt = ent[1]
        deps = {}
        rk = [k for t in reads for k in t.keys]
        wkeys = [k for t in writes for k in t.keys]
        for k in rk:
            w = self.kw.get(k)
            if w is not None:
                deps[id(w)] = (w, True)
        for k in wkeys:
            w = self.kw.get(k)
            if w is not None and id(w) not in deps:
                deps[id(w)] = (w, False)
            for r in self.kr.get(k, {}).values():
                if id(r) not in deps:
                    deps[id(r)] = (r, False)
        ekey = ("dma", id(n)) if n.is_dma else eng
        for k in rk:
            self.kr.setdefault(k, {})[ekey] = n
        for k in wkeys:
            self.kw[k] = n
            self.kr[k] = {}
        fd = []
        for (dn, raw) in deps.values():
            if dn is n:
                continue
            if (not dn.is_dma) and (not n.is_dma) and dn.eng == eng:
                if eng == "pe" or not raw:
                    continue
            fd.append(dn)
        n.deps = fd
        self.pending.append(n)
        if fn is not None:
            if n.is_dma:
                self.last_dma[semkey] = n
            else:
                self.last_node[eng] = n
        return n

    def barrier(self):
        deps = list(self.last_node.values()) + [d for k, d in self.last_dma.items() if not k.startswith("cast_")]
        for e in ("pe", "act", "dve", "pool", "sp"):
            n = self.op(e, None, [], [])
            n.deps = [d for d in deps]
        self.flush()

    def flush(self):
        last = {}
        for n in self.pending:
            for d in n.deps:
                if not d.is_dma and not d.emitted:
                    d.signal = True
            if not n.is_dma and n.fn is not None:
                last[n.eng] = n
        for n in last.values():
            n.signal = True
        for n in self.pending:
            e = self.engs[n.eng]
            need = {}
            for d in n.deps:
                if d.is_dma:
                    sem, val = d.sem, d.cnt
                else:
                    sem = self.esem[d.eng]
                    val = d.sigidx if d.sigidx is not None else d.cover
                    assert val is not None
                key = id(sem)
                if key not in need or need[key][1] < val:
                    need[key] = (sem, val)
            wt = self.waited[n.eng]
            if n.tag:
                print("DBG", n.tag, n.eng, "deps", [(d.eng, d.tag, d.sigidx, d.cover, d.cnt) for d in n.deps], "need", [(s_.name, v_) for s_, v_ in need.values()], "waited", {k_: v_ for k_, v_ in wt.items()})
            for key, (sem, val) in need.items():
                if wt.get(key, 0) >= val:
                    continue
                e.wait_ge(sem, val)
                self.n_wait += 1
                wt[key] = val
            if n.fn is not None:
                ins = n.fn(e)
                self.n_inst += 1
                if n.is_dma:
                    ins.then_inc(n.sem, 16)
                elif n.signal:
                    self.sigcount[n.eng] += 1
                    n.sigidx = self.sigcount[n.eng]
                    ins.then_inc(self.esem[n.eng], 1)
                    if n.tag:
                        print("DBG  signal", n.tag, n.eng, n.sigidx)
            n.emitted = True
        nxt = {}
        for n in reversed(self.pending):
            if n.is_dma or n.fn is None:
                continue
            if n.sigidx is not None:
                nxt[n.eng] = n.sigidx
            else:
                n.cover = nxt[n.eng]
        for n in self.pending:
            n.fn = None
            n.deps = None
        self.pending = []

    def mm(self, out, lhsT, rhs, start, stop, skip=False):
        o, l, r = out.ap, lhsT.ap, rhs.ap
        if skip:
            f = lambda e: e.matmul(o, l, r, start=start, stop=stop, skip_group_check=True)
        else:
            f = lambda e: e.matmul(o, l, r, start=start, stop=stop)
        return self.op("pe", f, [lhsT, rhs], [out])

    def tr(self, out, in_, ident):
        o, i, d = out.ap, in_.ap, ident.ap
        return self.op("pe", lambda e: e.transpose(o, i, d), [in_, ident], [out])

    def act(self, out, in_, func, bias=0.0, scale=1.0, accum=None):
        reads = [in_]
        writes = [out]
        b = bias
        s = scale
        if isinstance(bias, T):
            reads.append(bias)
            b = bias.ap
        if isinstance(scale, T):
            reads.append(scale)
            s = scale.ap
        a = None
        if accum is not None:
            writes.append(accum)
            a = accum.ap
        o, i = out.ap, in_.ap
        if a is None:
            f = lambda e: e.activation(out=o, in_=i, func=func, bias=b, scale=s)
        else:
            f = lambda e: e.activation(out=o, in_=i, func=func, bias=b, scale=s, accum_out=a)
        return self.op("act", f, reads, writes)

    def ts(self, eng, out, in0, s1, s2, op0, op1=None):
        reads = [in0]
        a1, a2 = s1, s2
        if isinstance(s1, T):
            reads.append(s1)
            a1 = s1.ap
        if isinstance(s2, T):
            reads.append(s2)
            a2 = s2.ap
        o, i = out.ap, in0.ap
        if op1 is None:
            f = lambda e: e.tensor_scalar(out=o, in0=i, scalar1=a1, scalar2=None, op0=op0)
        else:
            f = lambda e: e.tensor_scalar(out=o, in0=i, scalar1=a1, scalar2=a2, op0=op0, op1=op1)
        return self.op(eng, f, reads, [out])

    def tt(self, eng, out, in0, in1, op):
        o, a, b = out.ap, in0.ap, in1.ap
        return self.op(eng, lambda e: e.tensor_tensor(out=o, in0=a, in1=b, op=op), [in0, in1], [out])

    def stt(self, eng, out, in0, scalar, in1, op0, op1):
        reads = [in0, in1]
        s = scalar
        if isinstance(scalar, T):
            reads.append(scalar)
            s = scalar.ap
        o, a, b = out.ap, in0.ap, in1.ap
        return self.op(eng, lambda e: e.scalar_tensor_tensor(out=o, in0=a, scalar=s, in1=b, op0=op0, op1=op1),
                       reads, [out])

    def cp(self, eng, out, in_, after=()):
        o, i = out.ap, in_.ap
        if eng == "act":
            return self.op("act", lambda e: e.copy(out=o, in_=i), [in_] + list(after), [out])
        return self.op(eng, lambda e: e.tensor_copy(out=o, in_=i), [in_] + list(after), [out])

    def recip(self, out, in_):
        o, i = out.ap, in_.ap
        return self.op("dve", lambda e: e.reciprocal(out=o, in_=i), [in_], [out])

    def reduce_sum(self, out, in_):
        o, i = out.ap, in_.ap
        return self.op("dve", lambda e: e.reduce_sum(out=o, in_=i, axis=AX.X), [in_], [out])

    def memset(self, eng, out, val):
        o = out.ap
        return self.op(eng, lambda e: e.memset(o, val), [], [out])

    def dma(self, q, out, in_, semkey, slow=False, maxlast=None):
        o, i = out.ap, in_.ap
        kw = {}
        if slow:
            kw["allow_slow_non_contiguous"] = True
        if maxlast is not None:
            kw["max_dma_last_dim"] = maxlast
        return self.op(q, lambda e: e.dma_start(out=o, in_=i, **kw), [in_], [out], semkey=semkey)

    def fence(self, eng, reads):
        return self.op(eng, None, reads, [])


class Dims:
    def __init__(self, S_B):
        self.S_B = S_B
        self.NKT = S_B // 128 + 2
        self.NKEY = self.NKT * 128
        self.HALFC = self.NKT // 2
        self.HK = self.HALFC * 128
        self.NLO = S_B // 4 // 128
        self.NL = self.NLO + 2
        self.NT = self.NL + 2
        self.NTOK = self.NT * 128
        self.NSUP = self.NT // 4
        assert self.NT % 4 == 0 and self.NKT % 2 == 0
        gs = 1
        for g in (13, 5, 3, 2, 1):
            if self.HALFC % g == 0:
                gs = g
                break
        self.GS = gs
        self.NG = self.HALFC // gs


def build(S_B, dbg=None):
    dm = Dims(S_B)
    NKT, NKEY, HALFC, HK, NLO, NL, NT, NTOK, NSUP = (dm.NKT, dm.NKEY, dm.HALFC, dm.HK, dm.NLO, dm.NL,
                                                      dm.NT, dm.NTOK, dm.NSUP)
    nc = bass.Bass("TRN2", target_bir_lowering=False)
    dbg = dbg or ()

    def din(name, shape, dt=F32):
        return nc.dram_tensor(name, list(shape), dt, kind="ExternalInput").ap()

    def dscr(name, shape, dt):
        kind = "ExternalOutput" if name in dbg else "Internal"
        return nc.dram_tensor(name, list(shape), dt, kind=kind).ap()

    xo = din("xo", [NTOK, D])
    xb = din("xb", [NKEY, D])
    cvec = din("cvec", [2, D])
    w_mod = din("w_mod", [2, D, 6 * D])
    b_mod = din("b_mod", [2, 6 * D])
    g_norm = din("g_norm", [2, 4, D])
    w_ff_in = din("w_ff_in", [2, D, DFF])
    w_ff_out = din("w_ff_out", [2, DFF, D])
    mla_w_in = din("mla_w_in", [D, 1088])
    mla_g_qa = din("mla_g_qa", [512])
    mla_g_kva = din("mla_g_kva", [512])
    mla_w_qb = din("mla_w_qb", [512, 3072])
    mla_w_kvb = din("mla_w_kvb", [512, 4096])
    mla_w_out = din("mla_w_out", [2048, 2048])
    swa_w_qkv = din("swa_w_qkv", [2048, 2560])
    swa_sink = din("swa_sink", [32])
    swa_w_out = din("swa_w_out", [2048, 2048])
    ident_in = din("ident", [128, 128])
    sel_in = din("sel", [2, 256])
    cosk2 = din("cosk2", [NKEY, 64])
    sink2 = din("sink2", [NKEY, 64])
    cosq2 = din("cosq2", [64, NTOK])
    sinq2 = din("sinq2", [64, NTOK])
    coso2 = din("coso2", [NTOK, 64])
    sino2 = din("sino2", [NTOK, 64])
    masks = din("masks", [4, 128, 128])
    y_out = nc.dram_tensor("y_out", [NLO * 128, D], F32, kind="ExternalOutput").ap()

    w_in_bf = dscr("w_in_bf", [D, 1088], BF16)
    w_qb_bf = dscr("w_qb_bf", [512, 3072], BF16)
    w_kvb_bf = dscr("w_kvb_bf", [512, 4096], BF16)
    mla_wo_bf = dscr("mla_wo_bf", [2048, 2048], BF16)
    w1_bf = dscr("w1_bf", [2, D, DFF], BF16)
    w2_bf = dscr("w2_bf", [2, DFF, D], BF16)
    wqkv_bf = dscr("wqkv_bf", [2048, 2560], BF16)
    swa_wo_bf = dscr("swa_wo_bf", [2048, 2048], BF16)
    modvec = dscr("modvec", [2, 6, 2, D], F32)
    KTs = dscr("KTs", [H, 128, NKEY], BF16)
    Vs = dscr("Vs", [NKEY, H * DV], BF16)
    QTN = dscr("QTN", [H, 128, NTOK], BF16)
    QTR = dscr("QTR", [H, 64, NTOK], BF16)
    OTs = dscr("OTs", [H * DV, NTOK], BF16)
    X1 = dscr("X1", [NTOK, D], F32)
    FT = dscr("FT", [D, NTOK], BF16)
    X2 = dscr("X2", [NTOK, D], F32)
    Q1T = dscr("Q1T", [NT, 64, HS * 128], BF16)
    K1Ts = dscr("K1Ts", [NT, 64, KVH * 128], BF16)
    V1s = dscr("V1s", [NT, 128, KVH * 65], BF16)

    es = contextlib.ExitStack()
    with es:
        p = Prog(nc, es)

        L0s = [None]

        def sb(name, shape, dt, stack=None):
            p._uid = getattr(p, "_uid", 0) + 1
            uname = "sb%d_%s" % (p._uid, name)
            nbytes = int(np.prod(shape[1:])) * (2 if dt == BF16 else 4)
            acc = p.__dict__.setdefault("_sbacc", {})
            sid = id(stack or es)
            acc[sid] = acc.get(sid, 0) + nbytes
            p._sbmax = max(getattr(p, "_sbmax", 0), acc.get(id(es), 0) + acc.get(id(L0s[0]), 0) * (L0s[0] is not None and sid != id(L0s[0]) or 0) + acc[sid])
            t = (stack or es).enter_context(nc.sbuf_tensor(uname, list(shape), dt))
            return T(t[tuple(slice(None) for _ in shape)], (uname,))

        banks = [es.enter_context(nc.psum_tensor("bank%d" % i, [128, 512], F32)) for i in range(8)]

        def bank():
            i = p._bank
            p._bank = (i + 1) % 8
            return T(banks[i][:, :], (("ps", i),))

        def bank_bf(t):
            return T(t.ap.bitcast(BF16), t.keys)

        ident_f = sb("ident_f", [128, 128], F32)
        ident_b = sb("ident_b", [128, 128], BF16)
        ones_b = sb("ones_b", [128, 128], BF16)
        sel = sb("sel", [2, 256], F32)
        p.dma("sp", ident_f[:, :], T(ident_in, ()), "c0")
        p.dma("sp", sel[:, :], T(sel_in, ()), "c1")
        p.cp("dve", ident_b[:, :], ident_f[:, :])
        p.memset("dve", ones_b[:, :], 1.0)

        with contextlib.ExitStack() as ph:
            cf = [sb("cf%d" % i, [128, 8192], F32, ph) for i in range(2)]
            cb = [sb("cb%d" % i, [128, 8192], BF16, ph) for i in range(2)]
            cctr = [0]

            def cast_w(src, dst, key, nsplit):
                rows, cols = src.shape
                for r in range(0, rows, 128):
                    i = cctr[0] % 2
                    e = ("dve", "pool", "act")[cctr[0] % 3]
                    cctr[0] += 1
                    p.dma("sp", cf[i][:, 0:cols], T(src[r:r + 128, :], ()), "cf%d" % i)
                    p.cp(e, cb[i][:, 0:cols], cf[i][:, 0:cols])
                    p.dma("sp", T(dst[r:r + 128, :], (key,)), cb[i][:, 0:cols], "cb%d" % i)

            cast_w(mla_w_in, w_in_bf, "w_in_bf", 1)
            cast_w(mla_w_kvb, w_kvb_bf, "w_kvb_bf", 1)
            cast_w(mla_w_qb, w_qb_bf, "w_qb_bf", 1)
            p.barrier()
        deferred = [(mla_w_out, mla_wo_bf, "mla_wo_bf"), (w_ff_in[0], w1_bf[0], "w1_bf0"), (w_ff_out[0], w2_bf[0], "w2_bf0"),
                    (swa_w_qkv, wqkv_bf, "wqkv_bf"), (swa_w_out, swa_wo_bf, "swa_wo_bf"),
                    (w_ff_in[1], w1_bf[1], "w1_bf1"), (w_ff_out[1], w2_bf[1], "w2_bf1")]
        cast_items = []
        for (src_, dst_, key_) in deferred:
            rows_, cols_ = src_.shape
            for r_ in range(0, rows_, 128):
                for c_ in range(0, cols_, 1024):
                    w_ = min(1024, cols_ - c_)
                    cast_items.append((src_[r_:r_ + 128, c_:c_ + w_], dst_[r_:r_ + 128, c_:c_ + w_], key_, w_))

        with contextlib.ExitStack() as ph:
            craw = sb("craw", [128, 2, 16], F32, ph)
            s_sb = sb("s_sb", [128, 2, 16], F32, ph)
            p.dma("sp", craw[:, :, :], T(cvec.rearrange("c (p k) -> p c k", k=16), ()), "c2", slow=True)
            p.act(s_sb[:, :, :], craw[:, :, :], AF.Silu)
            wm = [sb("wm%d" % i, [128, 16, 512], F32, ph) for i in range(3)]
            brow = [sb("brow%d" % i, [2, D], F32, ph) for i in range(2)]
            grow = [sb("grow%d" % i, [2, D], F32, ph) for i in range(2)]
            rrow = [sb("rrow%d" % i, [2, D], F32, ph) for i in range(2)]
            cnt = 0
            gidx = {1: 0, 2: 1, 4: 2, 5: 3}
            for l in range(2):
                wv = w_mod[l].rearrange("(p k) n -> p k n", k=16)
                for j in range(6):
                    i2 = (l * 6 + j) % 2
                    for c in range(2):
                        p.dma("sp", brow[i2][c:c + 1, :], T(b_mod[l:l + 1, j * D:(j + 1) * D], ()), "brow%d" % i2)
                        if j in gidx:
                            p.dma("sp", grow[i2][c:c + 1, :], T(g_norm[l, gidx[j]:gidx[j] + 1, :], ()),
                                  "grow%d" % i2)
                    pss = []
                    for nb in range(4):
                        w = wm[cnt % 3]
                        cnt += 1
                        col = j * D + nb * 512
                        p.dma("sp", w[:, :, :], T(wv[:, :, col:col + 512], ()), "wm%d" % ((cnt - 1) % 3))
                        ps = bank()
                        for k in range(16):
                            p.mm(ps[0:2, :], s_sb[:, :, k], w[:, k, :], k == 0, k == 15)
                        pss.append(ps)
                    r = rrow[i2]
                    for nb in range(4):
                        p.tt("dve", r[:, nb * 512:(nb + 1) * 512], pss[nb][0:2, :], brow[i2][:, nb * 512:(nb + 1) * 512],
                             ALU.add)
                    if j in (1, 4):
                        p.stt("dve", r[:, :], r[:, :], 1.0, grow[i2][:, :], ALU.add, ALU.mult)
                    elif j in (2, 5):
                        p.tt("dve", r[:, :], r[:, :], grow[i2][:, :], ALU.mult)
                    p.dma("sp", T(modvec[l, j, :, :], (("modvec", l),)), r[:, :], "rrow%d" % i2)
            p.barrier()

        def load_cols(dst, l, j, q="sp", sk="mc"):
            for c in range(2):
                p.dma(q, dst[:, c, :], T(modvec[l, j, c, :].rearrange("(k p) -> p k", p=128), (("modvec", l),)),
                      sk, slow=True)

        def load_rowbc(dst, l, j, c, sk):
            p.dma("sp", T(dst.ap.unsqueeze(1), dst.keys), T(modvec[l, j, c:c + 1, :].partition_broadcast(128), (("modvec", l),)), sk)

        def rstd_of(dst, ms_):
            p.act(dst, ms_, AF.Sqrt, bias=EPS)
            p.recip(dst, dst)

        def ln_T(x_t, hT_dst, Acol, Bcol, typ, tmp):
            junk, ms, rstd, xn = tmp
            p.act(junk[:, :], x_t[:, :], AF.Square, scale=float(D) ** -0.5, accum=ms[:, 0:1])
            rstd_of(rstd[:, 0:1], ms[:, 0:1])
            p.ts("dve", xn[:, :], x_t[:, :], rstd[:, 0:1], None, ALU.mult)
            for g in range(2):
                pb = bank_bf(bank())
                for kk in range(8):
                    k = g * 8 + kk
                    p.tr(pb[:, kk * 128:(kk + 1) * 128], xn[:, k * 128:(k + 1) * 128], ident_b[:, :])
                for kk in range(8):
                    k = g * 8 + kk
                    src = pb[:, kk * 128:(kk + 1) * 128]
                    if kk % 2 == 0:
                        p.act(hT_dst[:, k, :], src, AF.Identity, bias=Bcol[:, typ, k:k + 1],
                              scale=Acol[:, typ, k:k + 1])
                    else:
                        p.ts("dve", hT_dst[:, k, :], src, Acol[:, typ, k:k + 1], Bcol[:, typ, k:k + 1],
                             ALU.mult, ALU.add)

        L0 = contextlib.ExitStack()
        krT = sb("krT", [128, HK], BF16, L0)
        Acol = sb("Acol", [128, 2, 16], F32, L0)
        Bcol = sb("Bcol", [128, 2, 16], F32, L0)
        load_cols(Acol, 0, 1, sk="mcA")
        load_cols(Bcol, 0, 0, sk="mcB")

        with contextlib.ExitStack() as ph:
            gkva = sb("gkva", [128, 4], F32, ph)
            p.dma("sp", gkva[:, :], T(mla_g_kva.rearrange("(k p) -> p k", p=128), ()), "c3", slow=True)
            win = sb("win", [128, 16, 576], BF16, ph)
            p.dma("sp", win[:, :, :], T(w_in_bf.rearrange("(k p) c -> p k c", p=128)[:, :, 512:1088], ("w_in_bf",)),
                  "c4")
            wk = sb("wk", [128, 4, 2048], BF16, ph)
            wvv = sb("wvv", [128, 4, 2048], BF16, ph)
            kvv = w_kvb_bf.rearrange("(k p) (h two d) -> p k h two d", p=128, two=2, d=128)
            for k4 in range(4):
                p.dma("sp", T(wk.ap[:, k4, :].rearrange("p (h d) -> p h d", d=128), wk.keys),
                      T(kvv[:, k4, :, 0, :], ("w_kvb_bf",)), "c5")
                p.dma("sp", T(wvv.ap[:, k4, :].rearrange("p (h d) -> p h d", d=128), wvv.keys),
                      T(kvv[:, k4, :, 1, :], ("w_kvb_bf",)), "c6")
            cosk = [sb("cosk%d" % i, [128, 64], F32, ph) for i in range(2)]
            sink = [sb("sink%d" % i, [128, 64], F32, ph) for i in range(2)]
            xs = [sb("xs%d" % i, [128, D], F32, ph) for i in range(2)]
            junk = sb("junk", [128, D], BF16, ph)
            xn = [sb("xn%d" % i, [128, D], BF16, ph) for i in range(2)]
            ms = [sb("ms%d" % i, [128, 4], F32, ph) for i in range(2)]
            rstd = [sb("rstd%d" % i, [128, 4], F32, ph) for i in range(2)]
            hT = [sb("hT0", [128, 16, 512], BF16, ph)] * 2
            ckvn = [sb("ckvn%d" % i, [128, 512], BF16, ph) for i in range(2)]
            ckT = [sb("ckT%d" % i, [128, 4, 512], BF16, ph) for i in range(2)]
            krtok = [sb("krtok%d" % i, [128, 128], BF16, ph) for i in range(2)]
            rt1 = [sb("rt1_%d" % i, [128, 64], F32, ph) for i in range(2)]
            rt2 = [sb("rt2_%d" % i, [128, 64], F32, ph) for i in range(2)]
            ktsb = [sb("ktsb0", [128, 16, 512], BF16, ph)] * 2
            vsb = [sb("vsb%d" % i, [128, 2048], BF16, ph) for i in range(2)]
            for i in range(2):
                p.memset("dve", krtok[i][:, :], 0.0)
            nblk = (NKT + 3) // 4
            tcount = 0
            for blk in range(nblk):
                tiles = list(range(blk * 4, min(NKT, blk * 4 + 4)))
                ntk = len(tiles) * 128
                bs = blk % 2
                for ti, t in enumerate(tiles):
                    s = tcount % 2
                    tcount += 1
                    typ = 1 if t >= NKT - 2 else 0
                    half = 0 if t < HALFC else 1
                    lt = t - half * HALFC
                    p.dma("sp", xs[s][:, :], T(xb[t * 128:(t + 1) * 128, :], ()), "xs%d" % s)
                    p.dma("sp", cosk[s][:, :], T(cosk2[t * 128:(t + 1) * 128, :], ()), "cosk%d" % s)
                    p.dma("sp", sink[s][:, :], T(sink2[t * 128:(t + 1) * 128, :], ()), "sink%d" % s)
                    hTd = T(hT[bs].ap[:, :, ti * 128:(ti + 1) * 128], (("hT", 0, ti),))
                    ln_T(xs[s], hTd, Acol, Bcol, typ, (junk, ms[s], rstd[s], xn[s]))
                    psx = bank()
                    psy = bank()
                    for k in range(16):
                        p.mm(psx[:, :], hTd[:, k, :], win[:, k, 0:512], k == 0, k == 15)
                    for k in range(16):
                        p.mm(psy[:, 0:64], hTd[:, k, :], win[:, k, 512:576], k == 0, k == 15)
                    p.act(junk[:, 0:512], psx[:, :], AF.Square, scale=512.0 ** -0.5, accum=ms[s][:, 1:2])
                    rstd_of(rstd[s][:, 1:2], ms[s][:, 1:2])
                    p.ts("dve", ckvn[s][:, :], psx[:, :], rstd[s][:, 1:2], None, ALU.mult)
                    pb = bank_bf(bank())
                    for k4 in range(4):
                        p.tr(pb[:, k4 * 128:(k4 + 1) * 128], ckvn[s][:, k4 * 128:(k4 + 1) * 128], ident_b[:, :])
                    ckd = T(ckT[bs].ap[:, :, ti * 128:(ti + 1) * 128], (("ckT", bs, ti),))
                    for k4 in range(4):
                        p.act(ckd[:, k4, :], pb[:, k4 * 128:(k4 + 1) * 128], AF.Identity, scale=gkva[:, k4:k4 + 1])
                    p.tt("dve", rt1[s][:, :], psy[:, 0:64], cosk[s][:, :], ALU.mult)
                    p.tt("dve", rt2[s][:, 0:32], psy[:, 32:64], sink[s][:, 0:32], ALU.mult)
                    p.tt("dve", rt2[s][:, 32:64], psy[:, 0:32], sink[s][:, 32:64], ALU.mult)
                    p.tt("dve", krtok[s][:, half * 64:(half + 1) * 64], rt1[s][:, :], rt2[s][:, :], ALU.add)
                    pk = bank_bf(bank())
                    p.tr(pk[:, 0:128], krtok[s][:, :], ident_b[:, :])
                    p.cp("act", T(krT.ap[half * 64:(half + 1) * 64, lt * 128:(lt + 1) * 128], (("krT", t),)),
                         pk[half * 64:(half + 1) * 64, 0:128])
                ckb = T(ckT[bs].ap, tuple(("ckT", bs, ti) for ti in range(len(tiles))))
                for h in range(H):
                    ps = bank()
                    for k4 in range(4):
                        p.mm(ps[:, 0:ntk], wk[:, k4, h * 128:(h + 1) * 128], ckb[:, k4, 0:ntk], k4 == 0, k4 == 3)
                    if h % 2 == 0:
                        p.cp("act", ktsb[bs][:, h, 0:ntk], ps[:, 0:ntk])
                    else:
                        p.cp("dve", ktsb[bs][:, h, 0:ntk], ps[:, 0:ntk])
                c0 = blk * 512
                p.dma("sp", T(KTs[:, :, c0:c0 + ntk].rearrange("h d c -> d h c"), (("KTs", blk),)),
                      ktsb[bs][:, :, 0:ntk], "ktsb0")
                for ti, t in enumerate(tiles):
                    s = tcount % 2
                    tcount += 1
                    ckd = T(ckT[bs].ap[:, :, ti * 128:(ti + 1) * 128], (("ckT", bs, ti),))
                    for hg in range(4):
                        ps = bank()
                        for k4 in range(4):
                            p.mm(ps[:, :], ckd[:, k4, :], wvv[:, k4, hg * 512:(hg + 1) * 512], k4 == 0, k4 == 3)
                        if hg % 2 == 0:
                            p.cp("act", vsb[s][:, hg * 512:(hg + 1) * 512], ps[:, :])
                        else:
                            p.cp("dve", vsb[s][:, hg * 512:(hg + 1) * 512], ps[:, :])
                    p.dma("sp", T(Vs[t * 128:(t + 1) * 128, :], (("Vs", t),)), vsb[s][:, :], "vsb%d" % s)
            p.barrier()

        def pool_bank(ids, ctr):
            i = ids[ctr[0] % len(ids)]
            ctr[0] += 1
            return T(banks[i][:, :], (("ps", i),))

        if "STOP_AB" in dbg:
            fin = [T(KTs, tuple(("KTs", b) for b in range((NKT + 3) // 4))), T(Vs, tuple(("Vs", t) for t in range(NKT)))]
            p.fence("sp", fin)
            p.flush()
            L0.close()
            return nc, dm

        with contextlib.ExitStack() as ph:
            gqa = sb("gqa", [128, 4], F32, ph)
            p.dma("sp", gqa[:, :], T(mla_g_qa.rearrange("(k p) -> p k", p=128), ()), "c3", slow=True)
            winq = sb("winq", [128, 16, 512], BF16, ph)
            p.dma("sp", winq[:, :, :], T(w_in_bf.rearrange("(k p) c -> p k c", p=128)[:, :, 0:512], ("w_in_bf",)), "c4")
            wqb = sb("wqb", [128, 4, 3072], BF16, ph)
            p.dma("sp", wqb[:, :, :], T(w_qb_bf.rearrange("(k p) c -> p k c", p=128), ("w_qb_bf",)), "c5")
            wsw = sb("wsw", [128, 4, 1024], BF16, ph)
            for k4 in range(4):
                w4 = wqb.ap[:, k4, :].rearrange("p (h c) -> p h c", c=192)
                s4 = wsw.ap[:, k4, :].rearrange("p (h c) -> p h c", c=64)
                p.ts("dve", T(s4[:, :, 0:32], wsw.keys), T(w4[:, :, 160:192], wqb.keys), -1.0, None, ALU.mult)
                p.cp("dve", T(s4[:, :, 32:64], wsw.keys), T(w4[:, :, 128:160], wqb.keys))
            cq = [sb("cq%d" % i, [64, 512], F32, ph) for i in range(2)]
            sq = [sb("sq%d" % i, [64, 512], F32, ph) for i in range(2)]
            xs = [sb("xs%d" % i, [128, D], F32, ph) for i in range(2)]
            junk = sb("junk", [128, D], BF16, ph)
            xn = [sb("xn%d" % i, [128, D], BF16, ph) for i in range(2)]
            ms = [sb("ms%d" % i, [128, 4], F32, ph) for i in range(2)]
            rstd = [sb("rstd%d" % i, [128, 4], F32, ph) for i in range(2)]
            hT = sb("hT", [128, 16, 512], BF16, ph)
            qan = [sb("qan%d" % i, [128, 512], BF16, ph) for i in range(2)]
            qaT = sb("qaT", [128, 4, 512], BF16, ph)
            qnsb = sb("qnsb", [128, 16, 512], BF16, ph)
            qrsb = sb("qrsb", [64, 16, 512], BF16, ph)
            t1 = [sb("t1_%d" % i, [64, 512], F32, ph) for i in range(2)]
            t2 = [sb("t2_%d" % i, [64, 512], F32, ph) for i in range(2)]
            tcount = 0
            for blk in range(NSUP):
                c0 = blk * 512
                bs = blk % 2
                p.dma("sp", cq[bs][:, :], T(cosq2[:, c0:c0 + 512], ()), "cq%d" % bs)
                p.dma("sp", sq[bs][:, :], T(sinq2[:, c0:c0 + 512], ()), "sq%d" % bs)
                for ti in range(4):
                    t = blk * 4 + ti
                    s = tcount % 2
                    tcount += 1
                    typ = 1 if t >= NL else 0
                    p.dma("sp", xs[s][:, :], T(xo[t * 128:(t + 1) * 128, :], ()), "xs%d" % s)
                    hTd = T(hT.ap[:, :, ti * 128:(ti + 1) * 128], (("hT", ti),))
                    ln_T(xs[s], hTd, Acol, Bcol, typ, (junk, ms[s], rstd[s], xn[s]))
                    psx = bank()
                    for k in range(16):
                        p.mm(psx[:, :], hTd[:, k, :], winq[:, k, :], k == 0, k == 15)
                    p.act(junk[:, 0:512], psx[:, :], AF.Square, scale=512.0 ** -0.5, accum=ms[s][:, 1:2])
                    rstd_of(rstd[s][:, 1:2], ms[s][:, 1:2])
                    p.ts("dve", qan[s][:, :], psx[:, :], rstd[s][:, 1:2], None, ALU.mult)
                    pb = bank_bf(bank())
                    for k4 in range(4):
                        p.tr(pb[:, k4 * 128:(k4 + 1) * 128], qan[s][:, k4 * 128:(k4 + 1) * 128], ident_b[:, :])
                    qad = T(qaT.ap[:, :, ti * 128:(ti + 1) * 128], (("qaT", ti),))
                    for k4 in range(4):
                        p.act(qad[:, k4, :], pb[:, k4 * 128:(k4 + 1) * 128], AF.Identity, scale=gqa[:, k4:k4 + 1])
                qab = T(qaT.ap, tuple(("qaT", ti) for ti in range(4)))
                for h in range(H):
                    hs = h % 2
                    psn = bank()
                    for k4 in range(4):
                        p.mm(psn[:, :], wqb[:, k4, h * 192:h * 192 + 128], qab[:, k4, :], k4 == 0, k4 == 3)
                    qnd = T(qnsb.ap[:, h, :], (("qnsb", h),))
                    p.cp("act", qnd, psn[:, :])
                    psr = bank()
                    pss = bank()
                    for k4 in range(4):
                        p.mm(psr[0:64, :], wqb[:, k4, h * 192 + 128:h * 192 + 192], qab[:, k4, :], k4 == 0, k4 == 3)
                    for k4 in range(4):
                        p.mm(pss[0:64, :], wsw[:, k4, h * 64:(h + 1) * 64], qab[:, k4, :], k4 == 0, k4 == 3)
                    p.tt("dve", t1[hs][:, :], psr[0:64, :], cq[bs][:, :], ALU.mult)
                    p.tt("dve", t2[hs][:, :], pss[0:64, :], sq[bs][:, :], ALU.mult)
                    qrd = T(qrsb.ap[:, h, :], (("qrsb", h),))
                    p.tt("dve", qrd, t1[hs][:, :], t2[hs][:, :], ALU.add)
                allq = tuple(("qnsb", h) for h in range(H))
                allr = tuple(("qrsb", h) for h in range(H))
                p.dma("sp", T(QTN[:, :, c0:c0 + 512].rearrange("h d c -> d h c"), tuple(("QTN", h, blk) for h in range(H))),
                      T(qnsb.ap, allq), "qnsb")
                p.dma("sp", T(QTR[:, :, c0:c0 + 512].rearrange("h d c -> d h c"), tuple(("QTR", h, blk) for h in range(H))),
                      T(qrsb.ap, allr), "qrsb")
            p.barrier()

        if "STOP_C" in dbg:
            p.fence("sp", [T(QTN, tuple(("QTN", h, b) for h in range(H) for b in range(NSUP)))])
            p.flush()
            L0.close()
            return nc, dm

        with contextlib.ExitStack() as ph:
            GS, NG = dm.GS, dm.NG
            kbuf = [sb("kbuf%d" % i, [128, HK], BF16, ph) for i in range(2)]
            vbuf = [sb("vbuf%d" % i, [128, HALFC, 128], BF16, ph) for i in range(2)]
            qn = [sb("qn%d" % i, [128, 512], BF16, ph) for i in range(2)]
            qr = [[sb("qr%d_%d" % (hf, i), [128, 512], BF16, ph) for i in range(2)] for hf in range(2)]
            for hf in range(2):
                for i in range(2):
                    p.memset("dve", qr[hf][i][:, :], 0.0)
            Pb = [sb("Pb%d" % i, [128, 512], BF16, ph) for i in range(4)]
            Oacc = sb("Oacc", [128, NL * 128], F32, ph)
            Sacc = sb("Sacc", [128, NL * 128], F32, ph)
            tmpS = [sb("tmpS%d" % i, [128, 512], F32, ph) for i in range(2)]
            tmpO = [sb("tmpO%d" % i, [128, 512], F32, ph) for i in range(2)]
            osb = [sb("osb%d" % i, [128, 512], BF16, ph) for i in range(2)]
            cS, cO, cZ = [0], [0], [0]
            SC = float(NOPE + ROPE) ** -0.5
            qblocks = []
            q0 = 0
            while q0 < NL * 128:
                nq = min(512, NL * 128 - q0)
                qblocks.append((q0, nq, False))
                q0 += nq
            ctxblk = (NL * 128, 256, True)
            steps = []
            u = 0
            for h in range(H):
                for half in range(2):
                    qbs = qblocks + ([ctxblk] if half == 1 else [])
                    for qi, (q0, nq, isctx) in enumerate(qbs):
                        chunks = [HALFC - 2, HALFC - 1] if isctx else list(range(HALFC))
                        for ci, c in enumerate(chunks):
                            steps.append(dict(u=u, h=h, half=half, qi=qi, q0=q0, nq=nq, isctx=isctx, c=c,
                                              first=ci == 0, last=ci == len(chunks) - 1,
                                              ufirst=(qi == 0 and ci == 0)))
                    u += 1
            qctr = [0]
            state = {}

            def load_unit(st):
                ub = st["u"] % 2
                h, half = st["h"], st["half"]
                for g in range(NG):
                    a = half * HK + g * GS * 128
                    b = a + GS * 128
                    kkeys = tuple(("KTs", bb) for bb in range(a // 512, (b - 1) // 512 + 1))
                    p.dma("sp", T(kbuf[ub].ap[:, g * GS * 128:(g + 1) * GS * 128], (("kb", ub, g),)),
                          T(KTs[h, :, a:b], kkeys), "kb%d_%d" % (ub, g))
                    vkeys = tuple(("Vs", tt) for tt in range(a // 128, b // 128))
                    p.dma("sp", T(vbuf[ub].ap[:, g * GS:(g + 1) * GS, :], (("vb", ub, g),)),
                          T(Vs[a:b, :].rearrange("(c p) d -> p c d", p=128)[:, :, h * 128:(h + 1) * 128], vkeys),
                          "vb%d_%d" % (ub, g))

            ufirsts = [st for st in steps if st["ufirst"]]

            def emit_qk(st):
                ub = st["u"] % 2
                h, half, q0, nq, c = st["h"], st["half"], st["q0"], st["nq"], st["c"]
                if st["first"]:
                    qs = qctr[0] % 2
                    qctr[0] += 1
                    st["qs"] = qs
                    blkq = q0 // 512
                    p.dma("sp", qn[qs][:, 0:nq], T(QTN[h, :, q0:q0 + nq], (("QTN", h, blkq),)), "qn%d" % qs)
                    p.dma("sp", qr[half][qs][half * 64:(half + 1) * 64, 0:nq], T(QTR[h, :, q0:q0 + nq], (("QTR", h, blkq),)),
                          "qr%d_%d" % (half, qs))
                    state[(st["u"], st["qi"])] = dict(qs=qs)
                sd = state[(st["u"], st["qi"])]
                qs = sd["qs"]
                ps = pool_bank([0, 1, 2, 3], cS)
                st["ps"] = ps
                g = c // GS
                p.mm(ps[:, 0:nq], T(kbuf[ub].ap[:, c * 128:(c + 1) * 128], (("kb", ub, g),)), qn[qs][:, 0:nq],
                     True, False)
                tkey = half * HALFC + c
                p.mm(ps[:, 0:nq], T(krT.ap[:, c * 128:(c + 1) * 128], (("krT", c), ("krT", HALFC + c))),
                     qr[half][qs][:, 0:nq], False, True)

            pcount = [0]

            def emit_rest(st):
                ub = st["u"] % 2
                h, half, q0, nq, c = st["h"], st["half"], st["q0"], st["nq"], st["c"]
                sd = state[(st["u"], st["qi"])]
                P = Pb[pcount[0] % 4]
                pcount[0] += 1
                p.act(P[:, 0:nq], st["ps"][:, 0:nq], AF.Exp, scale=SC)
                if st["first"]:
                    sd["psO"] = pool_bank([4, 5], cO)
                    sd["psZ"] = pool_bank([6, 7], cZ)
                psO, psZ = sd["psO"], sd["psZ"]
                g = c // GS
                p.mm(psO[:, 0:nq], T(vbuf[ub].ap[:, c, :], (("vb", ub, g),)), P[:, 0:nq], st["first"], st["last"])
                p.mm(psZ[:, 0:nq], ones_b[:, :], P[:, 0:nq], st["first"], st["last"])
                if st["last"]:
                    fs = sd["qs"]
                    if st["isctx"]:
                        p.recip(tmpS[fs][:, 0:nq], psZ[:, 0:nq])
                        p.tt("dve", osb[fs][:, 0:nq], psO[:, 0:nq], tmpS[fs][:, 0:nq], ALU.mult)
                    elif half == 0:
                        ak = (("acc", st["qi"]),)
                        p.cp("dve", T(Oacc.ap[:, q0:q0 + nq], ak), psO[:, 0:nq])
                        p.cp("dve", T(Sacc.ap[:, q0:q0 + nq], ak), psZ[:, 0:nq])
                        return
                    else:
                        ak = (("acc", st["qi"]),)
                        p.tt("dve", tmpS[fs][:, 0:nq], psZ[:, 0:nq], T(Sacc.ap[:, q0:q0 + nq], ak), ALU.add)
                        p.recip(tmpS[fs][:, 0:nq], tmpS[fs][:, 0:nq])
                        p.tt("dve", tmpO[fs][:, 0:nq], psO[:, 0:nq], T(Oacc.ap[:, q0:q0 + nq], ak), ALU.add)
                        p.tt("dve", osb[fs][:, 0:nq], tmpO[fs][:, 0:nq], tmpS[fs][:, 0:nq], ALU.mult)
                    okey = ("OTs", h, q0 // 512, 1 if st["isctx"] else 0)
                    p.dma("sp", T(OTs[h * 128:(h + 1) * 128, q0:q0 + nq], (okey,)), osb[fs][:, 0:nq], "osb%d" % fs)

            dcf = [sb("dcf%d" % i, [128, 1024], F32, ph) for i in range(2)]
            dcb = [sb("dcb%d" % i, [128, 1024], BF16, ph) for i in range(2)]
            cstate = dict(k=0, out=None)

            def cast_tick():
                if cstate["out"] is not None:
                    dst_, key_, sl_, w_ = cstate["out"]
                    p.dma("sp", T(dst_, (key_,)), dcb[sl_][:, 0:w_], "cb%d" % sl_)
                    cstate["out"] = None
                if cstate["k"] >= len(cast_items):
                    return False
                src_, dst_, key_, w_ = cast_items[cstate["k"]]
                sl_ = cstate["k"] % 2
                cstate["k"] += 1
                p.dma("sp", dcf[sl_][:, 0:w_], T(src_, ()), "cf%d" % sl_)
                p.cp("pool", dcb[sl_][:, 0:w_], dcf[sl_][:, 0:w_])
                cstate["out"] = (dst_, key_, sl_, w_)
                return True

            cast_every = max(1, len(steps) // (len(cast_items) + 2))
            LA = 2
            load_unit(ufirsts[0])
            if len(ufirsts) > 1:
                load_unit(ufirsts[1])
            for i in range(len(steps) + LA):
                if i < len(steps):
                    emit_qk(steps[i])
                if i >= LA:
                    st = steps[i - LA]
                    emit_rest(st)
                    is_ulast = (i - LA + 1 == len(steps)) or steps[i - LA + 1]["u"] != st["u"]
                    if is_ulast and st["u"] + 2 < len(ufirsts):
                        load_unit(ufirsts[st["u"] + 2])
                if i % cast_every == 0:
                    cast_tick()
            while cast_tick():
                pass
            cast_tick()
            p.barrier()
        L0.close()

        if "STOP_D" in dbg:
            p.flush()
            return nc, dm

        def proj_res_ln2(l, t, oT_t, x_t, typ, R, yset, dst_row):
            wout, G1, A2, B2, junk, ms, rstd, xn, tmpy, fTt = R
            ys = [T(banks[b][:, :], (("ps", b),)) for b in yset]
            for nb in range(4):
                for k in range(16):
                    p.mm(ys[nb][:, :], oT_t[:, k, :], wout[:, k, nb * 512:(nb + 1) * 512], k == 0, k == 15)
            for nb in range(4):
                p.act(junk[:, nb * 512:(nb + 1) * 512], ys[nb][:, :], AF.Square, scale=float(D) ** -0.5,
                      accum=ms[:, nb:nb + 1])
            p.reduce_sum(ms[:, 4:5], ms[:, 0:4])
            rstd_of(rstd[:, 0:1], ms[:, 4:5])
            for nb in range(4):
                p.stt("dve", tmpy[:, nb * 512:(nb + 1) * 512], ys[nb][:, :], rstd[:, 0:1],
                      G1[typ][:, nb * 512:(nb + 1) * 512], ALU.mult, ALU.mult)
            p.tt("pool", x_t[:, :], x_t[:, :], tmpy[:, :], ALU.add)
            p.dma("sp", T(X1[dst_row * 128:(dst_row + 1) * 128, :], (("X1", dst_row),)), x_t[:, :], "x1st%d" % (t % 2))
            p.act(junk[:, :], x_t[:, :], AF.Square, scale=float(D) ** -0.5, accum=ms[:, 5:6])
            rstd_of(rstd[:, 1:2], ms[:, 5:6])
            p.ts("dve", xn[:, :], x_t[:, :], rstd[:, 1:2], None, ALU.mult)
            for g in range(2):
                pb = bank_bf(ys[g])
                for kk in range(8):
                    k = g * 8 + kk
                    p.tr(pb[:, kk * 128:(kk + 1) * 128], xn[:, k * 128:(k + 1) * 128], ident_b[:, :])
                for kk in range(8):
                    k = g * 8 + kk
                    src = pb[:, kk * 128:(kk + 1) * 128]
                    if kk % 2 == 0:
                        p.act(fTt[:, k, :], src, AF.Identity, bias=B2[:, typ, k:k + 1], scale=A2[:, typ, k:k + 1])
                    else:
                        p.ts("dve", fTt[:, k, :], src, A2[:, typ, k:k + 1], B2[:, typ, k:k + 1], ALU.mult, ALU.add)
            p.dma("sp", T(FT.rearrange("(k p) c -> p k c", p=128)[:, :, dst_row * 128:(dst_row + 1) * 128],
                          (("FT", dst_row),)), fTt[:, :, :], "ftst%d" % (t % 2))

        def alloc_proj(ph, l, wo_bf, wo_key):
            wout = sb("wout", [128, 16, 2048], BF16, ph)
            p.dma("sp", wout[:, :, :], T(wo_bf.rearrange("(k p) c -> p k c", p=128), (wo_key,)), "c4")
            ntyp = 2 if l == 0 else 1
            G1 = [sb("G1_%d" % c, [128, D], F32, ph) for c in range(ntyp)]
            for c in range(ntyp):
                load_rowbc(G1[c], l, 2, c, "c5")
            A2 = sb("A2", [128, 2, 16], F32, ph)
            B2 = sb("B2", [128, 2, 16], F32, ph)
            load_cols(A2, l, 4, sk="mcA")
            load_cols(B2, l, 3, sk="mcB")
            junk = sb("junk", [128, D], BF16, ph)
            tmpy = sb("tmpy", [128, D], F32, ph)
            res = []
            for i in range(2):
                res.append((wout, G1, A2, B2, junk, sb("ms%d" % i, [128, 8], F32, ph), sb("rstd%d" % i, [128, 4], F32, ph),
                            sb("xn%d" % i, [128, D], BF16, ph), tmpy,
                            sb("fTt%d" % i, [128, 16, 128], BF16, ph)))
            return res

        def ffn(l, tile_groups, final):
            with contextlib.ExitStack() as ph:
                ntyp = 2 if l == 0 else 1
                G3 = [sb("G3_%d" % c, [128, D], F32, ph) for c in range(ntyp)]
                for c in range(ntyp):
                    load_rowbc(G3[c], l, 5, c, "c5")
                fTb = sb("fTb", [128, 16, 512], BF16, ph)
                w1s = [sb("w1s%d" % i, [128, 16, 256], BF16, ph) for i in range(2)] * 2
                w2s = [sb("w2s%d" % i, [128, 2, 1024], BF16, ph) for i in range(3)]
                uT = sb("uT", [128, 64, 512], BF16, ph)
                rl = [sb("rl%d" % i, [128, 512], F32, ph) for i in range(2)]
                ysb = [sb("ysb%d" % i, [128, 1024], F32, ph) for i in range(4)]
                xt = [sb("xt%d" % i, [128, D], F32, ph) for i in range(2)]
                junk = sb("junk", [128, 1024], BF16, ph)
                ms = [sb("ms%d" % i, [128, 8], F32, ph) for i in range(4)]
                rstd = [sb("rstd%d" % i, [128, 4], F32, ph) for i in range(4)]
                w1v = w1_bf[l].rearrange("(k p) n -> p k n", p=128)
                w2v = w2_bf[l].rearrange("(j p) n -> p j n", p=128)
                cA = [0]
                c1 = 0
                c2 = 0
                rc = 0
                xc = 0
                for grp in tile_groups:
                    ntk = len(grp) * 128
                    r0 = grp[0][0]
                    p.dma("sp", fTb[:, :, 0:ntk], T(FT.rearrange("(k p) c -> p k c", p=128)[:, :, r0 * 128:r0 * 128 + ntk],
                                                     tuple(("FT", g[0]) for g in grp)), "fTb")
                    for jg in range(32):
                        w = w1s[c1 % 2]
                        p.dma("sp", w[:, :, :], T(w1v[:, :, jg * 256:(jg + 1) * 256], ("w1_bf%d" % l,)), "w1s%d" % (c1 % 2))
                        c1 += 1
                        for j2 in range(2):
                            j = jg * 2 + j2
                            ps = pool_bank([0, 1, 2, 3, 4, 5, 6, 7], cA)
                            for k in range(16):
                                p.mm(ps[:, 0:ntk], w[:, k, j2 * 128:(j2 + 1) * 128], fTb[:, k, 0:ntk], k == 0, k == 15)
                            r = rl[rc % 2]
                            rc += 1
                            p.act(r[:, 0:ntk], ps[:, 0:ntk], AF.Relu)
                            ud = T(uT.ap[:, j, 0:ntk], (("uT", j),))
                            p.tt("dve", ud, r[:, 0:ntk], r[:, 0:ntk], ALU.mult)
                    if "FFN_A" in dbg:
                        break
                    for dh in range(2):
                        p.tag = None
                        yb = [[T(banks[ti * 2 + n2][:, :], (("ps", ti * 2 + n2),)) for n2 in range(2)] for ti in range(len(grp))]
                        for jg in range(32):
                            w = w2s[c2 % 3]
                            p.dma("sp", w[:, :, :], T(w2v[:, jg * 2:(jg + 1) * 2, dh * 1024:(dh + 1) * 1024], ("w2_bf%d" % l,)),
                                  "w2s%d" % (c2 % 3))
                            c2 += 1
                            for j2 in range(2):
                                j = jg * 2 + j2
                                for ti in range(len(grp)):
                                    for n2 in range(2):
                                        p.mm(yb[ti][n2][:, :], T(uT.ap[:, j, ti * 128:(ti + 1) * 128], (("uT", j),)),
                                             w[:, j2, n2 * 512:(n2 + 1) * 512], j == 0, j == 63)
                        for ti, (row, typ, orow) in enumerate(grp):
                            p.tag = "epi_dh%d_ti%d" % (dh, ti) if "FFN_DBG" in dbg else None
                            if "FFN_NOEPI" in dbg:
                                continue
                            for n2 in range(2):
                                if "FFN_NOSQ" in dbg:
                                    continue
                                p.act(junk[:, n2 * 512:(n2 + 1) * 512], yb[ti][n2][:, :], AF.Square, scale=float(D) ** -0.5,
                                      accum=ms[ti][:, dh * 2 + n2:dh * 2 + n2 + 1])
                            if dh == 0 and "FFN_NOCP" in dbg:
                                pass
                            elif dh == 0:
                                for n2 in range(2):
                                    p.cp("dve", ysb[ti][:, n2 * 512:(n2 + 1) * 512], yb[ti][n2][:, :], after=[junk])
                            elif "FFN_B" in dbg:
                                pass
                            else:
                                p.reduce_sum(ms[ti][:, 4:5], ms[ti][:, 0:4])
                                rstd_of(rstd[ti][:, 0:1], ms[ti][:, 4:5])
                                x_t = xt[xc % 2]
                                xc += 1
                                p.dma("sp", x_t[:, :], T(X1[row * 128:(row + 1) * 128, :], (("X1", row),)), "xt%d" % ((xc - 1) % 2))
                                p.stt("dve", ysb[ti][:, :], ysb[ti][:, :], rstd[ti][:, 0:1], G3[typ][:, 0:1024], ALU.mult, ALU.mult)
                                p.tt("pool", x_t[:, 0:1024], x_t[:, 0:1024], ysb[ti][:, :], ALU.add)
                                for n2 in range(2):
                                    p.stt("dve", ysb[ti][:, n2 * 512:(n2 + 1) * 512], yb[ti][n2][:, :], rstd[ti][:, 0:1],
                                          G3[typ][:, 1024 + n2 * 512:1024 + (n2 + 1) * 512], ALU.mult, ALU.mult)
                                p.tt("pool", x_t[:, 1024:2048], x_t[:, 1024:2048], ysb[ti][:, :], ALU.add)
                                final(row, orow, x_t, "xt%d" % ((xc - 1) % 2))
                    if "FFN_G0" in dbg:
                        break
                p.barrier()

        with contextlib.ExitStack() as ph:
            R = alloc_proj(ph, 0, mla_wo_bf, "mla_wo_bf")
            xs = [sb("xs%d" % i, [128, D], F32, ph) for i in range(2)]
            ot = [sb("ot%d" % i, [128, 16, 128], BF16, ph) for i in range(2)]
            OTv = OTs.rearrange("(k p) c -> p k c", p=128)
            for t in range(NT):
                s = t % 2
                typ = 1 if t >= NL else 0
                okeys = tuple(("OTs", h, t // 4, typ) for h in range(H))
                p.dma("sp", ot[s][:, :, :], T(OTv[:, :, t * 128:(t + 1) * 128], okeys), "ot%d" % s)
                p.dma("sp", xs[s][:, :], T(xo[t * 128:(t + 1) * 128, :], ()), "xs%d" % s)
                proj_res_ln2(0, t, ot[s], xs[s], typ, R[s], [0, 1, 2, 3] if s == 0 else [4, 5, 6, 7], t)
            p.barrier()

        if "STOP_E1" in dbg:
            p.flush()
            return nc, dm

        def final0(row, orow, x_t, sk):
            p.dma("sp", T(X2[row * 128:(row + 1) * 128, :], (("X2", row),)), x_t[:, :], sk)

        ffn(0, [[(t, 1 if t >= NL else 0, None) for t in range(b * 4, b * 4 + 4)] for b in range(NSUP)], final0)

        if "STOP_L0" in dbg:
            p.fence("sp", [T(X2, tuple(("X2", t) for t in range(NT)))])
            p.flush()
            return nc, dm

        L1 = contextlib.ExitStack()
        sinkexp = sb("sinkexp", [128, HS], F32, L1)
        p.dma("sp", T(sinkexp.ap.unsqueeze(1), sinkexp.keys), T(swa_sink.rearrange("(o n) -> o n", o=1).partition_broadcast(128), ()), "c6")
        p.act(sinkexp[:, :], sinkexp[:, :], AF.Exp)

        with contextlib.ExitStack() as ph:
            A1c = sb("A1c", [128, 2, 16], F32, ph)
            B1c = sb("B1c", [128, 2, 16], F32, ph)
            load_cols(A1c, 1, 1, sk="mcA")
            load_cols(B1c, 1, 0, sk="mcB")
            wqkv = sb("wqkv", [128, 16, 2560], BF16, ph)
            wqv = wqkv_bf.rearrange("(k p) c -> p k c", p=128)
            for k in range(0, 16, 4):
                p.dma("sp", wqkv[:, k:k + 4, :], T(wqv[:, k:k + 4, :], ("wqkv_bf",)), "c4")
            xs = [sb("xs%d" % i, [128, D], F32, ph) for i in range(2)]
            junk = sb("junk", [128, D], BF16, ph)
            xn = [sb("xn%d" % i, [128, D], BF16, ph) for i in range(2)]
            ms = [sb("ms%d" % i, [128, 4], F32, ph) for i in range(2)]
            rstd = [sb("rstd%d" % i, [128, 4], F32, ph) for i in range(2)]
            hTt = [sb("hTt%d" % i, [128, 16, 128], BF16, ph) for i in range(2)]
            co = [sb("co%d" % i, [128, 64], F32, ph) for i in range(2)]
            so = [sb("so%d" % i, [128, 64], F32, ph) for i in range(2)]
            ra = [sb("ra%d" % i, [128, 512], F32, ph) for i in range(2)]
            rb = [sb("rb%d" % i, [128, 512], F32, ph) for i in range(2)]
            qtok = [sb("qtok%d" % i, [128, HS, 128], BF16, ph) for i in range(2)]
            ktok = [sb("ktok%d" % i, [128, KVH, 128], BF16, ph) for i in range(2)]
            for i in range(2):
                p.memset("dve", qtok[i][:, :, :], 0.0)
                p.memset("dve", ktok[i][:, :, :], 0.0)
            qTt = [sb("qTt%d" % i, [64, HS * 128], BF16, ph) for i in range(2)]
            kTt = [sb("kTt%d" % i, [64, KVH * 128], BF16, ph) for i in range(2)]
            vtt = [sb("vtt%d" % i, [128, KVH * 65], BF16, ph) for i in range(2)]
            for i in range(2):
                p.memset("dve", vtt[i][:, :], 1.0)

            def rope_tok(ps, ncol, dst, s):
                nh = ncol // 64
                pv = ps.ap[:, 0:ncol].rearrange("p (h c) -> p h c", c=64)
                av = ra[s].ap[:, 0:ncol].rearrange("p (h c) -> p h c", c=64)
                bv = rb[s].ap[:, 0:ncol].rearrange("p (h c) -> p h c", c=64)
                cb = co[s].ap.unsqueeze(1).to_broadcast([128, nh, 64])
                sb1 = so[s].ap[:, 0:32].unsqueeze(1).to_broadcast([128, nh, 32])
                sb2 = so[s].ap[:, 32:64].unsqueeze(1).to_broadcast([128, nh, 32])
                p.tt("dve", T(av, ra[s].keys), T(pv, ps.keys), T(cb, co[s].keys), ALU.mult)
                p.tt("dve", T(bv[:, :, 0:32], rb[s].keys), T(pv[:, :, 32:64], ps.keys), T(sb1, so[s].keys), ALU.mult)
                p.tt("dve", T(bv[:, :, 32:64], rb[s].keys), T(pv[:, :, 0:32], ps.keys), T(sb2, so[s].keys), ALU.mult)
                p.tt("pool", dst, T(av, ra[s].keys), T(bv, rb[s].keys), ALU.add)

            for t in range(NT):
                s = t % 2
                typ = 1 if t >= NL else 0
                p.dma("sp", xs[s][:, :], T(X2[t * 128:(t + 1) * 128, :], (("X2", t),)), "xs%d" % s)
                p.dma("sp", co[s][:, :], T(coso2[t * 128:(t + 1) * 128, :], ()), "co%d" % s)
                p.dma("sp", so[s][:, :], T(sino2[t * 128:(t + 1) * 128, :], ()), "so%d" % s)
                ln_T(xs[s], hTt[s], A1c, B1c, typ, (junk, ms[s], rstd[s], xn[s]))
                pq = [bank() for _ in range(5)]
                for nb in range(5):
                    for k in range(16):
                        p.mm(pq[nb][:, :], hTt[s][:, k, :], wqkv[:, k, nb * 512:(nb + 1) * 512], k == 0, k == 15)
                for nb in range(4):
                    rope_tok(pq[nb], 512, qtok[s][:, nb * 8:(nb + 1) * 8, 0:64], s)
                rope_tok(pq[4], 256, ktok[s][:, :, 0:64], s)
                v1d = T(vtt[s].ap.rearrange("p (k c) -> p k c", c=65)[:, :, 0:64], vtt[s].keys)
                p.cp("act", v1d, T(pq[4].ap[:, 256:512].rearrange("p (k c) -> p k c", c=64), pq[4].keys), after=[rb[s]])
                p.dma("sp", T(V1s[t, :, :], (("V1s", t),)), vtt[s][:, :], "vtt%d" % s)
                for g in range(4):
                    pb = bank_bf(bank())
                    for hh in range(8):
                        hq = g * 8 + hh
                        p.tr(pb[:, hh * 128:(hh + 1) * 128], qtok[s][:, hq, :], ident_b[:, :])
                    if g % 2 == 0:
                        p.cp("act", qTt[s][:, g * 1024:(g + 1) * 1024], pb[0:64, :])
                    else:
                        p.cp("dve", qTt[s][:, g * 1024:(g + 1) * 1024], pb[0:64, :])
                pb = bank_bf(bank())
                for kv in range(KVH):
                    p.tr(pb[:, kv * 128:(kv + 1) * 128], ktok[s][:, kv, :], ident_b[:, :])
                p.cp("act", kTt[s][:, :], pb[0:64, 0:512])
                p.dma("sp", T(K1Ts[t, :, :], (("K1Ts", t),)), kTt[s][:, :], "kTt%d" % s)
                p.dma("sp", T(Q1T[t, :, :], (("Q1T", t),)), qTt[s][:, :], "qTt%d" % s)
            p.barrier()

        if "STOP_A1" in dbg:
            p.flush()
            L1.close()
            return nc, dm

        with contextlib.ExitStack() as ph:
            R = alloc_proj(ph, 1, swa_wo_bf, "swa_wo_bf")
            mk = sb("mk", [128, 4, 128], BF16, ph)
            mkf = sb("mkf", [128, 4, 128], F32, ph)
            p.dma("sp", mkf[:, :, :], T(masks.rearrange("m k q -> k m q"), ()), "c6")
            p.cp("dve", mk[:, :, :], mkf[:, :, :])
            xs = [sb("xs%d" % i, [128, D], F32, ph) for i in range(2)]
            qT = [sb("qT%d" % i, [128, HS * 128], BF16, ph) for i in range(2)]
            for i in range(2):
                p.memset("dve", qT[i][:, :], 0.0)
            Pb = [sb("Pb%d" % i, [128, 512], BF16, ph) for i in range(4)]
            otok = [sb("otok%d" % i, [128, D], BF16, ph) for i in range(2)]
            oTt = [sb("oTt%d" % i, [128, 16, 128], BF16, ph) for i in range(2)]
            den = [sb("den%d" % i, [128, 4], F32, ph) for i in range(2)]
            ctxK = sb("ctxK", [128, 2, KVH * 128], BF16, ph)
            ctxV = sb("ctxV", [128, 2, KVH * 65], BF16, ph)
            p.memset("dve", ctxK[:, :, :], 0.0)
            p.dma("sp", ctxK[0:64, :, :], T(K1Ts[NT - 2:NT, :, :].rearrange("t d c -> d t c"), (("K1Ts", NT - 2), ("K1Ts", NT - 1))), "c7")
            p.dma("sp", ctxV[:, :, :], T(V1s[NT - 2:NT, :, :].rearrange("t k c -> k t c"), (("V1s", NT - 2), ("V1s", NT - 1))), "c8")
            kwin = [sb("kwin%d" % i, [128, 3, KVH * 128], BF16, ph) for i in range(2)]
            for i in range(2):
                p.memset("dve", kwin[i][:, :, :], 0.0)
            vwin = [sb("vwin%d" % i, [128, 3, KVH * 65], BF16, ph) for i in range(2)]
            cS, cO = [0], [0]
            pc = 0
            dc = 0
            for i in range(1, NL - 1):
                s = i % 2
                p.dma("sp", xs[s][:, :], T(X2[i * 128:(i + 1) * 128, :], (("X2", i),)), "xs%d" % s)
                p.dma("sp", qT[s][0:64, :], T(Q1T[i, :, :], (("Q1T", i),)), "qT%d" % s)
                wkeys = tuple(("K1Ts", tt) for tt in (i - 1, i, i + 1))
                vkeys = tuple(("V1s", tt) for tt in (i - 1, i, i + 1))
                p.dma("sp", kwin[s][0:64, :, :], T(K1Ts[i - 1:i + 2, :, :].rearrange("t d c -> d t c"), wkeys), "kwin%d" % s)
                p.dma("sp", vwin[s][:, :, :], T(V1s[i - 1:i + 2, :, :].rearrange("t k c -> k t c"), vkeys), "vwin%d" % s)
                chunks = [((ctxK, ctxV, 0), None), ((ctxK, ctxV, 1), None), ((kwin[s], vwin[s], 0), 0 if i == 1 else 1),
                          ((kwin[s], vwin[s], 1), None), ((kwin[s], vwin[s], 2), 3 if i == NL - 2 else 2)]
                for kv in range(KVH):
                    for h2 in range(2):
                        hb = kv * 8 + h2 * 4
                        po = pool_bank([4, 5], cO)
                        for ci, (kt, mi) in enumerate(chunks):
                            ps = pool_bank([6, 7], cS)
                            kT_, vT_, ki = kt
                            p.mm(ps[:, :], kT_[:, ki, kv * 128:(kv + 1) * 128],
                                 qT[s][:, hb * 128:(hb + 4) * 128], True, True)
                            P = Pb[pc % 4]
                            pc += 1
                            p.act(P[:, :], ps[:, :], AF.Exp, scale=float(HD) ** -0.5)
                            if mi is not None:
                                pv = P.ap.rearrange("p (h q) -> p h q", q=128)
                                mb = mk.ap[:, mi, :].unsqueeze(1).to_broadcast([128, 4, 128])
                                p.tt("dve", T(pv, P.keys), T(pv, P.keys), T(mb, mk.keys), ALU.mult)
                            for hh in range(4):
                                p.mm(po[:, hh * 65:(hh + 1) * 65], P[:, hh * 128:(hh + 1) * 128],
                                     vT_[:, ki, kv * 65:(kv + 1) * 65],
                                     ci == 0 and hh == 0, ci == 4 and hh == 3, skip=True)
                        d = den[dc % 2]
                        dc += 1
                        pov = po.ap[:, 0:260].rearrange("p (h c) -> p h c", c=65)
                        p.tt("dve", d[:, 0:4], T(pov[:, :, 64], po.keys), sinkexp[:, hb:hb + 4], ALU.add)
                        p.recip(d[:, 0:4], d[:, 0:4])
                        ov = otok[s].ap[:, hb * 64:(hb + 4) * 64].rearrange("p (h c) -> p h c", c=64)
                        db = d.ap[:, 0:4].unsqueeze(2).to_broadcast([128, 4, 64])
                        p.tt("dve", T(ov, otok[s].keys), T(pov[:, :, 0:64], po.keys), T(db, d.keys), ALU.mult)
                yset = [0, 1, 2, 3]
                for g in range(2):
                    pb = bank_bf(T(banks[yset[g + 2]][:, :], (("ps", yset[g + 2]),)))
                    for kk in range(8):
                        k = g * 8 + kk
                        p.tr(pb[:, kk * 128:(kk + 1) * 128], otok[s][:, k * 128:(k + 1) * 128], ident_b[:, :])
                    if g == 0:
                        p.cp("act", T(oTt[s].ap[:, 0:8, :].rearrange("p k c -> p (k c)"), oTt[s].keys), pb[:, :])
                    else:
                        p.cp("dve", T(oTt[s].ap[:, 8:16, :].rearrange("p k c -> p (k c)"), oTt[s].keys), pb[:, :])
                proj_res_ln2(1, i, oTt[s], xs[s], 0, R[s], yset, i)
            p.barrier()
        L1.close()

        if "STOP_A2" in dbg:
            p.flush()
            return nc, dm

        def final1(row, orow, x_t, sk):
            p.dma("sp", T(y_out[orow * 128:(orow + 1) * 128, :], (("y", orow),)), x_t[:, :], sk)

        ffn(1, [[(t, 0, t - 1) for t in range(1 + b * 4, 1 + b * 4 + 4)] for b in range(NLO // 4)], final1)
        p.fence("sp", [T(y_out, tuple(("y", t) for t in range(NLO)))])
        p.flush()
        print("program: %d instructions, %d waits" % (p.n_inst, p.n_wait), "sbuf KB per stack:", [round(v / 1024, 1) for v in p._sbacc.values()])
    return nc, dm


def rope_tables(S_B):
    rows = S_B // GRID_W
    row = np.repeat(np.arange(rows, dtype=np.float32), GRID_W)
    col = np.tile(np.arange(GRID_W, dtype=np.float32), rows)
    n_freq = 16
    freqs = (np.float32(10000.0) ** (-np.arange(n_freq, dtype=np.float32) / np.float32(n_freq))).astype(np.float32)
    ang = np.concatenate([row[:, None] * freqs, col[:, None] * freqs], axis=-1).astype(np.float32)
    return np.cos(ang).astype(np.float32), np.sin(ang).astype(np.float32)


def make_in_maps(inputs, S_B):
    dm = Dims(S_B)
    f = lambda a: np.ascontiguousarray(np.asarray(a, dtype=np.float32))
    x = f(inputs["x"])
    ctx = f(inputs["ctx"])
    c = f(inputs["c"])
    c_ctx = f(inputs["c_ctx"])
    B = x.shape[0]
    cos, sin = rope_tables(S_B)
    cosk2 = np.ones((dm.NKEY, 64), np.float32)
    sink2 = np.zeros((dm.NKEY, 64), np.float32)
    cosk2[:S_B] = np.concatenate([cos, cos], axis=1)
    sink2[:S_B] = np.concatenate([-sin, sin], axis=1)
    ident = np.eye(128, dtype=np.float32)
    sel = np.zeros((2, 256), np.float32)
    sel[0, :128] = 1.0
    sel[1, 128:] = 1.0
    kp = np.arange(128)[:, None]
    qp = np.arange(128)[None, :]
    mL = (qp <= kp).astype(np.float32)
    mR = (kp <= qp).astype(np.float32)
    shared = dict(
        w_mod=f(inputs["w_mod"]), b_mod=f(inputs["b_mod"]), g_norm=f(inputs["g_norm"]),
        w_ff_in=f(inputs["w_ff_in"]), w_ff_out=f(inputs["w_ff_out"]),
        mla_w_in=f(inputs["mla_w_in"])[0], mla_g_qa=f(inputs["mla_g_qa"])[0], mla_g_kva=f(inputs["mla_g_kva"])[0],
        mla_w_qb=f(inputs["mla_w_qb"])[0], mla_w_kvb=f(inputs["mla_w_kvb"])[0], mla_w_out=f(inputs["mla_w_out"])[0],
        swa_w_qkv=f(inputs["swa_w_qkv"])[0], swa_sink=f(inputs["swa_sink"])[0], swa_w_out=f(inputs["swa_w_out"])[0],
        ident=ident, sel=sel, cosk2=cosk2, sink2=sink2,
    )
    own = S_B // 4
    in_maps = []
    xbs = [np.ascontiguousarray(np.concatenate([x[b], ctx[b]], axis=0)) for b in range(B)]
    for core in range(8):
        b, j = core // 4, core % 4
        t0 = j * own
        lo, hi = t0 - 128, t0 + own + 128
        xo = np.zeros((dm.NTOK, D), np.float32)
        pos = np.arange(lo, hi)
        valid = (pos >= 0) & (pos < S_B)
        xo[:dm.NL * 128][valid] = x[b, pos[valid]]
        xo[dm.NL * 128:] = ctx[b]
        cq = np.ones((dm.NTOK, 32), np.float32)
        sq = np.zeros((dm.NTOK, 32), np.float32)
        cq[:dm.NL * 128][valid] = cos[pos[valid]]
        sq[:dm.NL * 128][valid] = sin[pos[valid]]
        m = np.stack([mL if j > 0 else np.zeros_like(mL), mL, mR, mR if j < 3 else np.zeros_like(mR)])
        d = dict(shared)
        d.update(
            xo=xo, xb=xbs[b], cvec=np.ascontiguousarray(np.stack([c[b], c_ctx])),
            cosq2=np.ascontiguousarray(np.concatenate([cq, cq], axis=1).T),
            sinq2=np.ascontiguousarray(np.concatenate([sq, sq], axis=1).T),
            coso2=np.ascontiguousarray(np.concatenate([cq, cq], axis=1)),
            sino2=np.ascontiguousarray(np.concatenate([-sq, sq], axis=1)),
            masks=np.ascontiguousarray(m),
        )
        in_maps.append(d)
    return in_maps


_CACHE = {}


def kernel(**inputs):
    S_B = int(np.asarray(inputs["x"]).shape[1])
    if S_B not in _CACHE:
        _CACHE[S_B] = build(S_B)
    nc, dm = _CACHE[S_B]
    in_maps = make_in_maps(inputs, S_B)
    res = run_bass_kernel_spmd(nc, in_maps, core_ids=list(range(8)))
    B = 2
    out = np.zeros((B, S_B, D), np.float32)
    own = S_B // 4
    for core in range(8):
        b, j = core // 4, core % 4
        out[b, j * own:(j + 1) * own] = res.results[core]["y_out"]
    return out
```
